# Optimizing a Trainium2 kernel written in Bass

```python
import math
import jax, jax.numpy as jnp
from jax import lax
import numpy as np

D_MODEL = 1024
BATCH = 16
SEQ = 2048
DEPTH = 4

N_MEM = 256
EPS = 1e-6
A_HEADS = 8
A_HEAD_DIM = 64
A_WIDTH = A_HEADS * A_HEAD_DIM
A_PATTERNS = ((128, 1), (512, 4), (2048, 16))
ROPE_THETA = 10000.0
B_HEADS = 4
B_DK = 64
B_DV = 128
B_KW = B_HEADS * B_DK
B_VW = B_HEADS * B_DV
B_RANK = 16
B_TAU = 16.0
B_CHUNK = 64
C_WIDTH = 512
C_GROUP = 16
C_NGROUPS = C_WIDTH // C_GROUP
C_STATE = 64
D_WIDTH = 512
D_KERNEL = 31
X_HEADS = 4
X_HEAD_DIM = D_MODEL // X_HEADS
D_FF = 2816
FFN_KERNEL = 3
N_EVEN = (DEPTH + 1) // 2
N_ODD = DEPTH // 2
IN_AB = 3 * A_WIDTH + 2 * B_KW + 2 * B_VW + 2 * B_RANK
IN_CD = C_WIDTH + 2 * D_WIDTH

kernel_name = "hybrid_dilated_gla_s5_conformer_encoder"


def split_sizes(t, sizes):
    idx = np.cumsum(sizes)[:-1].tolist()
    return jnp.split(t, idx, axis=-1)


def rmsnorm(x, g):
    xf = x.astype(jnp.float32)
    r = lax.rsqrt(jnp.mean(xf * xf, -1, keepdims=True) + EPS)
    return (xf * r).astype(x.dtype) * g


def layernorm(x, g, b):
    xf = x.astype(jnp.float32)
    mu = jnp.mean(xf, -1, keepdims=True)
    var = jnp.mean(jnp.square(xf - mu), -1, keepdims=True)
    return ((xf - mu) * lax.rsqrt(var + EPS)).astype(x.dtype) * g + b


def rope(x, cos, sin):
    x1, x2 = jnp.split(x, 2, axis=-1)
    return jnp.concatenate([x1 * cos - x2 * sin, x2 * cos + x1 * sin], axis=-1)


def depthwise_conv(u, w, b):
    K = w.shape[0]
    out = lax.conv_general_dilated(
        u, w[:, None, :], window_strides=(1,), padding=[((K - 1) // 2, K // 2)],
        dimension_numbers=('NWC', 'WIO', 'NWC'), feature_group_count=u.shape[-1])
    return out + b


def dilated_window_attn(q, k, v, dilation, side):
    Bn, S, H, hd = q.shape
    L = S // dilation
    nb = -(-L // side)
    Lp = nb * side

    def to_blocks(t):
        t = t.reshape(Bn, L, dilation, H, hd)
        t = jnp.pad(t, ((0, 0), (0, Lp - L), (0, 0), (0, 0), (0, 0)))
        return t.reshape(Bn, nb, side, dilation, H, hd)

    def neighbours(t):
        tp = jnp.pad(t, ((0, 0), (1, 1), (0, 0), (0, 0), (0, 0), (0, 0)))
        return jnp.concatenate([tp[:, :-2], tp[:, 1:-1], tp[:, 2:]], axis=2)

    qb = to_blocks(q)
    kw = neighbours(to_blocks(k))
    vw = neighbours(to_blocks(v))
    s = jnp.einsum('bnqrhd,bnkrhd->bnrhqk', qb, kw).astype(jnp.float32) * (hd ** -0.5)
    q_idx = jnp.arange(nb)[:, None] * side + jnp.arange(side)[None, :]
    k_idx = (jnp.arange(nb)[:, None] - 1) * side + jnp.arange(3 * side)[None, :]
    rel = k_idx[:, None, :] - q_idx[:, :, None]
    valid = (jnp.abs(rel) <= side) & (k_idx[:, None, :] >= 0) & (k_idx[:, None, :] < L)
    s = jnp.where(valid[None, :, None, None], s, -1e30)
    m = jnp.max(s, -1, keepdims=True)
    p = jnp.exp(s - m)
    den = jnp.sum(p, -1, keepdims=True)
    o = jnp.einsum('bnrhqk,bnkrhd->bnqrhd', (p / den).astype(v.dtype), vw)
    lse = (m + jnp.log(den))[..., 0]
    o = o.reshape(Bn, Lp, dilation, H, hd)[:, :L].reshape(Bn, S, H, hd)
    lse = jnp.transpose(lse, (0, 1, 4, 2, 3)).reshape(Bn, Lp, dilation, H)[:, :L].reshape(Bn, S, H)
    return o, lse


def dilated_mixture_attention(q, k, v):
    outs, lses = [], []
    for window, dilation in A_PATTERNS:
        o, l = dilated_window_attn(q, k, v, dilation, window // (2 * dilation))
        outs.append(o)
        lses.append(l)
    w = jax.nn.softmax(jnp.stack(lses, 0), axis=0)
    return jnp.einsum('gbsh,gbshd->bshd', w.astype(q.dtype), jnp.stack(outs, 0))


def gla_chunked(q, k, v, logg, include_diag):
    Bn, H, S, dk = q.shape
    dv = v.shape[-1]
    C = B_CHUNK
    N = S // C
    qc = q.reshape(Bn, H, N, C, dk)
    kc = k.reshape(Bn, H, N, C, dk)
    vc = v.reshape(Bn, H, N, C, dv)
    G = jnp.cumsum(logg.reshape(Bn, H, N, C, dk), axis=3)
    Gtot = G[:, :, :, -1:, :]
    q_in = qc * jnp.exp(G)
    k_in = kc * jnp.exp(-G)
    att = jnp.einsum('bhnik,bhnjk->bhnij', q_in, k_in)
    mask = jnp.tril(jnp.ones((C, C), dtype=bool), 0 if include_diag else -1)
    att = jnp.where(mask, att, jnp.zeros_like(att))
    o_intra = jnp.einsum('bhnij,bhnjv->bhniv', att, vc)
    U = jnp.einsum('bhnjk,bhnjv->bhnkv', kc * jnp.exp(Gtot - G), vc)
    a = jnp.exp(Gtot[:, :, :, 0, :])

    def step(s_prev, inp):
        a_n, u_n = inp
        return a_n[..., None] * s_prev + u_n, s_prev

    s0 = jnp.zeros((Bn, H, dk, dv), dtype=U.dtype)
    _, s_before = lax.scan(step, s0, (jnp.moveaxis(a, 2, 0), jnp.moveaxis(U, 2, 0)))
    s_before = jnp.moveaxis(s_before, 0, 2)
    o_inter = jnp.einsum('bhnik,bhnkv->bhniv', q_in, s_before)
    return (o_intra + o_inter).reshape(Bn, H, S, dv)


def head_rmsnorm(o, g):
    Bn, H, S, dv = o.shape
    of = o.astype(jnp.float32)
    of = of * lax.rsqrt(jnp.mean(of * of, -1, keepdims=True) + EPS)
    return of.astype(o.dtype).transpose(0, 2, 1, 3).reshape(Bn, S, H * dv) * g


def mixer_ab(hn, cos, sin, w_in, w_out, wg2, bg, g_norm):
    Bn, S, _ = hn.shape
    q_a, k_a, v_a, q_b, k_b, v_b, r_b, z_f, z_b = split_sizes(
        hn @ w_in, (A_WIDTH, A_WIDTH, A_WIDTH, B_KW, B_KW, B_VW, B_VW, B_RANK, B_RANK))
    heads_a = lambda t: t.reshape(Bn, S, A_HEADS, A_HEAD_DIM)
    qa = rope(heads_a(q_a), cos, sin)
    ka = rope(heads_a(k_a), cos, sin)
    o_a = dilated_mixture_attention(qa, ka, heads_a(v_a)).reshape(Bn, S, A_WIDTH)
    heads_b = lambda t, d: t.reshape(Bn, S, B_HEADS, d).transpose(0, 2, 1, 3)
    qb = heads_b(q_b, B_DK) * (B_DK ** -0.5)
    kb = heads_b(k_b, B_DK)
    vb = heads_b(v_b, B_DV)

    def log_gate(z, w2, b):
        logit = (z @ w2 + b).astype(jnp.float32)
        return heads_b((jax.nn.log_sigmoid(logit) / B_TAU).astype(hn.dtype), B_DK)

    g_f = log_gate(z_f, wg2[0], bg[0])
    g_b = log_gate(z_b, wg2[1], bg[1])
    flip = lambda t: jnp.flip(t, axis=2)
    o_b = gla_chunked(qb, kb, vb, g_f, True) + flip(
        gla_chunked(flip(qb), flip(kb), flip(vb), flip(g_b), False))
    o_b = head_rmsnorm(o_b.astype(hn.dtype), g_norm) * jax.nn.silu(r_b)
    return jnp.concatenate([o_a, o_b], axis=-1) @ w_out


def s5_scan(u, lam_re, lam_im, log_dt, b_re, b_im, c_re, c_im):
    f32 = jnp.float32
    S = u.shape[1]
    dt = jnp.exp(log_dt.astype(f32))[:, None]
    lr, li = lam_re.astype(f32), lam_im.astype(f32)
    mag = jnp.exp(lr * dt)
    ab_re, ab_im = mag * jnp.cos(li * dt), mag * jnp.sin(li * dt)
    den = lr * lr + li * li
    nr = ab_re - 1.0
    f_re = (nr * lr + ab_im * li) / den
    f_im = (ab_im * lr - nr * li) / den
    br, bi = b_re.astype(f32), b_im.astype(f32)
    bb_re = (f_re[..., None] * br - f_im[..., None] * bi).astype(u.dtype)
    bb_im = (f_re[..., None] * bi + f_im[..., None] * br).astype(u.dtype)
    bu_re = jnp.einsum('gph,bsgh->bsgp', bb_re, u)
    bu_im = jnp.einsum('gph,bsgh->bsgp', bb_im, u)
    G, P = lam_re.shape
    a_re = jnp.broadcast_to(ab_re.astype(u.dtype)[None, None], (1, S, G, P))
    a_im = jnp.broadcast_to(ab_im.astype(u.dtype)[None, None], (1, S, G, P))

    def combine(e1, e2):
        a1r, a1i, b1r, b1i = e1
        a2r, a2i, b2r, b2i = e2
        return (a1r * a2r - a1i * a2i, a1r * a2i + a1i * a2r,
                a2r * b1r - a2i * b1i + b2r, a2r * b1i + a2i * b1r + b2i)

    _, _, s_re, s_im = lax.associative_scan(combine, (a_re, a_im, bu_re, bu_im), axis=1)
    return (jnp.einsum('ghp,bsgp->bsgh', c_re, s_re)
            - jnp.einsum('ghp,bsgp->bsgh', c_im, s_im))


def conformer_conv(h, conv_w, conv_b, ln_g, ln_b):
    val, gate = jnp.split(h, 2, axis=-1)
    u = depthwise_conv(val * jax.nn.sigmoid(gate), conv_w, conv_b)
    return jax.nn.silu(layernorm(u, ln_g, ln_b))


def mixer_cd(hn, w_in, w_out, lam_re, lam_im, log_dt, b_re, b_im, c_re, c_im, d_skip,
             w_glu, b_glu, conv_w, conv_b, ln_g, ln_b):
    Bn, S, _ = hn.shape
    u_c, h_d = split_sizes(hn @ w_in, (C_WIDTH, 2 * D_WIDTH))
    u = u_c.reshape(Bn, S, C_NGROUPS, C_GROUP)
    y_f = s5_scan(u, lam_re[0], lam_im[0], log_dt[0], b_re[0], b_im[0], c_re[0], c_im[0])
    y_b = jnp.flip(s5_scan(jnp.flip(u, 1), lam_re[1], lam_im[1], log_dt[1],
                           b_re[1], b_im[1], c_re[1], c_im[1]), 1)
    y = (y_f + y_b).reshape(Bn, S, C_WIDTH) + d_skip * u_c
    z = jax.nn.gelu(y)
    o_c = z * jax.nn.sigmoid(z @ w_glu + b_glu)
    o_d = conformer_conv(h_d, conv_w, conv_b, ln_g, ln_b)
    return jnp.concatenate([o_c, o_d], axis=-1) @ w_out


def cross_attention(xn, memn, wq, wkv, wo):
    Bn, S, _ = xn.shape
    M = memn.shape[1]
    q = (xn @ wq).reshape(Bn, S, X_HEADS, X_HEAD_DIM)
    k, v = jnp.split(memn @ wkv, 2, axis=-1)
    k = k.reshape(Bn, M, X_HEADS, X_HEAD_DIM)
    v = v.reshape(Bn, M, X_HEADS, X_HEAD_DIM)
    s = jnp.einsum('bshd,bmhd->bhsm', q, k).astype(jnp.float32) * (X_HEAD_DIM ** -0.5)
    p = jax.nn.softmax(s, axis=-1).astype(v.dtype)
    o = jnp.einsum('bhsm,bmhd->bshd', p, v).reshape(Bn, S, D_MODEL)
    return o @ wo


def conv_ffn(hn, w_up, w_conv, b_conv, w_down):
    u = depthwise_conv(hn @ w_up, w_conv, b_conv)
    val, gate = jnp.split(u, 2, axis=-1)
    return (jax.nn.silu(gate) * val) @ w_down


def setup_inputs(seed: int = 0) -> dict:
    key = jax.random.key(seed)
    ks = jax.random.split(key, 40)
    nrm = lambda k, shape, scale: jax.random.normal(k, shape, jnp.float32) * scale
    gain = lambda k, shape: 1.0 + 0.02 * jax.random.normal(k, shape, jnp.float32)
    D, F = D_MODEL, D_FF
    offsets = jax.random.randint(ks[2], (BATCH, 1), 0, 8192, dtype=jnp.int32)
    positions = (jnp.arange(SEQ, dtype=jnp.int32)[None, :] + offsets).astype(jnp.int32)
    log_dt_lo, log_dt_hi = math.log(1e-3), math.log(1e-1)
    lam_im0 = math.pi * jnp.arange(C_STATE, dtype=jnp.float32)
    return {
        "x": nrm(ks[0], (BATCH, SEQ, D), 1.0),
        "mem": nrm(ks[1], (BATCH, N_MEM, D), 1.0),
        "positions": positions,
        "g_mix": gain(ks[3], (DEPTH, D)),
        "g_xattn": gain(ks[4], (DEPTH, D)),
        "g_mem": gain(ks[5], (DEPTH, D)),
        "w_xq": nrm(ks[6], (DEPTH, D, D), D ** -0.5),
        "w_xkv": nrm(ks[7], (DEPTH, D, 2 * D), D ** -0.5),
        "w_xo": nrm(ks[8], (DEPTH, D, D), D ** -0.5),
        "g_ffn": gain(ks[9], (DEPTH, D)),
        "w_up": nrm(ks[10], (DEPTH, D, 2 * F), D ** -0.5),
        "w_conv_ffn": nrm(ks[11], (DEPTH, FFN_KERNEL, 2 * F), FFN_KERNEL ** -0.5),
        "b_conv_ffn": nrm(ks[12], (DEPTH, 2 * F), 0.01),
        "w_down": nrm(ks[13], (DEPTH, F, D), F ** -0.5),
        "w_in_ab": nrm(ks[14], (N_EVEN, D, IN_AB), D ** -0.5),
        "w_out_ab": nrm(ks[15], (N_EVEN, D, D), D ** -0.5),
        "gla_wg2": nrm(ks[16], (N_EVEN, 2, B_RANK, B_KW), B_RANK ** -0.5),
        "gla_bg": nrm(ks[17], (N_EVEN, 2, B_KW), 0.1),
        "gla_norm": gain(ks[18], (N_EVEN, B_VW)),
        "w_in_cd": nrm(ks[19], (N_ODD, D, IN_CD), D ** -0.5),
        "w_out_cd": nrm(ks[20], (N_ODD, D, D), D ** -0.5),
        "s5_lam_re": -0.5 + nrm(ks[21], (N_ODD, 2, C_NGROUPS, C_STATE), 0.01),
        "s5_lam_im": lam_im0 + nrm(ks[22], (N_ODD, 2, C_NGROUPS, C_STATE), 0.01),
        "s5_log_dt": log_dt_lo + (log_dt_hi - log_dt_lo) * jax.random.uniform(
            ks[23], (N_ODD, 2, C_NGROUPS), jnp.float32),
        "s5_b_re": nrm(ks[24], (N_ODD, 2, C_NGROUPS, C_STATE, C_GROUP), (2.0 * C_GROUP) ** -0.5),
        "s5_b_im": nrm(ks[25], (N_ODD, 2, C_NGROUPS, C_STATE, C_GROUP), (2.0 * C_GROUP) ** -0.5),
        "s5_c_re": nrm(ks[26], (N_ODD, 2, C_NGROUPS, C_GROUP, C_STATE), (2.0 / C_STATE) ** 0.5),
        "s5_c_im": nrm(ks[27], (N_ODD, 2, C_NGROUPS, C_GROUP, C_STATE), (2.0 / C_STATE) ** 0.5),
        "s5_d": nrm(ks[28], (N_ODD, C_WIDTH), 1.0),
        "s5_w_glu": nrm(ks[29], (N_ODD, C_WIDTH, C_WIDTH), C_WIDTH ** -0.5),
        "s5_b_glu": nrm(ks[30], (N_ODD, C_WIDTH), 0.01),
        "conv_w": nrm(ks[31], (N_ODD, D_KERNEL, D_WIDTH), D_KERNEL ** -0.5),
        "conv_b": nrm(ks[32], (N_ODD, D_WIDTH), 0.01),
        "conv_ln_g": gain(ks[33], (N_ODD, D_WIDTH)),
        "conv_ln_b": nrm(ks[34], (N_ODD, D_WIDTH), 0.01),
        "g_final": gain(ks[35], (D,)),
    }


def reference(x, mem, positions, g_mix, g_xattn, g_mem, w_xq, w_xkv, w_xo, g_ffn, w_up,
              w_conv_ffn, b_conv_ffn, w_down, w_in_ab, w_out_ab, gla_wg2, gla_bg, gla_norm,
              w_in_cd, w_out_cd, s5_lam_re, s5_lam_im, s5_log_dt, s5_b_re, s5_b_im, s5_c_re,
              s5_c_im, s5_d, s5_w_glu, s5_b_glu, conv_w, conv_b, conv_ln_g, conv_ln_b, g_final):
    inv_freq = ROPE_THETA ** (-jnp.arange(0, A_HEAD_DIM, 2, dtype=jnp.float32) / A_HEAD_DIM)
    ang = positions.astype(jnp.float32)[..., None] * inv_freq
    cos = jnp.cos(ang)[:, :, None, :].astype(x.dtype)
    sin = jnp.sin(ang)[:, :, None, :].astype(x.dtype)
    h = x
    for layer in range(DEPTH):
        i = layer // 2
        hn = rmsnorm(h, g_mix[layer])
        if layer % 2 == 0:
            h = h + mixer_ab(hn, cos, sin, w_in_ab[i], w_out_ab[i], gla_wg2[i], gla_bg[i],
                             gla_norm[i])
        else:
            h = h + mixer_cd(hn, w_in_cd[i], w_out_cd[i], s5_lam_re[i], s5_lam_im[i],
                             s5_log_dt[i], s5_b_re[i], s5_b_im[i], s5_c_re[i], s5_c_im[i],
                             s5_d[i], s5_w_glu[i], s5_b_glu[i], conv_w[i], conv_b[i],
                             conv_ln_g[i], conv_ln_b[i])
        memn = rmsnorm(mem, g_mem[layer])
        h = h + cross_attention(rmsnorm(h, g_xattn[layer]), memn, w_xq[layer], w_xkv[layer],
                                w_xo[layer])
        h = h + conv_ffn(rmsnorm(h, g_ffn[layer]), w_up[layer], w_conv_ffn[layer],
                         b_conv_ffn[layer], w_down[layer])
    return rmsnorm(h, g_final)
```

```python
import math
import numpy as np
from contextlib import ExitStack
import concourse.bass as bass
import concourse.mybir as mybir
from concourse.bass_utils import run_bass_kernel_spmd

F32 = mybir.dt.float32
BF16 = mybir.dt.bfloat16
I32 = mybir.dt.int32
AF = mybir.ActivationFunctionType
ALU = mybir.AluOpType

D = 1024
S = 2048
NS = 2
T = NS * S
DEPTH = 4
NMEM = 256
DFF = 2816
EPS = 1e-6
NCORES = 8
TWO_PI = 2.0 * math.pi

ENGS = ("pe", "act", "dve", "pool", "sp")
N_DMA_SLOTS = 16


class Buf:
    __slots__ = ("w", "r")

    def __init__(self):
        self.w = None
        self.r = []


class Op:
    __slots__ = ("eng", "fn", "deps", "dma", "needed", "ev", "slot")

    def __init__(self, eng, fn, dma):
        self.eng = eng
        self.fn = fn
        self.deps = set()
        self.dma = dma
        self.needed = False
        self.ev = None
        self.slot = None


class Prog:
    def __init__(self, nc, stack):
        self.nc = nc
        self.esem = {e: stack.enter_context(nc.semaphore("s_" + e)) for e in ENGS}
        self.dsem = {}
        for e in ("sp", "act", "pool"):
            for k in range(N_DMA_SLOTS):
                self.dsem[(e, k)] = stack.enter_context(nc.semaphore("d_%s%d" % (e, k)))
        self.bar = stack.enter_context(nc.semaphore("bar"))
        self.cnt = {e: 0 for e in ENGS}
        self.dcnt = {}
        self.dma_count = {e: 0 for e in ENGS}
        self.slot_last = {}
        self.nphase = 0
        self.ops = None
        self.total_ops = 0

    def begin(self):
        self.ops = {e: [] for e in ENGS}

    def op(self, eng, fn, reads=(), writes=(), dma=False):
        o = Op(eng, fn, dma)
        for b in reads:
            if b.w is not None:
                o.deps.add(b.w)
        for b in writes:
            if b.w is not None:
                o.deps.add(b.w)
            for r in b.r:
                o.deps.add(r)
        for b in reads:
            b.r.append(o)
        for b in writes:
            b.w = o
            b.r = []
        if dma:
            k = self.dma_count[eng]
            self.dma_count[eng] = k + 1
            o.slot = (eng, k % N_DMA_SLOTS)
            prev = self.slot_last.get(o.slot)
            if prev is not None:
                o.deps.add(prev)
            self.slot_last[o.slot] = o
            o.needed = True
        o.deps.discard(o)
        self.ops[eng].append(o)
        return o

    def dma(self, eng, out, in_, reads=(), writes=(), **kw):
        return self.op(eng, lambda e: e.dma_start(out=out, in_=in_, **kw), reads, writes, dma=True)

    def end(self):
        nc = self.nc
        ops = self.ops
        for e in ENGS:
            for o in ops[e]:
                for d in o.deps:
                    if not (d.eng == "pe" and e == "pe" and not d.dma):
                        d.needed = True
            if ops[e]:
                ops[e][-1].needed = True
        for e in ENGS:
            for o in ops[e]:
                if o.ev is not None:
                    continue
                if o.dma:
                    v = self.dcnt.get(o.slot, 0) + 16
                    self.dcnt[o.slot] = v
                    o.ev = (self.dsem[o.slot], v)
                elif o.needed:
                    self.cnt[e] += 1
                    o.ev = (self.esem[e], self.cnt[e])
        phase = self.nphase
        bar = self.bar
        tail = []
        for e in ENGS:
            last_c = None
            for o in ops[e]:
                if o.dma:
                    tail.append(o.ev)
                else:
                    last_c = o
            if last_c is not None:
                tail.append(last_c.ev)

        def body(ename):
            elist = ops[ename]

            def run(eng):
                known = {}
                if phase > 0:
                    eng.wait_ge(bar, phase)
                for o in elist:
                    need = {}
                    for d in o.deps:
                        if d.eng == "pe" and ename == "pe" and not d.dma:
                            continue
                        if d.ev is None:
                            continue
                        s, v = d.ev
                        if known.get(s, 0) < v and need.get(s, 0) < v:
                            need[s] = v
                    for s, v in need.items():
                        eng.wait_ge(s, v)
                        known[s] = v
                    ins = o.fn(eng)
                    if o.ev is not None:
                        ins.then_inc(o.ev[0], 16 if o.dma else 1)
                if ename == "sp":
                    need = {}
                    for s, v in tail:
                        if known.get(s, 0) < v and need.get(s, 0) < v:
                            need[s] = v
                    for s, v in need.items():
                        eng.wait_ge(s, v)
                    eng.sem_inc(bar, 1)

            return run

        with nc.Block() as block:
            block.sync(body("sp"))
            if ops["pe"]:
                block.tensor(body("pe"))
            if ops["act"]:
                block.scalar(body("act"))
            if ops["dve"]:
                block.vector(body("dve"))
            if ops["pool"]:
                block.gpsimd(body("pool"))
        self.total_ops += sum(len(v) for v in ops.values())
        self.nphase += 1
        self.ops = None


def _chunks(v):
    return np.ascontiguousarray(np.asarray(v, np.float32).reshape(-1, 128).T)


class PPack:
    def __init__(self):
        self.cols = []
        self.idx = {}
        self.n = 0

    def add(self, name, arr128xn):
        a = np.asarray(arr128xn, np.float32)
        if a.ndim == 1:
            a = a[:, None]
        assert a.shape[0] == 128
        self.idx[name] = self.n
        self.cols.append(a)
        self.n += a.shape[1]

    def build(self):
        return np.ascontiguousarray(np.concatenate(self.cols, axis=1))


def rope_perm():
    perm = np.zeros(512, np.int64)
    for h in range(8):
        for j in range(64):
            perm[h * 64 + j] = h * 64 + (j + 32) % 64
    return perm


def attn_masks():
    out = np.zeros((128, 20, 512), np.float32)
    i = np.arange(128)[:, None]
    j = np.arange(512)[None, :]
    for n in range(20):
        dl = 128 * (n - 8) + i - j
        a = np.abs(dl)
        m = (a <= 64).astype(np.float32) + ((dl % 4 == 0) & (a <= 256)) + ((dl % 16 == 0) & (a <= 1024))
        out[:, n, :] = m
    return out.reshape(128, 20 * 512)


def gla_consts():
    r = np.arange(128)[:, None]
    t = np.arange(128)[None, :]
    same = (r // 64) == (t // 64)
    sc = -1.0 / 16.0
    a_pi = np.where(same & (r <= t), sc, 0.0)
    a_si = np.where(same & (r >= t), sc, 0.0)
    a_se = np.where(same & (r > t), sc, 0.0)
    a_pe = np.where(same & (r < t), sc, 0.0)
    mf = np.where(same & (r <= t), 1.0, 0.0)
    mb = np.where(same & (r > t), 1.0, 0.0)
    ch = np.zeros((128, 2)); ch[:64, 0] = sc; ch[64:, 1] = sc
    return np.concatenate([a_pi, a_si, a_se, a_pe, mf, mb, ch], axis=1).astype(np.float32)


class KB:
    def __init__(self, ppidx, npp, dbg=(), stop_after=None):
        self.ppidx = ppidx
        self.npp = npp
        self.dbg = set(dbg)
        self.stop_after = stop_after
        self.nc = bass.Bass("TRN2", target_bir_lowering=False)
        self.din = {}
        self.scr = {}
        self.sbuf = {}
        self.uid = 0

    def inp(self, name, shape, dt=F32):
        self.din[name] = self.nc.dram_tensor(name, list(shape), dt, kind="ExternalInput").ap()
        return self.din[name]

    def scratch(self, name, shape, dt):
        kind = "ExternalOutput" if name in self.dbg else "Internal"
        t = self.nc.dram_tensor(name, list(shape), dt, kind=kind).ap()
        self.scr[name] = (t, Buf())
        return t

    def sb(self, st, name, shape, dt):
        self.uid += 1
        return st.enter_context(self.nc.sbuf_tensor("%s_u%d" % (name, self.uid), list(shape), dt))

    def ps(self, st, name, shape=(128, 512), dt=F32):
        self.uid += 1
        return st.enter_context(self.nc.psum_tensor("%s_u%d" % (name, self.uid), list(shape), dt))

    def pcol(self, name, off=0):
        i = self.ppidx[name] + off
        return self.pp[:, i:i + 1]


def build_program(ppidx, npp, dbg=(), stop_after=None):
    K = KB(ppidx, npp, dbg, stop_after)
    nc = K.nc
    xT = K.inp("xT", [8, 128, T])
    memT = K.inp("memT", [8, 128, NS * NMEM])
    pos = K.inp("pos", [1, T], I32)
    ppd = K.inp("pp", [128, npp])
    maskd = K.inp("amask", [128, 20 * 512])
    glac = K.inp("glac", [128, 770])
    identd = K.inp("ident", [128, 128])
    w_in_ab = K.inp("w_in_ab", [2, D, 3104])
    w_qkperm = K.inp("w_qkperm", [2, D, 1024])
    w_out_ab = K.inp("w_out_ab", [2, D, D])
    w2blk_d = K.inp("w2blk", [2, 33, 512])
    w_in_cd = K.inp("w_in_cd", [2, D, 1536])
    w_out_cd = K.inp("w_out_cd", [2, D, D])
    w_xq = K.inp("w_xq", [DEPTH, D, D])
    w_xkv = K.inp("w_xkv", [DEPTH, D, 2 * D])
    w_xo = K.inp("w_xo", [DEPTH, D, D])
    w_up = K.inp("w_up", [DEPTH, D, 2 * DFF])
    w_down = K.inp("w_down", [DEPTH, DFF, D])
    w_glu = K.inp("w_glu", [2, 512, 512])
    s5rows = K.inp("s5rows", [2, 3, 32, 128])
    s5lane = K.inp("s5lane", [2, 128, 96])
    s5B = K.inp("s5B", [2, 2, 2, 128, 2048])
    s5C = K.inp("s5C", [2, 2, 2, 128, 2048])
    s5idx = K.inp("s5idx", [1, 96])
    out_d = nc.dram_tensor("outT", [8, 128, T], F32, kind="ExternalOutput").ap()

    Hs = [K.scratch("H0", [8, 128, T], F32), K.scratch("H1", [8, 128, T], F32)]
    bHs = [K.scr["H0"][1], K.scr["H1"][1]]
    hsel = [0]
    COS = K.scratch("COS", [128, T], F32)
    SINS = K.scratch("SINS", [128, T], F32)
    MIXO = K.scratch("MIXO", [8, 128, T], BF16)
    QA = K.scratch("QA", [4, 128, T], BF16)
    KA = K.scratch("KA", [4, 128, T], BF16)
    VA = K.scratch("VA", [T, 512], BF16)
    QB = K.scratch("QB", [2, 128, T], F32)
    KBf = K.scratch("KBF", [2, 128, T], F32)
    KBT = K.scratch("KBT", [T, 256], F32)
    VB = K.scratch("VB", [T, 512], BF16)
    RB = K.scratch("RB", [4, 128, T], BF16)
    LG = K.scratch("LG", [T, 512], F32)
    UC = K.scratch("UC", [4, 128, T], F32)
    UCB = K.scratch("UCB", [4, 128, T], BF16)
    DV = K.scratch("DV", [4, 128, T], F32)
    XQ = K.scratch("XQ", [8, 128, T], BF16)
    XK = K.scratch("XK", [8, 128, NS * NMEM], BF16)
    XV = K.scratch("XV", [NS * NMEM, D], BF16)
    HID = K.scratch("HID", [22, 128, T], BF16)
    bXT = Buf()

    def B(name):
        return K.scr[name][1]

    with ExitStack() as top:
        P = Prog(nc, top)
        K.P = P
        pp = K.sb(top, "pp", [128, npp], F32); K.pp = pp; bpp = Buf()
        ones_bf = K.sb(top, "ones_bf", [128, 128], BF16)
        onesD = K.sb(top, "onesD", [128, 128], BF16)
        ones128 = K.sb(top, "ones128", [128, 128], BF16)
        ones512f = K.sb(top, "ones512f", [128, 128], F32)
        bconst = Buf()

        stop = [False]

        def phase_done(name):
            P.end()
            if K.stop_after == name:
                stop[0] = True
            return stop[0]

        P.begin()
        with ExitStack() as st:
            P.dma("sp", pp[:], ppd, writes=[bpp])
            P.op("dve", lambda e: e.memset(ones_bf[:], 1.0), [], [bconst])
            P.op("dve", lambda e: e.memset(onesD[:], 1.0 / 1024.0), [], [bconst])
            P.op("dve", lambda e: e.memset(ones128[:], 1.0 / 128.0), [], [bconst])
            P.op("dve", lambda e: e.memset(ones512f[:], 1.0 / 512.0), [], [bconst])
            posi = K.sb(st, "posi", [128, T], I32); bposi = Buf()
            ang = K.sb(st, "ang", [128, T], F32); bang = Buf()
            ki = K.sb(st, "ki", [128, T], I32); bki = Buf()
            kf = K.sb(st, "kf", [128, T], F32); bkf = Buf()
            tb_ = K.sb(st, "ttab", [128, T], F32); btab = Buf()
            invf = K.sb(st, "invf", [128, 1], F32); binvf = Buf()
            P.dma("sp", posi[:], pos.partition_broadcast(128), writes=[bposi])
            P.op("dve", lambda e: e.tensor_copy(out=invf[:], in_=K.pcol("invfreq")), [bpp], [binvf])
            P.op("dve", lambda e: e.tensor_copy(out=ang[:], in_=posi[:]), [bposi], [bang])
            P.op("dve", lambda e: e.tensor_scalar(out=ang[:], in0=ang[:], scalar1=invf[:, 0:1], scalar2=None,
                                                  op0=ALU.mult), [bang, binvf], [bang])
            C1 = 6.28125
            C2 = TWO_PI - C1
            P.op("dve", lambda e: e.tensor_scalar(out=ki[:], in0=ang[:], scalar1=1.0 / TWO_PI, scalar2=None,
                                                  op0=ALU.mult), [bang], [bki])
            P.op("dve", lambda e: e.tensor_copy(out=kf[:], in_=ki[:]), [bki], [bkf])
            P.op("dve", lambda e: e.scalar_tensor_tensor(out=ang[:], in0=kf[:], scalar=-C1, in1=ang[:],
                                                         op0=ALU.mult, op1=ALU.add), [bkf, bang], [bang])
            P.op("dve", lambda e: e.scalar_tensor_tensor(out=ang[:], in0=kf[:], scalar=-C2, in1=ang[:],
                                                         op0=ALU.mult, op1=ALU.add), [bkf, bang], [bang])
            PI_S = 3.1415925
            P.op("dve", lambda e: e.tensor_scalar(out=ang[:], in0=ang[:], scalar1=-PI_S, scalar2=PI_S,
                                                  op0=ALU.max, op1=ALU.min), [bang], [bang])
            P.op("act", lambda e: e.activation(out=tb_[:], in_=ang[:], func=AF.Sin), [bang], [btab])
            P.op("dve", lambda e: e.tensor_scalar(out=tb_[:], in0=tb_[:], scalar1=K.pcol("ropesign"), scalar2=None,
                                                  op0=ALU.mult), [btab, bpp], [btab])
            P.dma("act", SINS, tb_[:], reads=[btab], writes=[B("SINS")])
            P.op("dve", lambda e: e.tensor_single_scalar(out=kf[:], in_=ang[:], scalar=math.pi / 2.0, op=ALU.is_gt),
                 [bang], [bkf])
            P.op("dve", lambda e: e.scalar_tensor_tensor(out=ang[:], in0=kf[:], scalar=-TWO_PI, in1=ang[:],
                                                         op0=ALU.mult, op1=ALU.add), [bkf, bang], [bang])
            P.op("dve", lambda e: e.tensor_scalar(out=ang[:], in0=ang[:], scalar1=math.pi / 2.0, scalar2=PI_S,
                                                  op0=ALU.add, op1=ALU.min), [bang], [bang])
            tb2 = K.sb(st, "ttab2", [128, T], F32); btab2 = Buf()
            P.op("act", lambda e: e.activation(out=tb2[:], in_=ang[:], func=AF.Sin), [bang], [btab2])
            P.dma("act", COS, tb2[:], reads=[btab2], writes=[B("COS")])
            if phase_done("const"):
                return K

        def emit_norm(st, src, bsrc, gname, hn, bhn, out_f32=None):
            ht = [K.sb(st, "n_ht%d" % i, [128, 8, 512], F32) for i in range(2)]
            bht = [Buf(), Buf()]
            sq = [K.sb(st, "n_sq%d" % i, [128, 512], BF16) for i in range(2)]
            bsq = [Buf(), Buf()]
            sd = K.sb(st, "n_sd", [128, 512], F32); bsd = Buf()
            rs = K.sb(st, "n_rs", [128, 512], F32); brs = Buf()
            pst = K.ps(st, "n_ps"); bps = Buf()
            ost = None
            if out_f32 is not None:
                ost = [K.sb(st, "n_o%d" % i, [128, 512], F32) for i in range(2)]
                bost = [Buf(), Buf()]
            ntok = src.shape[2]
            for tb in range(ntok // 512):
                sl = slice(tb * 512, (tb + 1) * 512)
                h_ = ht[tb % 2]; bh_ = bht[tb % 2]
                P.dma("sp", h_[:], src[:, :, sl].rearrange("c p t -> p c t"), reads=[bsrc], writes=[bh_])
                for c in range(8):
                    s_ = sq[c % 2]; bs_ = bsq[c % 2]
                    P.op("act", lambda e, s_=s_, h_=h_, c=c: e.activation(out=s_[:], in_=h_[:, c, :], func=AF.Square),
                         [bh_], [bs_])
                    P.op("pe", lambda e, s_=s_, c=c: e.matmul(pst[:], lhsT=onesD[:], rhs=s_[:], start=(c == 0), stop=(c == 7)),
                         [bs_, bconst], [bps])
                P.op("act", lambda e: e.activation(out=sd[:], in_=pst[:], func=AF.Sqrt, bias=EPS), [bps], [bsd])
                P.op("dve", lambda e: e.reciprocal(out=rs[:], in_=sd[:]), [bsd], [brs])
                for c in range(8):
                    if out_f32 is None:
                        P.op("dve", lambda e, h_=h_, c=c, sl=sl: e.scalar_tensor_tensor(
                            out=hn[:, c, sl], in0=h_[:, c, :], scalar=K.pcol(gname, c), in1=rs[:],
                            op0=ALU.mult, op1=ALU.mult), [bh_, brs, bpp], [bhn[tb] if isinstance(bhn, list) else bhn])
                    else:
                        o_ = ost[c % 2]; bo_ = bost[c % 2]
                        P.op("dve", lambda e, h_=h_, c=c, o_=o_: e.scalar_tensor_tensor(
                            out=o_[:], in0=h_[:, c, :], scalar=K.pcol(gname, c), in1=rs[:],
                            op0=ALU.mult, op1=ALU.mult), [bh_, brs, bpp], [bo_])
                        P.dma("act", out_f32[c, :, sl], o_[:], reads=[bo_], writes=[])

        class Lin:
            def __init__(self, st, KC, npair=1, tag="l"):
                self.KC = KC
                self.npair = npair
                self.wt = [[K.sb(st, "%s_w%d_%d" % (tag, i, j), [128, KC, 128], BF16) for j in range(npair)] for i in range(2)]
                self.bw = [[Buf() for j in range(npair)] for i in range(2)]
                self.pst = [[K.ps(st, "%s_p%d_%d" % (tag, i, j)) for j in range(npair)] for i in range(2)]
                self.bp = [[Buf() for j in range(npair)] for i in range(2)]
                self.it = 0
                self.pit = 0

            def run(self, xin, bxin, ntok, wspecs, epilogue, M=128):
                wi = self.it % 2
                self.it += 1
                for j, (W, c0) in enumerate(wspecs):
                    src = W.rearrange("(kc p) f -> p kc f", p=128)[:, :, c0:c0 + M]
                    P.dma("pool", self.wt[wi][j][:, :, 0:M], src, writes=[self.bw[wi][j]])
                for tb in range(ntok // 512):
                    sl = slice(tb * 512, (tb + 1) * 512)
                    pi = self.pit % 2
                    self.pit += 1
                    for j in range(len(wspecs)):
                        wt = self.wt[wi][j]; ps_ = self.pst[pi][j]
                        for kc in range(self.KC):
                            P.op("pe", lambda e, wt=wt, ps_=ps_, kc=kc, sl=sl: e.matmul(
                                ps_[0:M, :], lhsT=wt[:, kc, 0:M], rhs=xin[:, kc, sl], start=(kc == 0), stop=(kc == self.KC - 1)),
                                [self.bw[wi][j], (bxin[tb] if isinstance(bxin, list) else bxin)], [self.bp[pi][j]])
                    epilogue(tb, sl, self.pst[pi], self.bp[pi])

        class Stage:
            def __init__(self, st, name, shape, dt, n=3):
                self.t = [K.sb(st, "%s%d" % (name, i), shape, dt) for i in range(n)]
                self.b = [Buf() for _ in range(n)]
                self.i = 0

            def next(self):
                k = self.i % len(self.t)
                self.i += 1
                return self.t[k], self.b[k]

        def store(dst, src_ap, bsrc, bdst, eng="act"):
            P.dma(eng, dst, src_ap, reads=[bsrc], writes=[bdst])

        def out_proj_phase(W, residual_src, bres_src, KC=8, xsrc=None, bxsrc=None):
            H = Hs[hsel[0]]; bH = bHs[hsel[0]]
            hsel[0] ^= 1
            P.begin()
            with ExitStack() as st:
                xin = K.sb(st, "op_x", [128, 8, T], BF16); bx = Buf()
                for c in range(8):
                    P.dma("sp", xin[:, c, :], MIXO[c], reads=[B("MIXO")], writes=[bx])
                lin = Lin(st, 8, 1, "op")
                hst = Stage(st, "op_h", [128, 512], F32, 3)
                ost = Stage(st, "op_o", [128, 512], F32, 3)
                for fc in range(8):
                    def epi(tb, sl, pss, bps, fc=fc):
                        ht, bht = hst.next()
                        P.dma("sp", ht[:], residual_src[fc, :, sl], reads=[bres_src], writes=[bht])
                        ot, bot = ost.next()
                        P.op("dve", lambda e, ot=ot, ht=ht, p_=pss[0]: e.tensor_tensor(out=ot[:], in0=p_[:], in1=ht[:], op=ALU.add),
                             [bps[0], bht], [bot])
                        store(H[fc, :, sl], ot[:], bot, bH)
                    lin.run(xin, bx, T, [(W, fc * 128)], epi)
            return H, bH


        def xattn_phases(layer, cur_src, cur_bsrc):
            P.begin()
            with ExitStack() as st:
                hn = K.sb(st, "x_hn", [128, 8, T], BF16); bhn = [Buf() for _ in range(8)]
                emit_norm(st, cur_src, cur_bsrc, "g_xattn%d" % layer, hn, bhn)
                with ExitStack() as st2:
                    lin = Lin(st2, 8, 1, "x1")
                    obs = Stage(st2, "x1_ob", [128, 512], BF16, 3)
                    for j in range(8):
                        def epi(tb, sl, pss, bps, j=j):
                            ob, bob = obs.next()
                            P.op("act", lambda e: e.activation(out=ob[:], in_=pss[0][:], func=AF.Copy), [bps[0]], [bob])
                            store(XQ[j, :, sl], ob[:], bob, B("XQ"))
                        lin.run(hn, bhn, T, [(w_xq[layer], j * 128)], epi)
            P.end(); P.begin()
            NM = NS * NMEM
            with ExitStack() as st:
                mn = K.sb(st, "x_mn", [128, 8, NM], BF16); bmn = Buf()
                bmem = Buf()
                emit_norm(st, memT, bmem, "g_mem%d" % layer, mn, bmn)
                lin = Lin(st, 8, 1, "x1k")
                obs = Stage(st, "x1k_ob", [128, 512], BF16, 3)
                for j in range(8):
                    def epi(tb, sl, pss, bps, j=j):
                        ob, bob = obs.next()
                        P.op("act", lambda e: e.activation(out=ob[:], in_=pss[0][:], func=AF.Copy), [bps[0]], [bob])
                        store(XK[j, :, sl], ob[:], bob, B("XK"))
                    lin.run(mn, bmn, NM, [(w_xkv[layer], j * 128)], epi)
                wr = K.sb(st, "x1_wr", [128, 8, 512], BF16); bwr = Buf()
                pst = [K.ps(st, "x1_tp%d" % i) for i in range(2)]; bpt = [Buf(), Buf()]
                it = [0]
                for half in range(2):
                    P.dma("pool", wr[:], w_xkv[layer].rearrange("(kc p) f -> p kc f", p=128)[:, :, D + half * 512: D + (half + 1) * 512], writes=[bwr])
                    for blk in range(NM // 128):
                        def body(half=half, blk=blk):
                            tsl = slice(blk * 128, (blk + 1) * 128)
                            ps_ = pst[it[0] % 2]; bp_ = bpt[it[0] % 2]; it[0] += 1
                            for kc in range(8):
                                P.op("pe", lambda e, kc=kc: e.matmul(ps_[:], lhsT=mn[:, kc, tsl], rhs=wr[:, kc, :], start=(kc == 0), stop=(kc == 7)),
                                     [bmn, bwr], [bp_])
                            ob, bob = obs.next()
                            P.op("act", lambda e: e.activation(out=ob[:], in_=ps_[:], func=AF.Copy), [bp_], [bob])
                            store(XV[tsl, half * 512:(half + 1) * 512], ob[:], bob, B("XV"))
                        body()
            if phase_done("X1_%d" % layer):
                return None
            P.begin()
            with ExitStack() as st:
                xk = K.sb(st, "x2_k", [128, 8, NM], BF16); bxk = Buf()
                xv = K.sb(st, "x2_v", [128, NM // 128, D], BF16); bxv = Buf()
                P.dma("sp", xk[:], XK.rearrange("c p t -> p c t"), reads=[B("XK")], writes=[bxk])
                P.dma("sp", xv[:], XV.rearrange("(b p) f -> p b f", p=128), reads=[B("XV")], writes=[bxv])
                xq = [K.sb(st, "x2_q%d" % i, [128, 2, S], BF16) for i in range(2)]; bxq = [Buf(), Buf()]
                pss = [K.ps(st, "x2_s%d" % i) for i in range(2)]; bpss = [Buf(), Buf()]
                pnum = [[K.ps(st, "x2_n%d_%d" % (i, j)) for j in range(2)] for i in range(2)]
                bpn = [[Buf(), Buf()], [Buf(), Buf()]]
                pden = [K.ps(st, "x2_d%d" % i) for i in range(2)]; bpd = [Buf(), Buf()]
                exs = Stage(st, "x2_ex", [128, 512], BF16, 3)
                rec = K.sb(st, "x2_rec", [128, 512], F32); brec = Buf()
                obs = Stage(st, "x2_ob", [128, 512], BF16, 4)
                cnt = {"q": 0, "s": 0, "a": 0}
                pend_x = []

                def flush_x():
                    while pend_x:
                        d_, o_, bo_ = pend_x.pop(0)
                        store(d_, o_[:], bo_, B("MIXO"))
                for s in range(NS):
                    for h in range(4):
                        def head(s=s, h=h):
                            b_ = cnt["q"] % 2; cnt["q"] += 1
                            for dc in range(2):
                                P.dma("sp", xq[b_][:, dc, :], XQ[2 * h + dc, :, s * S:(s + 1) * S], reads=[B("XQ")], writes=[bxq[b_]])
                            units = [(qb, mb) for qb in range(4) for mb in range(2)]
                            acc_of = {}
                            stx = {}

                            def xA(u):
                                qb, mb = units[u]
                                if mb == 0:
                                    acc_of[qb] = cnt["a"] % 2; cnt["a"] += 1
                                p_ = pss[cnt["s"] % 2]; bp_ = bpss[cnt["s"] % 2]; cnt["s"] += 1
                                qsl = slice(qb * 512, (qb + 1) * 512)
                                msl = slice(s * NMEM + mb * 128, s * NMEM + (mb + 1) * 128)
                                for dc in range(2):
                                    P.op("pe", lambda e, dc=dc: e.matmul(p_[:], lhsT=xk[:, 2 * h + dc, msl], rhs=xq[b_][:, dc, qsl],
                                                                         start=(dc == 0), stop=(dc == 1)), [bxk, bxq[b_]], [bp_])
                                stx[u] = (p_, bp_)

                            def xB(u):
                                p_, bp_ = stx[u]
                                ex, bex = exs.next()
                                P.op("act", lambda e: e.activation(out=ex[:], in_=p_[:], func=AF.Exp, scale=1.0 / 16.0), [bp_], [bex])
                                stx[u] = (ex, bex)

                            def xC(u):
                                qb, mb = units[u]
                                a_ = acc_of[qb]
                                ex, bex = stx[u]
                                for dvc in range(2):
                                    P.op("pe", lambda e, dvc=dvc: e.matmul(pnum[a_][dvc][:], lhsT=xv[:, s * 2 + mb, h * 256 + dvc * 128: h * 256 + (dvc + 1) * 128],
                                                                           rhs=ex[:], start=(mb == 0), stop=(mb == 1)), [bxv, bex], [bpn[a_][dvc]])
                                P.op("pe", lambda e: e.matmul(pden[a_][:], lhsT=ones_bf[:], rhs=ex[:], start=(mb == 0), stop=(mb == 1)),
                                     [bconst, bex], [bpd[a_]])
                                if mb == 1:
                                    P.op("dve", lambda e: e.reciprocal(out=rec[:], in_=pden[a_][:]), [bpd[a_]], [brec])
                                    for dvc in range(2):
                                        ob, bob = obs.next()
                                        P.op("dve", lambda e, dvc=dvc, ob=ob: e.tensor_tensor(out=ob[:], in0=pnum[a_][dvc][:], in1=rec[:], op=ALU.mult),
                                             [bpn[a_][dvc], brec], [bob])
                                        pend_x.append((MIXO[2 * h + dvc, :, s * S + qb * 512: s * S + (qb + 1) * 512], ob, bob))

                            xA(0)
                            for u in range(len(units)):
                                if u + 1 < len(units):
                                    xA(u + 1)
                                xB(u)
                                if u % 2 == 1:
                                    flush_x()
                                xC(u)
                        head()
                flush_x()
            if phase_done("X2_%d" % layer):
                return None
            r_ = out_proj_phase(w_xo[layer], cur_src, cur_bsrc)
            if phase_done("X3_%d" % layer):
                return None
            return r_

        def ffn_phases(layer, cur_src, cur_bsrc):
            P.begin()
            with ExitStack() as st:
                hn = K.sb(st, "f_hn", [128, 8, T], BF16); bhn = Buf()
                with ExitStack() as st2:
                    emit_norm(st2, cur_src, cur_bsrc, "g_ffn%d" % layer, hn, bhn)
                P.end(); P.begin()
                with ExitStack() as st2:
                    lin = Lin(st2, 8, 2, "f1")
                    uV = K.sb(st2, "f_uV", [128, 2, S + 2], F32); buV = [Buf(), Buf()]
                    uG = K.sb(st2, "f_uG", [128, 2, S + 2], F32); buG = [Buf(), Buf()]
                    for t_, bb_ in ((uV, buV), (uG, buG)):
                        for q_ in range(2):
                            P.op("pool", lambda e, t_=t_, q_=q_: e.memset(t_[:, q_, 0:1], 0.0), [], [bb_[q_]])
                            P.op("pool", lambda e, t_=t_, q_=q_: e.memset(t_[:, q_, S + 1:S + 2], 0.0), [], [bb_[q_]])
                    aVs = Stage(st2, "f_aV", [128, S], F32, 2)
                    aGs = Stage(st2, "f_aG", [128, S], F32, 2)
                    sgs_ = Stage(st2, "f_sg", [128, S], F32, 2)
                    hids = Stage(st2, "f_hid", [128, S], BF16, 2)
                    pending = []
                    for c in range(22):
                        def epi(tb, sl, pss, bps, c=c):
                            sq_ = tb // 4
                            o_ = 1 + (tb % 4) * 512
                            P.op("act", lambda e: e.activation(out=uV[:, sq_, o_:o_ + 512], in_=pss[0][:], func=AF.Copy), [bps[0]], [buV[sq_]])
                            P.op("act", lambda e: e.activation(out=uG[:, sq_, o_:o_ + 512], in_=pss[1][:], func=AF.Copy), [bps[1]], [buG[sq_]])
                            if tb % 4 != 3:
                                return
                            while pending:
                                pending.pop(0)()
                            aV, baV = aVs.next(); aG, baG = aGs.next(); sg, bsg = sgs_.next()
                            for (u_, bu_, a_, ba_, ci) in ((uV, buV[sq_], aV, baV, c), (uG, buG[sq_], aG, baG, 22 + c)):
                                w0 = K.pcol("w_conv_ffn%d_0" % layer, ci); w1 = K.pcol("w_conv_ffn%d_1" % layer, ci)
                                w2 = K.pcol("w_conv_ffn%d_2" % layer, ci); bc = K.pcol("b_conv_ffn%d" % layer, ci)
                                P.op("act", lambda e, u_=u_, a_=a_, w0=w0, bc=bc: e.activation(out=a_[:], in_=u_[:, sq_, 0:S], func=AF.Identity, bias=bc, scale=w0),
                                     [bu_, bpp], [ba_])
                                P.op("dve", lambda e, u_=u_, a_=a_, w1=w1: e.scalar_tensor_tensor(out=a_[:], in0=u_[:, sq_, 1:S + 1], scalar=w1, in1=a_[:],
                                                                                                   op0=ALU.mult, op1=ALU.add), [bu_, bpp, ba_], [ba_])
                                P.op("dve", lambda e, u_=u_, a_=a_, w2=w2: e.scalar_tensor_tensor(out=a_[:], in0=u_[:, sq_, 2:S + 2], scalar=w2, in1=a_[:],
                                                                                                   op0=ALU.mult, op1=ALU.add), [bu_, bpp, ba_], [ba_])
                            def tail(aV=aV, baV=baV, aG=aG, baG=baG, sg=sg, bsg=bsg, c=c, sq_=sq_):
                                P.op("act", lambda e: e.activation(out=sg[:], in_=aG[:], func=AF.Silu), [baG], [bsg])
                                hd, bhd = hids.next()
                                P.op("dve", lambda e: e.tensor_tensor(out=hd[:], in0=aV[:], in1=sg[:], op=ALU.mult), [baV, bsg], [bhd])
                                store(HID[c, :, sq_ * S:(sq_ + 1) * S], hd[:], bhd, B("HID"))
                            pending.append(tail)
                        lin.run(hn, bhn, T, [(w_up[layer], c * 128), (w_up[layer], DFF + c * 128)], epi)
                    while pending:
                        pending.pop(0)()
            if phase_done("F1_%d" % layer):
                return None
            Hn = Hs[hsel[0]]; bHn = bHs[hsel[0]]
            hsel[0] ^= 1
            P.begin()
            with ExitStack() as st:
                xin = K.sb(st, "f2_x", [128, 22, S], BF16); bx = Buf()
                lin = Lin(st, 22, 1, "f2")
                hst = Stage(st, "f2_h", [128, 512], F32, 3)
                ost = Stage(st, "f2_o", [128, 512], F32, 3)
                for s in range(NS):
                    P.dma("sp", xin[:], HID[:, :, s * S:(s + 1) * S].rearrange("c p t -> p c t"), reads=[B("HID")], writes=[bx])
                    for fc in range(8):
                        def epi(tb, sl, pss, bps, fc=fc, s=s):
                            gsl = slice(s * S + tb * 512, s * S + (tb + 1) * 512)
                            ht, bht = hst.next()
                            P.dma("sp", ht[:], cur_src[fc, :, gsl], reads=[cur_bsrc], writes=[bht])
                            ot, bot = ost.next()
                            P.op("dve", lambda e: e.tensor_tensor(out=ot[:], in0=pss[0][:], in1=ht[:], op=ALU.add), [bps[0], bht], [bot])
                            store(Hn[fc, :, gsl], ot[:], bot, bHn)
                        lin.run(xin, bx, S, [(w_down[layer], fc * 128)], epi)
            if phase_done("F2_%d" % layer):
                return None
            return Hn, bHn


        C1_ = 6.28125
        C2_ = TWO_PI - C1_
        PI_S_ = 3.1415925

        def sincos(x, bx, ki, bki, kf, bkf, sn, bsn, cs, bcs, sl):
            P.op("dve", lambda e: e.tensor_scalar(out=ki[sl], in0=x[sl], scalar1=1.0 / TWO_PI, scalar2=None, op0=ALU.mult), [bx], [bki])
            P.op("dve", lambda e: e.tensor_copy(out=kf[sl], in_=ki[sl]), [bki], [bkf])
            P.op("dve", lambda e: e.scalar_tensor_tensor(out=x[sl], in0=kf[sl], scalar=-C1_, in1=x[sl], op0=ALU.mult, op1=ALU.add), [bkf, bx], [bx])
            P.op("dve", lambda e: e.scalar_tensor_tensor(out=x[sl], in0=kf[sl], scalar=-C2_, in1=x[sl], op0=ALU.mult, op1=ALU.add), [bkf, bx], [bx])
            P.op("dve", lambda e: e.tensor_scalar(out=x[sl], in0=x[sl], scalar1=-PI_S_, scalar2=PI_S_, op0=ALU.max, op1=ALU.min), [bx], [bx])
            P.op("act", lambda e: e.activation(out=sn[sl], in_=x[sl], func=AF.Sin), [bx], [bsn])
            P.op("dve", lambda e: e.tensor_single_scalar(out=kf[sl], in_=x[sl], scalar=math.pi / 2.0, op=ALU.is_gt), [bx], [bkf])
            P.op("dve", lambda e: e.scalar_tensor_tensor(out=x[sl], in0=kf[sl], scalar=-TWO_PI, in1=x[sl], op0=ALU.mult, op1=ALU.add), [bkf, bx], [bx])
            P.op("dve", lambda e: e.tensor_scalar(out=x[sl], in0=x[sl], scalar1=math.pi / 2.0, scalar2=PI_S_, op0=ALU.add, op1=ALU.min), [bx], [bx])
            P.op("act", lambda e: e.activation(out=cs[sl], in_=x[sl], func=AF.Sin), [bx], [bcs])

        def odd_phases(layer, li, cur_src, cur_bsrc):
            Wcd = w_in_cd[li]
            P.begin()
            with ExitStack() as st:
              if "S5_NOO1" not in K.dbg:
                  hn = K.sb(st, "o_hn", [128, 8, T], BF16); bhn = [Buf() for _ in range(8)]
                  emit_norm(st, cur_src, cur_bsrc, "g_mix%d" % layer, hn, bhn)
                  with ExitStack() as st2:
                      lin1 = Lin(st2, 8, 1, "o1a")
                      ofs = Stage(st2, "o1_of", [128, 512], F32, 3)
                      obs = Stage(st2, "o1_ob", [128, 512], BF16, 3)
                      for j in range(4):
                          def epi(tb, sl, pss, bps, j=j):
                              ot, bot = ofs.next()
                              P.op("act", lambda e: e.activation(out=ot[:], in_=pss[0][:], func=AF.Copy), [bps[0]], [bot])
                              store(UC[j, :, sl], ot[:], bot, B("UC"), eng="sp")
                              ob, bob = obs.next()
                              P.op("pool", lambda e: e.tensor_copy(out=ob[:], in_=ot[:]), [bot], [bob])
                              store(UCB[j, :, sl], ob[:], bob, B("UCB"), eng="sp")
                          if "S5_NOO1A" not in K.dbg:
                              lin1.run(hn, bhn, T, [(Wcd, j * 128)], epi)
                  P.end(); P.begin()
                  with ExitStack() as st2:
                      lin2 = Lin(st2, 8, 2, "o1b")
                      sgs = Stage(st2, "o1_sg", [128, 512], F32, 2)
                      ofs = Stage(st2, "o1_of2", [128, 512], F32, 3)
                      for j in range(4):
                          def epi(tb, sl, pss, bps, j=j):
                              sg, bsg = sgs.next()
                              P.op("act", lambda e: e.activation(out=sg[:], in_=pss[1][:], func=AF.Sigmoid), [bps[1]], [bsg])
                              ot, bot = ofs.next()
                              P.op("dve", lambda e: e.tensor_tensor(out=ot[:], in0=pss[0][:], in1=sg[:], op=ALU.mult), [bps[0], bsg], [bot])
                              store(DV[j, :, sl], ot[:], bot, B("DV"), eng="sp")
                          if "S5_NOO1B" not in K.dbg:
                              lin2.run(hn, bhn, T, [(Wcd, 512 + j * 128), (Wcd, 1024 + j * 128)], epi)
            if phase_done("O1_%d" % layer):
                return None
            P.begin()
            with ExitStack() as st:
                PB = K.sb(st, "s5_PB", [128, 32, 2, 128], BF16); bPB = Buf()
                PC = K.sb(st, "s5_PC", [128, 32, 2, 128], BF16); bPC = Buf()
                rl = K.sb(st, "s5_rl", [128, 32], F32); brl = Buf()
                thl = K.sb(st, "s5_thl", [128, 32], F32); bthl = Buf()
                zb = K.sb(st, "s5_zb", [128, 4, T], BF16); bzb = Buf()
                with ExitStack() as st2:
                    NH = 16 * 128
                    tA = K.sb(st2, "s5_A", [128, NH], F32); bA = Buf()
                    tB = K.sb(st2, "s5_B", [128, NH], F32); bB = Buf()
                    tC = K.sb(st2, "s5_C", [128, NH], F32); bC = Buf()
                    tD = K.sb(st2, "s5_D", [128, NH], F32); bD = Buf()
                    tE = K.sb(st2, "s5_E", [128, NH], F32); bE = Buf()
                    tF = K.sb(st2, "s5_F", [128, NH], F32); bF = Buf()
                    tG = K.sb(st2, "s5_G", [128, NH], F32); bG = Buf()
                    tH = K.sb(st2, "s5_H", [128, NH], F32); bHh = Buf()
                    tI = K.sb(st2, "s5_I", [128, NH], I32); bI = Buf()
                    al = (slice(None), slice(None))
                    for half in range(2):
                        def prep(half=half):
                            q0 = half * 16
                            rows = s5rows[li]
                            def brow(k):
                                return rows[k, q0:q0 + 16, :].rearrange("q l -> (q l)").unsqueeze(0).partition_broadcast(128)
                            P.dma("sp", tA[:], brow(0), writes=[bA])
                            P.dma("sp", tB[:], brow(1), writes=[bB])
                            P.dma("sp", tC[:], brow(2), writes=[bC])
                            P.op("act", lambda e: e.activation(out=tC[:], in_=tC[:], func=AF.Exp), [bC], [bC])
                            P.op("dve", lambda e: e.tensor_tensor(out=tD[:], in0=tA[:], in1=tC[:], op=ALU.mult), [bA, bC], [bD])
                            P.op("dve", lambda e: e.tensor_tensor(out=tE[:], in0=tB[:], in1=tC[:], op=ALU.mult), [bB, bC], [bE])
                            P.op("act", lambda e: e.activation(out=tD[:], in_=tD[:], func=AF.Exp), [bD], [bD])
                            sincos(tE, bE, tI, bI, tH, bHh, tF, bF, tG, bG, al)
                            P.op("dve", lambda e: e.tensor_tensor(out=tG[:], in0=tD[:], in1=tG[:], op=ALU.mult), [bD, bG], [bG])
                            P.op("dve", lambda e: e.tensor_tensor(out=tF[:], in0=tD[:], in1=tF[:], op=ALU.mult), [bD, bF], [bF])
                            P.op("dve", lambda e: e.tensor_tensor(out=tD[:], in0=tA[:], in1=tA[:], op=ALU.mult), [bA], [bD])
                            P.op("dve", lambda e: e.tensor_tensor(out=tE[:], in0=tB[:], in1=tB[:], op=ALU.mult), [bB], [bE])
                            P.op("dve", lambda e: e.tensor_tensor(out=tD[:], in0=tD[:], in1=tE[:], op=ALU.add), [bD, bE], [bD])
                            P.op("dve", lambda e: e.reciprocal(out=tD[:], in_=tD[:]), [bD], [bD])
                            P.op("dve", lambda e: e.tensor_scalar(out=tG[:], in0=tG[:], scalar1=-1.0, scalar2=None, op0=ALU.add), [bG], [bG])
                            P.op("dve", lambda e: e.tensor_tensor(out=tE[:], in0=tG[:], in1=tA[:], op=ALU.mult), [bG, bA], [bE])
                            P.op("dve", lambda e: e.tensor_tensor(out=tH[:], in0=tF[:], in1=tB[:], op=ALU.mult), [bF, bB], [bHh])
                            P.op("dve", lambda e: e.tensor_tensor(out=tE[:], in0=tE[:], in1=tH[:], op=ALU.add), [bE, bHh], [bE])
                            P.op("dve", lambda e: e.tensor_tensor(out=tE[:], in0=tE[:], in1=tD[:], op=ALU.mult), [bE, bD], [bE])
                            P.op("dve", lambda e: e.tensor_tensor(out=tH[:], in0=tF[:], in1=tA[:], op=ALU.mult), [bF, bA], [bHh])
                            P.op("dve", lambda e: e.tensor_tensor(out=tG[:], in0=tG[:], in1=tB[:], op=ALU.mult), [bG, bB], [bG])
                            P.op("dve", lambda e: e.tensor_tensor(out=tH[:], in0=tH[:], in1=tG[:], op=ALU.subtract), [bHh, bG], [bHh])
                            P.op("dve", lambda e: e.tensor_tensor(out=tH[:], in0=tH[:], in1=tD[:], op=ALU.mult), [bHh, bD], [bHh])
                            P.dma("sp", tA[:], s5B[li, 0, half], reads=[], writes=[bA])
                            P.dma("sp", tB[:], s5B[li, 1, half], reads=[], writes=[bB])
                            P.op("dve", lambda e: e.tensor_tensor(out=tG[:], in0=tE[:], in1=tA[:], op=ALU.mult), [bE, bA], [bG])
                            P.op("dve", lambda e: e.tensor_tensor(out=tF[:], in0=tHh_[:], in1=tB[:], op=ALU.mult), [bHh, bB], [bF])
                            P.op("dve", lambda e: e.tensor_tensor(out=PB[:, q0:q0 + 16, 0, :], in0=tG[:].rearrange("p (q l) -> p q l", q=16),
                                                                  in1=tF[:].rearrange("p (q l) -> p q l", q=16), op=ALU.subtract), [bG, bF], [bPB])
                            P.op("dve", lambda e: e.tensor_tensor(out=tG[:], in0=tE[:], in1=tB[:], op=ALU.mult), [bE, bB], [bG])
                            P.op("dve", lambda e: e.tensor_tensor(out=tF[:], in0=tHh_[:], in1=tA[:], op=ALU.mult), [bHh, bA], [bF])
                            P.op("dve", lambda e: e.tensor_tensor(out=PB[:, q0:q0 + 16, 1, :], in0=tG[:].rearrange("p (q l) -> p q l", q=16),
                                                                  in1=tF[:].rearrange("p (q l) -> p q l", q=16), op=ALU.add), [bG, bF], [bPB])
                            P.dma("sp", tC[:], s5C[li, 0, half], reads=[], writes=[bC])
                            P.dma("sp", tD[:], s5C[li, 1, half], reads=[], writes=[bD])
                            P.op("act", lambda e: e.activation(out=PC[:, q0:q0 + 16, 0, :], in_=tC[:].rearrange("p (q l) -> p q l", q=16), func=AF.Copy), [bC], [bPC])
                            P.op("act", lambda e: e.activation(out=PC[:, q0:q0 + 16, 1, :], in_=tD[:].rearrange("p (q l) -> p q l", q=16), func=AF.Copy, scale=-1.0), [bD], [bPC])
                        tHh_ = tH
                        if "S5_NOROW" not in K.dbg:
                            prep()
                    ln_ = K.sb(st2, "s5_ln", [128, 96], F32); bln = Buf()
                    lki = K.sb(st2, "s5_lki", [128, 32], I32); blki = Buf()
                    lkf = K.sb(st2, "s5_lkf", [128, 32], F32); blkf = Buf()
                    if "S5_NOLANE" in K.dbg:
                        P.op("dve", lambda e: e.memset(ln_[:], 0.0), [], [bln])
                    else:
                        P.dma("sp", ln_[:], s5lane[li], writes=[bln])
                    P.op("act", lambda e: e.activation(out=ln_[:, 64:96], in_=ln_[:, 64:96], func=AF.Exp), [bln], [bln])
                    P.op("dve", lambda e: e.tensor_tensor(out=rl[:], in0=ln_[:, 0:32], in1=ln_[:, 64:96], op=ALU.mult), [bln], [brl])
                    P.op("act", lambda e: e.activation(out=rl[:], in_=rl[:], func=AF.Exp), [brl], [brl])
                    P.op("dve", lambda e: e.tensor_tensor(out=thl[:], in0=ln_[:, 32:64], in1=ln_[:, 64:96], op=ALU.mult), [bln], [bthl])
                    P.op("dve", lambda e: e.tensor_scalar(out=lki[:], in0=thl[:], scalar1=1.0 / TWO_PI, scalar2=None, op0=ALU.mult), [bthl], [blki])
                    P.op("dve", lambda e: e.tensor_copy(out=lkf[:], in_=lki[:]), [blki], [blkf])
                    P.op("dve", lambda e: e.scalar_tensor_tensor(out=thl[:], in0=lkf[:], scalar=-C1_, in1=thl[:], op0=ALU.mult, op1=ALU.add), [blkf, bthl], [bthl])
                    P.op("dve", lambda e: e.scalar_tensor_tensor(out=thl[:], in0=lkf[:], scalar=-C2_, in1=thl[:], op0=ALU.mult, op1=ALU.add), [blkf, bthl], [bthl])
                    if "DBGS5P" in K.dbg:
                        dpb = K.scratch("DBGPB", [128, 32 * 2 * 128], BF16)
                        P.dma("act", dpb, PB[:].rearrange("p a b c -> p (a b c)"), reads=[bPB], writes=[B("DBGPB")])
                        dpl = K.scratch("DBGRL", [128, 64], F32)
                        P.dma("act", dpl[:, 0:32], rl[:], reads=[brl], writes=[B("DBGRL")])
                        P.dma("act", dpl[:, 32:64], thl[:], reads=[bthl], writes=[B("DBGRL")])
                if phase_done("O2a_%d" % layer):
                    return None
                P.begin()
                with ExitStack() as st2:
                    idx = K.sb(st2, "s5_idx", [128, 96], F32); bidx = Buf()
                    if "S5_NOIDX" not in K.dbg:
                        P.dma("sp", idx[:], s5idx.partition_broadcast(128), writes=[bidx])
                    sm = K.sb(st2, "s5_sm", [128, 96], F32); bsm = Buf()
                    smi = K.sb(st2, "s5_smi", [128, 96], I32); bsmi = Buf()
                    smf = K.sb(st2, "s5_smf", [128, 96], F32); bsmf = Buf()
                    ssn = K.sb(st2, "s5_ssn", [128, 96], F32); bssn = Buf()
                    scs = K.sb(st2, "s5_scs", [128, 96], F32); bscs = Buf()
                    c1x = K.sb(st2, "s5_c1x", [128, S], BF16); bc1x = Buf()
                    s1x = K.sb(st2, "s5_s1x", [128, S], BF16); bs1x = Buf()
                    c2b = K.sb(st2, "s5_c2b", [128, 64], BF16); bc2b = Buf()
                    s2b = K.sb(st2, "s5_s2b", [128, 64], BF16); bs2b = Buf()
                    tt1 = K.sb(st2, "s5_t1", [128, S], F32); bt1 = Buf()
                    tt2 = K.sb(st2, "s5_t2", [128, S], F32); bt2 = Buf()
                    bubs = [K.sb(st2, "s5_bub%d" % i, [128, 2, S], BF16) for i in range(2)]; bbubs = [Buf(), Buf()]
                    r1 = K.sb(st2, "s5_r1", [128, S], BF16); br1 = Buf()
                    r2 = K.sb(st2, "s5_r2", [128, S], BF16); br2 = Buf()
                    r3 = K.sb(st2, "s5_r3", [128, S], BF16); br3 = Buf()
                    r4 = K.sb(st2, "s5_r4", [128, S], BF16); br4 = Buf()
                    shrb = K.sb(st2, "s5_shrb", [128, S], BF16); bshrb = Buf()
                    shib = K.sb(st2, "s5_shib", [128, S], BF16); bshib = Buf()
                    Cb = K.sb(st2, "s5_Cb", [128, S], BF16); bCb = Buf()
                    Sb = K.sb(st2, "s5_Sb", [128, S], BF16); bSb = Buf()
                    u1 = K.sb(st2, "s5_u1", [128, S], BF16); bu1 = Buf()
                    u2 = K.sb(st2, "s5_u2", [128, S], BF16); bu2 = Buf()
                    u3 = K.sb(st2, "s5_u3", [128, S], BF16); bu3 = Buf()
                    u4 = K.sb(st2, "s5_u4", [128, S], BF16); bu4 = Buf()
                    srb = K.sb(st2, "s5_srb", [128, S], BF16); bsrb = Buf()
                    sib = K.sb(st2, "s5_sib", [128, S], BF16); bsib = Buf()
                    ub = K.sb(st2, "s5_ub", [128, T], BF16); bub = Buf()
                    yacc = K.sb(st2, "s5_yacc", [128, T], F32); byacc = Buf()
                    pbr = [K.ps(st2, "s5_pbr%d" % i) for i in range(2)]; bpbr = [Buf(), Buf()]
                    pbi = [K.ps(st2, "s5_pbi%d" % i) for i in range(2)]; bpbi = [Buf(), Buf()]
                    py = [K.ps(st2, "s5_py%d" % i) for i in range(2)]; bpy = [Buf(), Buf()]
                    cn = {"b": 0, "y": 0, "u": 0}
                    if "S5_DUMMY" in K.dbg:
                        P.op("act", lambda e: e.activation(out=ssn[:], in_=idx[:], func=AF.Copy), [bidx], [bssn])
                        P.op("pool", lambda e: e.memset(scs[:], 0.0), [], [bscs])
                    lvl = 9
                    for f_ in K.dbg:
                        if f_.startswith("S5LVL"):
                            lvl = int(f_[5:].replace("m", "-"))
                    for ct in range(4):
                        def chtile(ct=ct):
                            if "S5_NOUB" not in K.dbg:
                                P.dma("sp", ub[:], UCB[ct], reads=[B("UCB")], writes=[bub])
                            for s0 in range(NS):
                                if "S5_NODSKIP" in K.dbg:
                                    continue
                                def dskip(s0=s0):
                                    tk0 = slice(s0 * S, (s0 + 1) * S)
                                    P.dma("sp", tt1[:], UC[ct, :, tk0], reads=[B("UC")], writes=[bt1])
                                    P.op("dve", lambda e: e.tensor_scalar(out=yacc[:, tk0], in0=tt1[:], scalar1=K.pcol("s5_d%d" % li, ct), scalar2=None, op0=ALU.mult),
                                         [bt1, bpp], [byacc])
                                dskip()
                            for d_ in range(2):
                                for q4 in range(4):
                                    def lanetile(d_=d_, q4=q4):
                                        dq = d_ * 16 + ct * 4 + q4
                                        if lvl < 0:
                                            return
                                        thc = thl[:, dq:dq + 1]
                                        P.op("dve", lambda e: e.tensor_scalar(out=sm[:], in0=idx[:], scalar1=thc, scalar2=None, op0=ALU.mult), [bidx, bthl], [bsm])
                                        sincos(sm, bsm, smi, bsmi, smf, bsmf, ssn, bssn, scs, bscs, (slice(None), slice(None)))
                                        v3 = lambda t_: t_[:].rearrange("p (a b) -> p a b", a=32)
                                        P.op("act", lambda e: e.activation(out=v3(c1x), in_=scs[:, 0:32].unsqueeze(2).to_broadcast([128, 32, 64]), func=AF.Copy), [bscs], [bc1x])
                                        P.op("act", lambda e: e.activation(out=v3(s1x), in_=ssn[:, 0:32].unsqueeze(2).to_broadcast([128, 32, 64]), func=AF.Copy), [bssn], [bs1x])
                                        P.op("act", lambda e: e.activation(out=c2b[:], in_=scs[:, 32:96], func=AF.Copy), [bscs], [bc2b])
                                        P.op("act", lambda e: e.activation(out=s2b[:], in_=ssn[:, 32:96], func=AF.Copy), [bssn], [bs2b])
                                        c2 = c2b[:].unsqueeze(1).to_broadcast([128, 32, 64]); s2 = s2b[:].unsqueeze(1).to_broadcast([128, 32, 64])
                                        P.op("dve", lambda e: e.tensor_tensor(out=v3(u1), in0=v3(c1x), in1=c2, op=ALU.mult), [bc1x, bc2b], [bu1])
                                        P.op("dve", lambda e: e.tensor_tensor(out=v3(u2), in0=v3(s1x), in1=s2, op=ALU.mult), [bs1x, bs2b], [bu2])
                                        P.op("dve", lambda e: e.tensor_tensor(out=v3(u3), in0=v3(s1x), in1=c2, op=ALU.mult), [bs1x, bc2b], [bu3])
                                        P.op("dve", lambda e: e.tensor_tensor(out=v3(u4), in0=v3(c1x), in1=s2, op=ALU.mult), [bc1x, bs2b], [bu4])
                                        P.op("dve", lambda e: e.tensor_tensor(out=Cb[:], in0=u1[:], in1=u2[:], op=ALU.subtract), [bu1, bu2], [bCb])
                                        P.op("dve", lambda e: e.tensor_tensor(out=Sb[:], in0=u3[:], in1=u4[:], op=ALU.add), [bu3, bu4], [bSb])
                                        rbc = rl[:, dq:dq + 1].to_broadcast([128, S])
                                        if lvl < 1:
                                            return
                                        for s in range(NS):
                                            def seq(s=s):
                                                rev = (d_ == 1)
                                                bub16 = bubs[cn["u"] % 2]; bbub = bbubs[cn["u"] % 2]; cn["u"] += 1

                                                def tv(tile_, a, b):
                                                    if not rev:
                                                        return tile_[:, a:b]
                                                    lo = S - 1 - a
                                                    hi = S - 1 - b
                                                    return tile_[:, lo:hi:-1] if hi >= 0 else tile_[:, lo::-1]
                                                rv = (lambda ap: ap) if not rev else (lambda ap: ap[:, ::-1])
                                                for tb in range(4):
                                                    k_ = cn["b"] % 2; cn["b"] += 1
                                                    tsl = slice(s * S + tb * 512, s * S + (tb + 1) * 512)
                                                    a0 = tb * 512; b0 = (tb + 1) * 512
                                                    lsl = slice(a0, b0)
                                                    P.op("pe", lambda e, k_=k_, tsl=tsl: e.matmul(pbr[k_][:], lhsT=PB[:, dq, 0, :], rhs=ub[:, tsl], start=True, stop=True), [bPB, bub], [bpbr[k_]])
                                                    P.op("pe", lambda e, k_=k_, tsl=tsl: e.matmul(pbi[k_][:], lhsT=PB[:, dq, 1, :], rhs=ub[:, tsl], start=True, stop=True), [bPB, bub], [bpbi[k_]])
                                                    P.op("act", lambda e, k_=k_, lsl=lsl: e.activation(out=bub16[:, 0, lsl], in_=pbr[k_][:], func=AF.Copy), [bpbr[k_]], [bbub])
                                                    P.op("act", lambda e, k_=k_, lsl=lsl: e.activation(out=bub16[:, 1, lsl], in_=pbi[k_][:], func=AF.Copy), [bpbi[k_]], [bbub])
                                                Cbv0 = rv(Cb[:]); Sbv0 = rv(Sb[:])
                                                P.op("dve", lambda e: e.tensor_tensor(out=r1[:], in0=bub16[:, 0, :], in1=Cbv0, op=ALU.mult), [bbub, bCb], [br1])
                                                P.op("dve", lambda e: e.tensor_tensor(out=r2[:], in0=bub16[:, 1, :], in1=Sbv0, op=ALU.mult), [bbub, bSb], [br2])
                                                P.op("dve", lambda e: e.tensor_tensor(out=r3[:], in0=bub16[:, 1, :], in1=Cbv0, op=ALU.mult), [bbub, bCb], [br3])
                                                P.op("dve", lambda e: e.tensor_tensor(out=r4[:], in0=bub16[:, 0, :], in1=Sbv0, op=ALU.mult), [bbub, bSb], [br4])
                                                P.op("dve", lambda e: e.tensor_tensor(out=r1[:], in0=r1[:], in1=r2[:], op=ALU.add), [br1, br2], [br1])
                                                P.op("dve", lambda e: e.tensor_tensor(out=r3[:], in0=r3[:], in1=r4[:], op=ALU.subtract), [br3, br4], [br3])
                                                P.op("dve", lambda e: e.tensor_tensor_scan(out=rv(shrb[:]), data0=rbc, data1=rv(r1[:]), initial=0.0, op0=ALU.mult, op1=ALU.add),
                                                     [brl, br1], [bshrb])
                                                P.op("dve", lambda e: e.tensor_tensor_scan(out=rv(shib[:]), data0=rbc, data1=rv(r3[:]), initial=0.0, op0=ALU.mult, op1=ALU.add),
                                                     [brl, br3], [bshib])
                                                Cbv = rv(Cb[:]); Sbv = rv(Sb[:])
                                                P.op("dve", lambda e: e.tensor_tensor(out=u1[:], in0=shrb[:], in1=Cbv, op=ALU.mult), [bshrb, bCb], [bu1])
                                                P.op("dve", lambda e: e.tensor_tensor(out=u2[:], in0=shib[:], in1=Sbv, op=ALU.mult), [bshib, bSb], [bu2])
                                                P.op("dve", lambda e: e.tensor_tensor(out=u3[:], in0=shrb[:], in1=Sbv, op=ALU.mult), [bshrb, bSb], [bu3])
                                                P.op("dve", lambda e: e.tensor_tensor(out=u4[:], in0=shib[:], in1=Cbv, op=ALU.mult), [bshib, bCb], [bu4])
                                                P.op("dve", lambda e: e.tensor_tensor(out=srb[:], in0=u1[:], in1=u2[:], op=ALU.subtract), [bu1, bu2], [bsrb])
                                                P.op("dve", lambda e: e.tensor_tensor(out=sib[:], in0=u3[:], in1=u4[:], op=ALU.add), [bu3, bu4], [bsib])
                                                for tb in range(4):
                                                    k_ = cn["y"] % 2; cn["y"] += 1
                                                    lsl = slice(tb * 512, (tb + 1) * 512)
                                                    tsl = slice(s * S + tb * 512, s * S + (tb + 1) * 512)
                                                    P.op("pe", lambda e, k_=k_, lsl=lsl: e.matmul(py[k_][:], lhsT=PC[:, dq, 0, :], rhs=srb[:, lsl], start=True, stop=False), [bPC, bsrb], [bpy[k_]])
                                                    P.op("pe", lambda e, k_=k_, lsl=lsl: e.matmul(py[k_][:], lhsT=PC[:, dq, 1, :], rhs=sib[:, lsl], start=False, stop=True), [bPC, bsib], [bpy[k_]])
                                                    P.op("dve", lambda e, k_=k_, tsl=tsl: e.tensor_tensor(out=yacc[:, tsl], in0=py[k_][:], in1=yacc[:, tsl], op=ALU.add), [bpy[k_], byacc], [byacc])
                                            seq()
                                    lanetile()
                            for s in range(NS):
                                if "S5_NOGELU" in K.dbg:
                                    continue
                                def gel(s=s):
                                    tk = slice(s * S, (s + 1) * S)
                                    P.op("pool", lambda e: e.tensor_tensor(out=tt1[:], in0=yacc[:, tk], in1=yacc[:, tk], op=ALU.mult), [byacc], [bt1])
                                    P.op("pool", lambda e: e.tensor_scalar(out=tt1[:], in0=tt1[:], scalar1=0.044715, scalar2=1.0, op0=ALU.mult, op1=ALU.add), [bt1], [bt1])
                                    P.op("pool", lambda e: e.tensor_tensor(out=tt1[:], in0=tt1[:], in1=yacc[:, tk], op=ALU.mult), [bt1, byacc], [bt1])
                                    P.op("act", lambda e: e.activation(out=tt2[:], in_=tt1[:], func=AF.Sigmoid, scale=2.0 * math.sqrt(2.0 / math.pi)), [bt1], [bt2])
                                    P.op("dve", lambda e: e.tensor_tensor(out=zb[:, ct, tk], in0=yacc[:, tk], in1=tt2[:], op=ALU.mult), [byacc, bt2], [bzb])
                                gel()
                        chtile()
                if phase_done("O2b_%d" % layer):
                    return None
                P.begin()
                with ExitStack() as st2:
                    lin = Lin(st2, 4, 1, "s5g")
                    sgs = Stage(st2, "s5_sg", [128, 512], F32, 2)
                    obs = Stage(st2, "s5_ob", [128, 512], BF16, 3)
                    for oc in range(4):
                        if "S5_NOGLU" in K.dbg:
                            continue
                        def epi(tb, sl, pss, bps, oc=oc):
                            sg, bsg = sgs.next()
                            P.op("act", lambda e: e.activation(out=sg[:], in_=pss[0][:], func=AF.Sigmoid, bias=K.pcol("s5_b_glu%d" % li, oc)), [bps[0], bpp], [bsg])
                            ob, bob = obs.next()
                            P.op("dve", lambda e: e.tensor_tensor(out=ob[:], in0=zb[:, oc, sl], in1=sg[:], op=ALU.mult), [bzb, bsg], [bob])
                            store(MIXO[oc, :, sl], ob[:], bob, B("MIXO"))
                        lin.run(zb, bzb, T, [(w_glu[li], oc * 128)], epi)
            if phase_done("O2_%d" % layer):
                return None
            P.begin()
            with ExitStack() as st:
                xt = [K.sb(st, "c_x%d" % i, [128, S + 30], BF16) for i in range(2)]; bxt = [Buf(), Buf()]
                for i in range(2):
                    P.op("pool", lambda e, i=i: e.memset(xt[i][:, 0:15], 0.0), [], [bxt[i]])
                    P.op("pool", lambda e, i=i: e.memset(xt[i][:, S + 15:S + 30], 0.0), [], [bxt[i]])
                ucv = K.sb(st, "c_u", [128, 4, S], F32); bucv = Buf()
                sqt = K.sb(st, "c_sq", [128, 512], F32); bsqt = Buf()
                sd = K.sb(st, "c_sd", [128, 512], F32); bsd = Buf()
                yt = K.sb(st, "c_y", [128, 512], F32); byt = Buf()
                obs = Stage(st, "c_ob", [128, 512], BF16, 3)
                pm = K.ps(st, "c_pm"); bpm = Buf()
                pv = K.ps(st, "c_pv"); bpv = Buf()
                pcv = [K.ps(st, "c_pc%d" % i) for i in range(2)]; bpcv = [Buf(), Buf()]
                identf = K.sb(st, "c_id", [128, 128], F32); bid = Buf()
                P.dma("sp", identf[:], identd, writes=[bid])
                Dg = K.sb(st, "c_Dg", [128, 4, 31, 128], BF16); bDg = Buf()
                for c in range(4):
                    for k in range(31):
                        eng = "dve" if (k % 3) != 2 else "pool"
                        P.op(eng, lambda e, c=c, k=k: e.tensor_scalar(out=Dg[:, c, k, :], in0=identf[:], scalar1=K.pcol("conv_w%d_%d" % (li, k), c), scalar2=None,
                                                                      op0=ALU.mult), [bid, bpp], [bDg])
                cn3 = {"x": 0, "p": 0}
                for s in range(NS):
                    def cseq(s=s):
                        for c in range(4):
                            def cconv(c=c):
                                i_ = cn3["x"] % 2; cn3["x"] += 1
                                x_ = xt[i_]; bx_ = bxt[i_]
                                P.dma("pool", x_[:, 15:15 + S], DV[c, :, s * S:(s + 1) * S], reads=[B("DV")], writes=[bx_])
                                for tb in range(4):
                                    j_ = cn3["p"] % 2; cn3["p"] += 1
                                    for k in range(31):
                                        P.op("pe", lambda e, k=k, tb=tb, j_=j_: e.matmul(pcv[j_][:], lhsT=Dg[:, c, k, :], rhs=x_[:, tb * 512 + k: tb * 512 + k + 512],
                                                                                          start=(k == 0), stop=(k == 30)), [bDg, bx_], [bpcv[j_]])
                                    P.op("act", lambda e, tb=tb, j_=j_: e.activation(out=ucv[:, c, tb * 512:(tb + 1) * 512], in_=pcv[j_][:], func=AF.Identity,
                                                                                     bias=K.pcol("conv_b%d" % li, c)), [bpcv[j_], bpp], [bucv])
                            cconv()
                        for tb in range(4):
                            def ln(tb=tb):
                                sl = slice(tb * 512, (tb + 1) * 512)
                                for c in range(4):
                                    P.op("pe", lambda e, c=c: e.matmul(pm[:], lhsT=ones512f[:], rhs=ucv[:, c, sl], start=(c == 0), stop=(c == 3)), [bconst, bucv], [bpm])
                                for c in range(4):
                                    P.op("dve", lambda e, c=c: e.tensor_tensor(out=ucv[:, c, sl], in0=ucv[:, c, sl], in1=pm[:], op=ALU.subtract), [bucv, bpm], [bucv])
                                for c in range(4):
                                    P.op("act", lambda e, c=c: e.activation(out=sqt[:], in_=ucv[:, c, sl], func=AF.Square), [bucv], [bsqt])
                                    P.op("pe", lambda e, c=c: e.matmul(pv[:], lhsT=ones512f[:], rhs=sqt[:], start=(c == 0), stop=(c == 3)), [bconst, bsqt], [bpv])
                                P.op("act", lambda e: e.activation(out=sd[:], in_=pv[:], func=AF.Sqrt, bias=EPS), [bpv], [bsd])
                                P.op("dve", lambda e: e.reciprocal(out=sd[:], in_=sd[:]), [bsd], [bsd])
                                for c in range(4):
                                    P.op("dve", lambda e, c=c: e.tensor_tensor(out=yt[:], in0=ucv[:, c, sl], in1=sd[:], op=ALU.mult), [bucv, bsd], [byt])
                                    P.op("pool", lambda e, c=c: e.tensor_scalar(out=yt[:], in0=yt[:], scalar1=K.pcol("conv_ln_g%d" % li, c),
                                                                                scalar2=K.pcol("conv_ln_b%d" % li, c), op0=ALU.mult, op1=ALU.add), [byt, bpp], [byt])
                                    ob, bob = obs.next()
                                    P.op("act", lambda e, ob=ob: e.activation(out=ob[:], in_=yt[:], func=AF.Silu), [byt], [bob])
                                    store(MIXO[4 + c, :, s * S + tb * 512: s * S + (tb + 1) * 512], ob[:], bob, B("MIXO"))
                            ln()
                    cseq()
            if phase_done("O3_%d" % layer):
                return None
            r_ = out_proj_phase(w_out_cd[li], cur_src, cur_bsrc)
            if phase_done("O4_%d" % layer):
                return None
            return r_

        cur_src, cur_bsrc = xT, bXT
        for layer in range(DEPTH):
            li = layer // 2
            if "ONLYODD" in K.dbg and layer == 0:
                continue
            if layer % 2 == 0:
                P.begin()
                with ExitStack() as st:
                    hn = K.sb(st, "hn", [128, 8, T], BF16); bhn = [Buf() for _ in range(8)]
                    emit_norm(st, cur_src, cur_bsrc, "g_mix%d" % layer, hn, bhn)
                    Wab = w_in_ab[li]
                    Wpm = w_qkperm[li]
                    with ExitStack() as st2:
                        cosT = K.sb(st2, "cosT", [128, T], F32); bcos = Buf()
                        sinT = K.sb(st2, "sinT", [128, T], F32); bsin = Buf()
                        P.dma("sp", cosT[:], COS, reads=[B("COS")], writes=[bcos])
                        P.dma("sp", sinT[:], SINS, reads=[B("SINS")], writes=[bsin])
                        lin2 = Lin(st2, 8, 2, "e1a")
                        t1s = Stage(st2, "e1_t1", [128, 512], F32, 2)
                        t2s = Stage(st2, "e1_t2", [128, 512], F32, 2)
                        obs = Stage(st2, "e1_ob", [128, 512], BF16, 3)
                        for j in range(8):
                            dst = QA if j < 4 else KA
                            bd = B("QA") if j < 4 else B("KA")
                            def epi(tb, sl, pss, bps, j=j, dst=dst, bd=bd):
                                t1, bt1 = t1s.next(); t2, bt2 = t2s.next(); ob, bob = obs.next()
                                P.op("dve", lambda e: e.tensor_tensor(out=t1[:], in0=pss[0][:], in1=cosT[:, sl], op=ALU.mult),
                                     [bps[0], bcos], [bt1])
                                P.op("dve", lambda e: e.tensor_tensor(out=t2[:], in0=pss[1][:], in1=sinT[:, sl], op=ALU.mult),
                                     [bps[1], bsin], [bt2])
                                P.op("pool", lambda e: e.tensor_tensor(out=ob[:], in0=t1[:], in1=t2[:], op=ALU.add),
                                     [bt1, bt2], [bob])
                                store(dst[j % 4, :, sl], ob[:], bob, bd)
                            lin2.run(hn, bhn, T, [(Wab, j * 128), (Wpm, j * 128)], epi)
                        if "DBGCOS" in K.dbg:
                            dc = K.scratch("DBGCOS", [128, T], F32)
                            P.dma("act", dc, cosT[:], reads=[bcos], writes=[B("DBGCOS")])
                    P.end(); P.begin()
                    if "DBGHN" in K.dbg:
                        dh = K.scratch("DBGHN", [8, 128, T], BF16)
                        for c in range(8):
                            P.dma("act", dh[c], hn[:, c, :], reads=bhn, writes=[B("DBGHN")])
                    with ExitStack() as st2:
                        lin1 = Lin(st2, 8, 1, "e1b")
                        ofs = Stage(st2, "e1_of", [128, 512], F32, 3)
                        obs = Stage(st2, "e1_ob2", [128, 512], BF16, 3)
                        zext = K.sb(st2, "zext", [33, T], F32); bz = Buf()
                        P.op("dve", lambda e: e.memset(zext[32:33, :], 1.0), [], [bz])
                        for j in range(2):
                            for (c0, dst, bd, scl) in ((1536, QB, B("QB"), 0.125), (1792, KBf, B("KBF"), 1.0)):
                                def epi(tb, sl, pss, bps, j=j, dst=dst, bd=bd, scl=scl):
                                    ot, bot = ofs.next()
                                    P.op("act", lambda e: e.activation(out=ot[:], in_=pss[0][:], func=AF.Copy, scale=scl),
                                         [bps[0]], [bot])
                                    store(dst[j, :, sl], ot[:], bot, bd)
                                lin1.run(hn, bhn, T, [(Wab, c0 + j * 128)], epi)
                        for j in range(4):
                            def epi(tb, sl, pss, bps, j=j):
                                ob, bob = obs.next()
                                P.op("act", lambda e: e.activation(out=ob[:], in_=pss[0][:], func=AF.Silu), [bps[0]], [bob])
                                store(RB[j, :, sl], ob[:], bob, B("RB"))
                            lin1.run(hn, bhn, T, [(Wab, 2560 + j * 128)], epi)
                        def epi(tb, sl, pss, bps):
                            P.op("act", lambda e: e.activation(out=zext[0:32, sl], in_=pss[0][0:32, :], func=AF.Copy), [bps[0]], [bz])
                        lin1.run(hn, bhn, T, [(Wab, 3072)], epi, M=32)
                        wr = K.sb(st2, "e1_wr", [128, 8, 512], BF16); bwr = Buf()
                        w2b = K.sb(st2, "e1_w2b", [33, 512], F32); bw2 = Buf()
                        P.dma("sp", w2b[:], w2blk_d[li], writes=[bw2])
                        pst = [K.ps(st2, "e1_tp%d" % i) for i in range(2)]
                        bpt = [Buf(), Buf()]
                        tobs = Stage(st2, "e1_tob", [128, 512], BF16, 3)
                        tofs = Stage(st2, "e1_tof", [128, 512], F32, 3)
                        it = 0
                        for (c0, ncol, dst, bd, isbf) in ((1024, 512, VA, B("VA"), True), (2048, 512, VB, B("VB"), True),
                                                         (1792, 256, KBT, B("KBT"), False)):
                            P.dma("pool", wr[:, :, 0:ncol], Wab.rearrange("(kc p) f -> p kc f", p=128)[:, :, c0:c0 + ncol], writes=[bwr])
                            for blk in range(T // 128):
                                tsl = slice(blk * 128, (blk + 1) * 128)
                                ps_ = pst[it % 2]; bp_ = bpt[it % 2]; it += 1
                                for kc in range(8):
                                    P.op("pe", lambda e, ps_=ps_, kc=kc, tsl=tsl, ncol=ncol: e.matmul(
                                        ps_[:, 0:ncol], lhsT=hn[:, kc, tsl], rhs=wr[:, kc, 0:ncol], start=(kc == 0), stop=(kc == 7)),
                                        [bhn[blk // 4], bwr], [bp_])
                                if isbf:
                                    ot, bot = tobs.next()
                                else:
                                    ot, bot = tofs.next()
                                P.op("act", lambda e, ot=ot, ps_=ps_, ncol=ncol: e.activation(out=ot[:, 0:ncol], in_=ps_[:, 0:ncol], func=AF.Copy),
                                     [bp_], [bot])
                                store(dst[tsl, :], ot[:, 0:ncol], bot, bd)
                        for blk in range(T // 128):
                            tsl = slice(blk * 128, (blk + 1) * 128)
                            ps_ = pst[it % 2]; bp_ = bpt[it % 2]; it += 1
                            P.op("pe", lambda e, ps_=ps_, tsl=tsl: e.matmul(ps_[:], lhsT=zext[0:33, tsl], rhs=w2b[0:33, :], start=True, stop=True),
                                 [bz, bw2], [bp_])
                            t1, bt1 = tofs.next()
                            P.op("act", lambda e, t1=t1, ps_=ps_: e.activation(out=t1[:], in_=ps_[:], func=AF.Exp, scale=-1.0), [bp_], [bt1])
                            ot, bot = tofs.next()
                            P.op("act", lambda e, t1=t1, ot=ot: e.activation(out=ot[:], in_=t1[:], func=AF.Ln, bias=1.0), [bt1], [bot])
                            store(LG[tsl, :], ot[:], bot, B("LG"))
                if phase_done("E1_%d" % layer):
                    return K
                P.begin()
                with ExitStack() as st:
                    mk = K.sb(st, "a_mask", [128, 20, 512], BF16); bmk = Buf()
                    for n in range(0, 20, 4):
                        P.dma("pool", mk[:, n:n + 4, :], maskd[:, n * 512:(n + 4) * 512].rearrange("p (a b) -> p a b", a=4), writes=[bmk])
                    onesel = K.sb(st, "a_osel", [128, 2, 128], BF16); bos = Buf()
                    P.op("dve", lambda e: e.memset(onesel[:], 0.0), [], [bos])
                    P.op("dve", lambda e: e.memset(onesel[:, 0, 0:64], 1.0), [], [bos])
                    P.op("dve", lambda e: e.memset(onesel[:, 1, 64:128], 1.0), [], [bos])
                    qt = [K.sb(st, "a_q%d" % i, [128, S], BF16) for i in range(2)]; bq = [Buf(), Buf()]
                    kt = [K.sb(st, "a_k%d" % i, [128, S], BF16) for i in range(2)]; bk = [Buf(), Buf()]
                    vp = [K.sb(st, "a_v%d" % i, [128, 16, 2, 128], BF16) for i in range(2)]; bv = [Buf(), Buf()]
                    for i in range(2):
                        P.op("pool", lambda e, i=i: e.memset(vp[i][:, :, 0, 64:128], 1.0), [], [bv[i]])
                        P.op("pool", lambda e, i=i: e.memset(vp[i][:, :, 1, 0:64], 1.0), [], [bv[i]])
                    pss = [K.ps(st, "a_s%d" % i) for i in range(3)]; bpss = [Buf() for _ in range(3)]
                    pacc = [[K.ps(st, "a_acc%d_%d" % (i, j)) for j in range(2)] for i in range(2)]
                    bpacc = [[Buf(), Buf()], [Buf(), Buf()]]
                    exs = Stage(st, "a_ex", [128, 512], BF16, 4)
                    rec = K.sb(st, "a_rec", [128, 512], F32); brec = Buf()
                    obs = Stage(st, "a_ob", [128, 512], BF16, 2)
                    it = 0; sit = [0]; ait = 0
                    pend_a = []

                    def flush_a():
                        while pend_a:
                            d_, o_, bo_ = pend_a.pop(0)
                            store(d_, o_[:], bo_, B("MIXO"))
                    for s in range(NS):
                        for c in range(4):
                            b_ = it % 2; it += 1
                            tok = slice(s * S, (s + 1) * S)
                            P.dma("sp", qt[b_][:], QA[c, :, tok], reads=[B("QA")], writes=[bq[b_]])
                            P.dma("sp", kt[b_][:], KA[c, :, tok], reads=[B("KA")], writes=[bk[b_]])
                            for e2 in range(2):
                                h = 2 * c + e2
                                P.dma("sp", vp[b_][:, :, e2, e2 * 64:(e2 + 1) * 64],
                                      VA[tok, h * 64:(h + 1) * 64].rearrange("(b p) f -> p b f", p=128),
                                      reads=[B("VA")], writes=[bv[b_]])
                            for qb in range(4):
                                a_ = ait % 2; ait += 1
                                kbs = [kb for kb in range(16) if -8 <= kb - 4 * qb <= 11]
                                units = [(kb, e2) for kb in kbs for e2 in range(2)]
                                nmm = len(units)
                                st_ = {}

                                def stA(u, b_=b_, qb=qb, units=units, st_=st_):
                                    kb, e2 = units[u]
                                    rows = slice(e2 * 64, (e2 + 1) * 64)
                                    p_ = pss[sit[0] % 3]; bp_ = bpss[sit[0] % 3]; sit[0] += 1
                                    P.op("pe", lambda e: e.matmul(p_[:], lhsT=kt[b_][rows, kb * 128:(kb + 1) * 128], rhs=qt[b_][rows, qb * 512:(qb + 1) * 512],
                                                                  start=True, stop=True), [bk[b_], bq[b_]], [bp_])
                                    st_[u] = (p_, bp_)

                                def stB(u, qb=qb, units=units, st_=st_):
                                    kb, e2 = units[u]
                                    p_, bp_ = st_[u]
                                    ex, bex = exs.next()
                                    P.op("act", lambda e: e.activation(out=ex[:], in_=p_[:], func=AF.Exp, scale=0.125), [bp_], [bex])
                                    n = kb - 4 * qb + 8
                                    eng = "dve" if (u % 3) != 2 else "pool"
                                    P.op(eng, lambda e: e.tensor_tensor(out=ex[:], in0=ex[:], in1=mk[:, n, :], op=ALU.mult), [bex, bmk], [bex])
                                    st_[u] = (ex, bex)

                                def stC(u, a_=a_, b_=b_, units=units, st_=st_, nmm=nmm):
                                    kb, e2 = units[u]
                                    ex, bex = st_[u]
                                    first = (u < 2); last = (u >= nmm - 2)
                                    P.op("pe", lambda e: e.matmul(pacc[a_][e2][:], lhsT=vp[b_][:, kb, e2, :], rhs=ex[:], start=first, stop=last),
                                         [bv[b_], bex], [bpacc[a_][e2]])

                                for step in range(nmm + 2):
                                    if step == 8:
                                        flush_a()
                                    if step < nmm:
                                        stA(step)
                                    if 0 <= step - 1 < nmm:
                                        stB(step - 1)
                                    if 0 <= step - 2 < nmm:
                                        stC(step - 2)
                                ob, bob = obs.next()
                                P.op("dve", lambda e, a_=a_: e.reciprocal(out=rec[64:128, :], in_=pacc[a_][0][64:128, :]), [bpacc[a_][0]], [brec])
                                P.op("dve", lambda e, a_=a_: e.reciprocal(out=rec[0:64, :], in_=pacc[a_][1][0:64, :]), [bpacc[a_][1]], [brec])
                                P.op("dve", lambda e, a_=a_, ob=ob: e.tensor_tensor(out=ob[0:64, :], in0=pacc[a_][0][0:64, :], in1=rec[64:128, :], op=ALU.mult),
                                     [bpacc[a_][0], brec], [bob])
                                P.op("dve", lambda e, a_=a_, ob=ob: e.tensor_tensor(out=ob[64:128, :], in0=pacc[a_][1][64:128, :], in1=rec[0:64, :], op=ALU.mult),
                                     [bpacc[a_][1], brec], [bob])
                                pend_a.append((MIXO[c, :, s * S + qb * 512: s * S + (qb + 1) * 512], ob, bob))
                    flush_a()
                if phase_done("E2_%d" % layer):
                    return K
                P.begin()
                with ExitStack() as st:
                    gc = K.sb(st, "g_c", [128, 770], F32); bgc = Buf()
                    P.dma("sp", gc[:], glac, writes=[bgc])
                    A_pi = gc[:, 0:128]; A_si = gc[:, 128:256]; A_se = gc[:, 256:384]; A_pe = gc[:, 384:512]
                    MFB = gc[:, 512:768]; CH = gc[:, 768:770]
                    qf = K.sb(st, "g_q", [128, 2, S], F32); bqf = Buf()
                    kf_ = K.sb(st, "g_k", [128, 2, S], F32); bkf_ = Buf()
                    kbt = K.sb(st, "g_kt", [128, 16, 256], F32); bkbt = Buf()
                    vb = K.sb(st, "g_v", [128, 16, 512], BF16); bvb = Buf()
                    lg = K.sb(st, "g_lg", [128, 16, 512], F32); blg = Buf()
                    rb = K.sb(st, "g_rb", [128, 4, S], BF16); brb = Buf()
                    Sst = K.sb(st, "g_Sst", [128, 2, 32, 128], BF16); bSst = Buf()
                    Sb = K.sb(st, "g_Sb", [128, 2, 128], F32); bSb = Buf()
                    Sf = K.sb(st, "g_Sf", [128, 2, 128], F32); bSf = Buf()
                    Sfb = K.sb(st, "g_Sfb", [128, 2, 128], BF16); bSfb = Buf()
                    eE = K.sb(st, "g_eE", [128, 256], F32); beE = Buf()
                    kU = K.sb(st, "g_kU", [128, 256], BF16); bkU = Buf()
                    av = K.sb(st, "g_a", [128, 2, 2], F32); bav = Buf()
                    eG = K.sb(st, "g_eG", [128, 4, 128], F32); beG = Buf()
                    enG = K.sb(st, "g_enG", [128, 4, 128], F32); benG = Buf()
                    qin = K.sb(st, "g_qin", [128, 2, 2, 128], BF16); bqin = Buf()
                    kin = K.sb(st, "g_kin", [128, 2, 2, 128], BF16); bkin = Buf()
                    atts = Stage(st, "g_att", [128, 256], BF16, 2)
                    osb = K.sb(st, "g_osb", [128, 512], F32); bosb = Buf()
                    osq = K.sb(st, "g_osq", [128, 512], BF16); bosq = Buf()
                    sd = K.sb(st, "g_sd", [128, 512], F32); bsd = Buf()
                    obs = Stage(st, "g_ob", [128, 512], BF16, 3)
                    pend_g = []

                    def flush_g():
                        while pend_g:
                            d_, o_, bo_ = pend_g.pop(0)
                            store(d_, o_[:], bo_, B("MIXO"))
                    pO = [K.ps(st, "g_pO%d" % i) for i in range(4)]; bpO = [Buf() for _ in range(4)]
                    pG = K.ps(st, "g_pG"); bpG = Buf()
                    pA = K.ps(st, "g_pA"); bpA = Buf()
                    pU = K.ps(st, "g_pU"); bpU = Buf()
                    pM = K.ps(st, "g_pM"); bpM = Buf()
                    for s in range(NS):
                        tok = slice(s * S, (s + 1) * S)
                        for ct in range(2):
                            P.dma("sp", qf[:, ct, :], QB[ct, :, tok], reads=[B("QB")], writes=[bqf])
                            P.dma("sp", kf_[:, ct, :], KBf[ct, :, tok], reads=[B("KBF")], writes=[bkf_])
                        for c in range(4):
                            P.dma("sp", rb[:, c, :], RB[c, :, tok], reads=[B("RB")], writes=[brb])
                        P.dma("sp", kbt[:], KBT[tok, :].rearrange("(b p) f -> p b f", p=128), reads=[B("KBT")], writes=[bkbt])
                        P.dma("sp", vb[:], VB[tok, :].rearrange("(b p) f -> p b f", p=128), reads=[B("VB")], writes=[bvb])
                        P.dma("sp", lg[:], LG[tok, :].rearrange("(b p) f -> p b f", p=128), reads=[B("LG")], writes=[blg])
                        P.op("dve", lambda e: e.memset(Sb[:], 0.0), [], [bSb])
                        P.op("dve", lambda e: e.memset(Sf[:], 0.0), [], [bSf])
                        P.op("dve", lambda e: e.memset(Sfb[:], 0.0), [], [bSfb])

                        def state_step(blk, cc, dcol, Sx, bSx):
                            trow = slice(cc * 64, (cc + 1) * 64)
                            for ct in range(2):
                                P.op("pe", lambda e, ct=ct: e.matmul(pU[:, ct * 256:(ct + 1) * 256], lhsT=kU[trow, ct * 128:(ct + 1) * 128],
                                                                     rhs=vb[trow, blk, ct * 256:(ct + 1) * 256], start=True, stop=True),
                                     [bkU, bvb], [bpU])
                            for ct in range(2):
                                for h2 in range(2):
                                    rows = slice(h2 * 64, (h2 + 1) * 64)
                                    P.op("dve", lambda e, ct=ct, h2=h2, rows=rows: e.scalar_tensor_tensor(
                                        out=Sx[rows, ct, :], in0=Sx[rows, ct, :], scalar=av[rows, ct, cc:cc + 1],
                                        in1=pU[rows, ct * 256 + h2 * 128: ct * 256 + (h2 + 1) * 128], op0=ALU.mult, op1=ALU.add),
                                        [bSx, bav, bpU], [bSx])

                        def prep_dir(blk, A_e, lcol0):
                            P.op("pe", lambda e: e.matmul(pM[:, 0:256], lhsT=A_e, rhs=lg[:, blk, lcol0:lcol0 + 256], start=True, stop=True),
                                 [bgc, blg], [bpM])
                            P.op("act", lambda e: e.activation(out=eE[:], in_=pM[:, 0:256], func=AF.Exp), [bpM], [beE])
                            P.op("dve", lambda e: e.tensor_tensor(out=kU[:], in0=kbt[:, blk, :], in1=eE[:], op=ALU.mult), [bkbt, beE], [bkU])
                            for ct in range(2):
                                P.op("pe", lambda e, ct=ct: e.matmul(pM[:, 256 + 2 * ct:256 + 2 * ct + 2], lhsT=lg[:, blk, lcol0 + ct * 128:lcol0 + (ct + 1) * 128],
                                                                     rhs=CH, start=True, stop=True), [bgc, blg], [bpM])
                            P.op("act", lambda e: e.activation(out=av[:].rearrange("p a b -> p (a b)"), in_=pM[:, 256:260], func=AF.Exp), [bpM], [bav])

                        for blk in range(15, -1, -1):
                            prep_dir(blk, A_pe, 256)
                            for cc in (1, 0):
                                n = 2 * blk + cc
                                P.op("act", lambda e, n=n: e.activation(out=Sst[:, :, n, :], in_=Sb[:], func=AF.Copy), [bSb], [bSst])
                                state_step(blk, cc, 256, Sb, bSb)
                        for grp in range(4):
                            for b4 in range(4):
                                blk = grp * 4 + b4
                                bt = slice(blk * 128, (blk + 1) * 128)
                                oc = slice(b4 * 128, (b4 + 1) * 128)
                                for d_, (A_g, lc0) in enumerate(((A_pi, 0), (A_si, 256))):
                                    for ct in range(2):
                                        P.op("pe", lambda e, d_=d_, ct=ct, A_g=A_g, lc0=lc0, blk=blk: e.matmul(
                                            pG[:, (d_ * 2 + ct) * 128:(d_ * 2 + ct + 1) * 128],
                                            lhsT=lg[:, blk, lc0 + ct * 128:lc0 + (ct + 1) * 128], rhs=A_g, start=True, stop=True),
                                            [bgc, blg], [bpG])
                                P.op("act", lambda e: e.activation(out=eG[:].rearrange("p a b -> p (a b)"), in_=pG[:], func=AF.Exp), [bpG], [beG])
                                P.op("act", lambda e: e.activation(out=enG[:].rearrange("p a b -> p (a b)"), in_=pG[:], func=AF.Exp, scale=-1.0), [bpG], [benG])
                                for d_ in range(2):
                                    P.op("dve", lambda e, d_=d_, bt=bt: e.tensor_tensor(out=qin[:, d_, :, :], in0=qf[:, :, bt], in1=eG[:, 2 * d_:2 * d_ + 2, :], op=ALU.mult),
                                         [bqf, beG], [bqin])
                                    P.op("pool", lambda e, d_=d_, bt=bt: e.tensor_tensor(out=kin[:, d_, :, :], in0=kf_[:, :, bt], in1=enG[:, 2 * d_:2 * d_ + 2, :], op=ALU.mult),
                                         [bkf_, benG], [bkin])
                                prep_dir(blk, A_se, 0)
                                for ct in range(2):
                                    for h2 in range(2):
                                        h = 2 * ct + h2
                                        rows = slice(h2 * 64, (h2 + 1) * 64)
                                        for d_ in range(2):
                                            P.op("pe", lambda e, d_=d_, ct=ct, rows=rows: e.matmul(
                                                pA[:, d_ * 128:(d_ + 1) * 128], lhsT=kin[rows, d_, ct, :], rhs=qin[rows, d_, ct, :], start=True, stop=True),
                                                [bkin, bqin], [bpA])
                                        at, bat = atts.next()
                                        P.op("dve", lambda e, at=at: e.tensor_tensor(out=at[:], in0=pA[:, 0:256], in1=MFB, op=ALU.mult), [bpA, bgc], [bat])
                                        P.op("pe", lambda e, at=at, h=h, oc=oc, blk=blk: e.matmul(pO[h][:, oc], lhsT=vb[:, blk, h * 128:(h + 1) * 128], rhs=at[:, 0:128], start=True, stop=False),
                                             [bvb, bat], [bpO[h]])
                                        if "NOBWD" not in K.dbg:
                                            P.op("pe", lambda e, at=at, h=h, oc=oc, blk=blk: e.matmul(pO[h][:, oc], lhsT=vb[:, blk, h * 128:(h + 1) * 128], rhs=at[:, 128:256], start=False, stop=False),
                                                 [bvb, bat], [bpO[h]])
                                for cc in range(2):
                                    n = 2 * blk + cc
                                    occ = slice(b4 * 128 + cc * 64, b4 * 128 + (cc + 1) * 64)
                                    qc = slice(cc * 64, (cc + 1) * 64)
                                    for ct in range(2):
                                        for h2 in range(2):
                                            h = 2 * ct + h2
                                            rows = slice(h2 * 64, (h2 + 1) * 64)
                                            nob = "NOBWD" in K.dbg
                                            nof = "NOFWDINTER" in K.dbg
                                            if not nof:
                                                P.op("pe", lambda e, h=h, ct=ct, rows=rows, occ=occ, qc=qc, cc=cc, nob=nob: e.matmul(
                                                    pO[h][:, occ], lhsT=Sfb[rows, ct, :], rhs=qin[rows, 0, ct, qc], start=False, stop=(nob and cc == 1)),
                                                    [bSfb, bqin], [bpO[h]])
                                            if not nob:
                                                P.op("pe", lambda e, h=h, ct=ct, rows=rows, occ=occ, qc=qc, n=n, cc=cc: e.matmul(
                                                    pO[h][:, occ], lhsT=Sst[rows, ct, n, :], rhs=qin[rows, 1, ct, qc], start=False, stop=(cc == 1)),
                                                    [bSst, bqin], [bpO[h]])
                                    state_step(blk, cc, 0, Sf, bSf)
                                    P.op("act", lambda e: e.activation(out=Sfb[:], in_=Sf[:], func=AF.Copy), [bSf], [bSfb])
                            gt = slice(grp * 512, (grp + 1) * 512)
                            for h in range(4):
                                P.op("act", lambda e, h=h: e.activation(out=osb[:], in_=pO[h][:], func=AF.Copy), [bpO[h]], [bosb])
                                if "DBGGLA" in K.dbg:
                                    if "DBGGLA" not in K.scr:
                                        K.scratch("DBGGLA", [4, 128, T], F32)
                                    P.dma("act", K.scr["DBGGLA"][0][h, :, s * S + grp * 512: s * S + (grp + 1) * 512], osb[:], reads=[bosb], writes=[B("DBGGLA")])
                                P.op("act", lambda e, h=h: e.activation(out=osq[:], in_=pO[h][:], func=AF.Square), [bpO[h]], [bosq])
                                P.op("pe", lambda e: e.matmul(pM[:], lhsT=ones128[:], rhs=osq[:], start=True, stop=True), [bosq, bconst], [bpM])
                                P.op("act", lambda e: e.activation(out=sd[:], in_=pM[:], func=AF.Sqrt, bias=EPS), [bpM], [bsd])
                                flush_g()
                                P.op("dve", lambda e: e.reciprocal(out=sd[:], in_=sd[:]), [bsd], [bsd])
                                P.op("dve", lambda e, h=h: e.scalar_tensor_tensor(out=osb[:], in0=osb[:], scalar=K.pcol("gla_norm%d" % li, h), in1=sd[:],
                                                                              op0=ALU.mult, op1=ALU.mult), [bosb, bsd, bpp], [bosb])
                                ob, bob = obs.next()
                                P.op("dve", lambda e, h=h, ob=ob, gt=gt: e.tensor_tensor(out=ob[:], in0=osb[:], in1=rb[:, h, gt], op=ALU.mult),
                                     [bosb, brb], [bob])
                                pend_g.append((MIXO[4 + h, :, s * S + grp * 512: s * S + (grp + 1) * 512], ob, bob))
                    flush_g()
                if phase_done("E3_%d" % layer):
                    return K
                cur_src, cur_bsrc = out_proj_phase(w_out_ab[li], cur_src, cur_bsrc)
                if phase_done("E4_%d" % layer):
                    return K
            else:
                r_ = odd_phases(layer, li, cur_src, cur_bsrc)
                if r_ is None:
                    return K
                cur_src, cur_bsrc = r_
            r_ = xattn_phases(layer, cur_src, cur_bsrc)
            if r_ is None:
                return K
            cur_src, cur_bsrc = r_
            r_ = ffn_phases(layer, cur_src, cur_bsrc)
            if r_ is None:
                return K
            cur_src, cur_bsrc = r_
        P.begin()
        with ExitStack() as st:
            emit_norm(st, cur_src, cur_bsrc, "g_final", None, None, out_f32=out_d)
        P.end()
        return K


def prepare_inputs(inputs):
    f = lambda k: np.asarray(inputs[k], np.float32)
    pk = PPack()
    invf = (np.float32(10000.0) ** (-np.arange(0, 64, 2, dtype=np.float32) / np.float32(64.0))).astype(np.float32)
    pk.add("invfreq", invf[np.arange(128) % 32])
    pk.add("ropesign", np.where((np.arange(128) % 64) < 32, -1.0, 1.0).astype(np.float32))
    for l in range(DEPTH):
        pk.add("g_mix%d" % l, _chunks(f("g_mix")[l]))
        pk.add("g_xattn%d" % l, _chunks(f("g_xattn")[l]))
        pk.add("g_mem%d" % l, _chunks(f("g_mem")[l]))
        pk.add("g_ffn%d" % l, _chunks(f("g_ffn")[l]))
        pk.add("b_conv_ffn%d" % l, _chunks(f("b_conv_ffn")[l]))
        for k in range(3):
            pk.add("w_conv_ffn%d_%d" % (l, k), _chunks(f("w_conv_ffn")[l, k]))
    pk.add("g_final", _chunks(f("g_final")))
    for i in range(2):
        pk.add("gla_norm%d" % i, _chunks(f("gla_norm")[i]))
        pk.add("s5_d%d" % i, _chunks(f("s5_d")[i]))
        pk.add("s5_b_glu%d" % i, _chunks(f("s5_b_glu")[i]))
        pk.add("conv_b%d" % i, _chunks(f("conv_b")[i]))
        pk.add("conv_ln_g%d" % i, _chunks(f("conv_ln_g")[i]))
        pk.add("conv_ln_b%d" % i, _chunks(f("conv_ln_b")[i]))
        for k in range(31):
            pk.add("conv_w%d_%d" % (i, k), _chunks(f("conv_w")[i, k]))
    pp = pk.build()
    perm = rope_perm()
    w_in_ab = f("w_in_ab")
    w_qkperm = np.ascontiguousarray(np.concatenate([w_in_ab[:, :, 0:512][:, :, perm], w_in_ab[:, :, 512:1024][:, :, perm]], axis=2))
    wg2 = f("gla_wg2"); bg = f("gla_bg")
    w2blk = np.zeros((2, 33, 512), np.float32)
    for i in range(2):
        w2blk[i, 0:16, 0:256] = wg2[i, 0]
        w2blk[i, 16:32, 256:512] = wg2[i, 1]
        w2blk[i, 32, 0:256] = bg[i, 0]
        w2blk[i, 32, 256:512] = bg[i, 1]
    lam_re = f("s5_lam_re"); lam_im = f("s5_lam_im"); log_dt = f("s5_log_dt")
    b_re = f("s5_b_re"); b_im = f("s5_b_im"); c_re = f("s5_c_re"); c_im = f("s5_c_im")
    s5rows = np.zeros((2, 3, 32, 128), np.float32)
    s5B = np.zeros((2, 2, 32, 128, 128), np.float32)
    s5C = np.zeros((2, 2, 32, 128, 128), np.float32)
    for i in range(2):
        for d in range(2):
            for q in range(16):
                dq = d * 16 + q
                for g2 in range(2):
                    g = 2 * q + g2
                    ls = slice(g2 * 64, (g2 + 1) * 64)
                    s5rows[i, 0, dq, ls] = lam_re[i, d, g]
                    s5rows[i, 1, dq, ls] = lam_im[i, d, g]
                    s5rows[i, 2, dq, ls] = log_dt[i, d, g]
                    r0 = (q % 4) * 32 + g2 * 16
                    s5B[i, 0, dq, r0:r0 + 16, ls] = b_re[i, d, g].T
                    s5B[i, 1, dq, r0:r0 + 16, ls] = b_im[i, d, g].T
                    s5C[i, 0, dq, ls, r0:r0 + 16] = c_re[i, d, g].T
                    s5C[i, 1, dq, ls, r0:r0 + 16] = c_im[i, d, g].T
    s5lane = np.ascontiguousarray(s5rows.transpose(0, 3, 1, 2).reshape(2, 128, 96))
    s5B = np.ascontiguousarray(s5B.reshape(2, 2, 2, 16, 128, 128).transpose(0, 1, 2, 4, 3, 5).reshape(2, 2, 2, 128, 2048))
    s5C = np.ascontiguousarray(s5C.reshape(2, 2, 2, 16, 128, 128).transpose(0, 1, 2, 4, 3, 5).reshape(2, 2, 2, 128, 2048))
    shared = {
        "pp": pp, "amask": attn_masks(), "glac": gla_consts(), "ident": np.eye(128, dtype=np.float32),
        "w_in_ab": w_in_ab, "w_qkperm": w_qkperm, "w_out_ab": f("w_out_ab"), "w2blk": w2blk,
        "w_in_cd": f("w_in_cd"), "w_out_cd": f("w_out_cd"), "w_xq": f("w_xq"), "w_xkv": f("w_xkv"),
        "w_xo": f("w_xo"), "w_up": f("w_up"), "w_down": f("w_down"), "w_glu": f("s5_w_glu"),
        "s5rows": s5rows, "s5lane": s5lane, "s5B": s5B, "s5C": s5C,
        "s5idx": np.concatenate([64.0 * np.arange(32), np.arange(64)]).astype(np.float32)[None, :],
    }
    x = f("x"); mem = f("mem"); posn = np.asarray(inputs["positions"], np.int32)
    in_maps = []
    for c in range(NCORES):
        xs = x[NS * c:NS * (c + 1)].reshape(T, D)
        ms = mem[NS * c:NS * (c + 1)].reshape(NS * NMEM, D)
        m = dict(shared)
        m["xT"] = np.ascontiguousarray(xs.T).reshape(8, 128, T)
        m["memT"] = np.ascontiguousarray(ms.T).reshape(8, 128, NS * NMEM)
        m["pos"] = np.ascontiguousarray(posn[NS * c:NS * (c + 1)].reshape(1, T))
        in_maps.append(m)
    return in_maps, pk.idx, pp.shape[1]


def kernel(**inputs):
    in_maps, ppidx, npp = prepare_inputs(inputs)
    K = build_program(ppidx, npp)
    res = run_bass_kernel_spmd(K.nc, in_maps, core_ids=list(range(NCORES)))
    out = np.zeros((NCORES * NS, S, D), np.float32)
    for c in range(NCORES):
        o = np.asarray(res.results[c]["outT"]).reshape(D, T)
        out[NS * c:NS * (c + 1)] = o.T.reshape(NS, S, D)
    return out
```

```python
import math
import numpy as np
from contextlib import ExitStack
import concourse.bass as bass
import concourse.mybir as mybir
from concourse.bass_utils import run_bass_kernel_spmd

F32 = mybir.dt.float32
BF16 = mybir.dt.bfloat16
I32 = mybir.dt.int32
AF = mybir.ActivationFunctionType
ALU = mybir.AluOpType

D = 1024
S = 2048
NS = 2
T = NS * S
DEPTH = 4
NMEM = 256
DFF = 2816
EPS = 1e-6
NCORES = 8
TWO_PI = 2.0 * math.pi

ENGS = ("pe", "act", "dve", "pool", "sp")
N_DMA_SLOTS = 16


class Buf:
    __slots__ = ("w", "r")

    def __init__(self):
        self.w = None
        self.r = []


class Op:
    __slots__ = ("eng", "fn", "deps", "dma", "needed", "ev", "slot")

    def __init__(self, eng, fn, dma):
        self.eng = eng
        self.fn = fn
        self.deps = set()
        self.dma = dma
        self.needed = False
        self.ev = None
        self.slot = None


class Prog:
    def __init__(self, nc, stack):
        self.nc = nc
        self.esem = {e: stack.enter_context(nc.semaphore("s_" + e)) for e in ENGS}
        self.dsem = {}
        for e in ("sp", "act", "pool"):
            for k in range(N_DMA_SLOTS):
                self.dsem[(e, k)] = stack.enter_context(nc.semaphore("d_%s%d" % (e, k)))
        self.bar = stack.enter_context(nc.semaphore("bar"))
        self.cnt = {e: 0 for e in ENGS}
        self.dcnt = {}
        self.dma_count = {e: 0 for e in ENGS}
        self.slot_last = {}
        self.nphase = 0
        self.ops = None
        self.total_ops = 0

    def begin(self):
        self.ops = {e: [] for e in ENGS}

    def op(self, eng, fn, reads=(), writes=(), dma=False):
        o = Op(eng, fn, dma)
        for b in reads:
            if b.w is not None:
                o.deps.add(b.w)
        for b in writes:
            if b.w is not None:
                o.deps.add(b.w)
            for r in b.r:
                o.deps.add(r)
        for b in reads:
            b.r.append(o)
        for b in writes:
            b.w = o
            b.r = []
        if dma:
            k = self.dma_count[eng]
            self.dma_count[eng] = k + 1
            o.slot = (eng, k % N_DMA_SLOTS)
            prev = self.slot_last.get(o.slot)
            if prev is not None:
                o.deps.add(prev)
            self.slot_last[o.slot] = o
            o.needed = True
        o.deps.discard(o)
        self.ops[eng].append(o)
        return o

    def dma(self, eng, out, in_, reads=(), writes=(), **kw):
        return self.op(eng, lambda e: e.dma_start(out=out, in_=in_, **kw), reads, writes, dma=True)

    def end(self):
        nc = self.nc
        ops = self.ops
        for e in ENGS:
            for o in ops[e]:
                for d in o.deps:
                    if not (d.eng == "pe" and e == "pe" and not d.dma):
                        d.needed = True
            if ops[e]:
                ops[e][-1].needed = True
        for e in ENGS:
            for o in ops[e]:
                if o.ev is not None:
                    continue
                if o.dma:
                    v = self.dcnt.get(o.slot, 0) + 16
                    self.dcnt[o.slot] = v
                    o.ev = (self.dsem[o.slot], v)
                elif o.needed:
                    self.cnt[e] += 1
                    o.ev = (self.esem[e], self.cnt[e])
        phase = self.nphase
        bar = self.bar
        tail = []
        for e in ENGS:
            last_c = None
            for o in ops[e]:
                if o.dma:
                    tail.append(o.ev)
                else:
                    last_c = o
            if last_c is not None:
                tail.append(last_c.ev)

        def body(ename):
            elist = ops[ename]

            def run(eng):
                known = {}
                if phase > 0:
                    eng.wait_ge(bar, phase)
                for o in elist:
                    need = {}
                    for d in o.deps:
                        if d.eng == "pe" and ename == "pe" and not d.dma:
                            continue
                        if d.ev is None:
                            continue
                        s, v = d.ev
                        if known.get(s, 0) < v and need.get(s, 0) < v:
                            need[s] = v
                    for s, v in need.items():
                        eng.wait_ge(s, v)
                        known[s] = v
                    ins = o.fn(eng)
                    if o.ev is not None:
                        ins.then_inc(o.ev[0], 16 if o.dma else 1)
                if ename == "sp":
                    need = {}
                    for s, v in tail:
                        if known.get(s, 0) < v and need.get(s, 0) < v:
                            need[s] = v
                    for s, v in need.items():
                        eng.wait_ge(s, v)
                    eng.sem_inc(bar, 1)

            return run

        with nc.Block() as block:
            block.sync(body("sp"))
            if ops["pe"]:
                block.tensor(body("pe"))
            if ops["act"]:
                block.scalar(body("act"))
            if ops["dve"]:
                block.vector(body("dve"))
            if ops["pool"]:
                block.gpsimd(body("pool"))
        self.total_ops += sum(len(v) for v in ops.values())
        self.nphase += 1
        self.ops = None


def _chunks(v):
    return np.ascontiguousarray(np.asarray(v, np.float32).reshape(-1, 128).T)


class PPack:
    def __init__(self):
        self.cols = []
        self.idx = {}
        self.n = 0

    def add(self, name, arr128xn):
        a = np.asarray(arr128xn, np.float32)
        if a.ndim == 1:
            a = a[:, None]
        assert a.shape[0] == 128
        self.idx[name] = self.n
        self.cols.append(a)
        self.n += a.shape[1]

    def build(self):
        return np.ascontiguousarray(np.concatenate(self.cols, axis=1))


def rope_perm():
    perm = np.zeros(512, np.int64)
    for h in range(8):
        for j in range(64):
            perm[h * 64 + j] = h * 64 + (j + 32) % 64
    return perm


def attn_masks():
    out = np.zeros((128, 20, 512), np.float32)
    i = np.arange(128)[:, None]
    j = np.arange(512)[None, :]
    for n in range(20):
        dl = 128 * (n - 8) + i - j
        a = np.abs(dl)
        m = (a <= 64).astype(np.float32) + ((dl % 4 == 0) & (a <= 256)) + ((dl % 16 == 0) & (a <= 1024))
        out[:, n, :] = m
    return out.reshape(128, 20 * 512)


def gla_consts():
    r = np.arange(128)[:, None]
    t = np.arange(128)[None, :]
    same = (r // 64) == (t // 64)
    sc = -1.0 / 16.0
    a_pi = np.where(same & (r <= t), sc, 0.0)
    a_si = np.where(same & (r >= t), sc, 0.0)
    a_se = np.where(same & (r > t), sc, 0.0)
    a_pe = np.where(same & (r < t), sc, 0.0)
    mf = np.where(same & (r <= t), 1.0, 0.0)
    mb = np.where(same & (r > t), 1.0, 0.0)
    ch = np.zeros((128, 2)); ch[:64, 0] = sc; ch[64:, 1] = sc
    return np.concatenate([a_pi, a_si, a_se, a_pe, mf, mb, ch], axis=1).astype(np.float32)


class KB:
    def __init__(self, ppidx, npp, dbg=(), stop_after=None):
        self.ppidx = ppidx
        self.npp = npp
        self.dbg = set(dbg)
        self.stop_after = stop_after
        self.nc = bass.Bass("TRN2", target_bir_lowering=False)
        self.din = {}
        self.scr = {}
        self.sbuf = {}
        self.uid = 0

    def inp(self, name, shape, dt=F32):
        self.din[name] = self.nc.dram_tensor(name, list(shape), dt, kind="ExternalInput").ap()
        return self.din[name]

    def scratch(self, name, shape, dt):
        kind = "ExternalOutput" if name in self.dbg else "Internal"
        t = self.nc.dram_tensor(name, list(shape), dt, kind=kind).ap()
        self.scr[name] = (t, Buf())
        return t

    def sb(self, st, name, shape, dt):
        self.uid += 1
        return st.enter_context(self.nc.sbuf_tensor("%s_u%d" % (name, self.uid), list(shape), dt))

    def ps(self, st, name, shape=(128, 512), dt=F32):
        self.uid += 1
        return st.enter_context(self.nc.psum_tensor("%s_u%d" % (name, self.uid), list(shape), dt))

    def pcol(self, name, off=0):
        i = self.ppidx[name] + off
        return self.pp[:, i:i + 1]


def build_program(ppidx, npp, dbg=(), stop_after=None):
    K = KB(ppidx, npp, dbg, stop_after)
    nc = K.nc
    xT = K.inp("xT", [8, 128, T])
    memT = K.inp("memT", [8, 128, NS * NMEM])
    pos = K.inp("pos", [1, T], I32)
    ppd = K.inp("pp", [128, npp])
    maskd = K.inp("amask", [128, 20 * 512])
    glac = K.inp("glac", [128, 770])
    identd = K.inp("ident", [128, 128])
    w_in_ab = K.inp("w_in_ab", [2, D, 3104])
    w_qkperm = K.inp("w_qkperm", [2, D, 1024])
    w_out_ab = K.inp("w_out_ab", [2, D, D])
    w2blk_d = K.inp("w2blk", [2, 33, 512])
    w_in_cd = K.inp("w_in_cd", [2, D, 1536])
    w_out_cd = K.inp("w_out_cd", [2, D, D])
    w_xq = K.inp("w_xq", [DEPTH, D, D])
    w_xkv = K.inp("w_xkv", [DEPTH, D, 2 * D])
    w_xo = K.inp("w_xo", [DEPTH, D, D])
    w_up = K.inp("w_up", [DEPTH, D, 2 * DFF])
    w_down = K.inp("w_down", [DEPTH, DFF, D])
    w_glu = K.inp("w_glu", [2, 512, 512])
    s5rows = K.inp("s5rows", [2, 3, 32, 128])
    s5lane = K.inp("s5lane", [2, 128, 96])
    s5B = K.inp("s5B", [2, 2, 2, 128, 2048])
    s5C = K.inp("s5C", [2, 2, 2, 128, 2048])
    s5idx = K.inp("s5idx", [1, 96])
    out_d = nc.dram_tensor("outT", [8, 128, T], F32, kind="ExternalOutput").ap()

    Hs = [K.scratch("H0", [8, 128, T], F32), K.scratch("H1", [8, 128, T], F32)]
    bHs = [K.scr["H0"][1], K.scr["H1"][1]]
    hsel = [0]
    COS = K.scratch("COS", [128, T], F32)
    SINS = K.scratch("SINS", [128, T], F32)
    MIXO = K.scratch("MIXO", [8, 128, T], BF16)
    QA = K.scratch("QA", [4, 128, T], BF16)
    KA = K.scratch("KA", [4, 128, T], BF16)
    VA = K.scratch("VA", [T, 512], BF16)
    QB = K.scratch("QB", [2, 128, T], F32)
    KBf = K.scratch("KBF", [2, 128, T], F32)
    KBT = K.scratch("KBT", [T, 256], F32)
    VB = K.scratch("VB", [T, 512], BF16)
    RB = K.scratch("RB", [4, 128, T], BF16)
    LG = K.scratch("LG", [T, 512], F32)
    UC = K.scratch("UC", [4, 128, T], F32)
    UCB = K.scratch("UCB", [4, 128, T], BF16)
    DV = K.scratch("DV", [4, 128, T], F32)
    XQ = K.scratch("XQ", [8, 128, T], BF16)
    XK = K.scratch("XK", [8, 128, NS * NMEM], BF16)
    XV = K.scratch("XV", [NS * NMEM, D], BF16)
    HID = K.scratch("HID", [22, 128, T], BF16)
    bXT = Buf()

    def B(name):
        return K.scr[name][1]

    with ExitStack() as top:
        P = Prog(nc, top)
        K.P = P
        pp = K.sb(top, "pp", [128, npp], F32); K.pp = pp; bpp = Buf()
        ones_bf = K.sb(top, "ones_bf", [128, 128], BF16)
        onesD = K.sb(top, "onesD", [128, 128], BF16)
        ones128 = K.sb(top, "ones128", [128, 128], BF16)
        ones512f = K.sb(top, "ones512f", [128, 128], F32)
        bconst = Buf()

        stop = [False]

        def phase_done(name):
            P.end()
            if K.stop_after == name:
                stop[0] = True
            return stop[0]

        P.begin()
        with ExitStack() as st:
            P.dma("sp", pp[:], ppd, writes=[bpp])
            P.op("dve", lambda e: e.memset(ones_bf[:], 1.0), [], [bconst])
            P.op("dve", lambda e: e.memset(onesD[:], 1.0 / 1024.0), [], [bconst])
            P.op("dve", lambda e: e.memset(ones128[:], 1.0 / 128.0), [], [bconst])
            P.op("dve", lambda e: e.memset(ones512f[:], 1.0 / 512.0), [], [bconst])
            posi = K.sb(st, "posi", [128, T], I32); bposi = Buf()
            ang = K.sb(st, "ang", [128, T], F32); bang = Buf()
            ki = K.sb(st, "ki", [128, T], I32); bki = Buf()
            kf = K.sb(st, "kf", [128, T], F32); bkf = Buf()
            tb_ = K.sb(st, "ttab", [128, T], F32); btab = Buf()
            invf = K.sb(st, "invf", [128, 1], F32); binvf = Buf()
            P.dma("sp", posi[:], pos.partition_broadcast(128), writes=[bposi])
            P.op("dve", lambda e: e.tensor_copy(out=invf[:], in_=K.pcol("invfreq")), [bpp], [binvf])
            P.op("dve", lambda e: e.tensor_copy(out=ang[:], in_=posi[:]), [bposi], [bang])
            P.op("dve", lambda e: e.tensor_scalar(out=ang[:], in0=ang[:], scalar1=invf[:, 0:1], scalar2=None,
                                                  op0=ALU.mult), [bang, binvf], [bang])
            C1 = 6.28125
            C2 = TWO_PI - C1
            P.op("dve", lambda e: e.tensor_scalar(out=ki[:], in0=ang[:], scalar1=1.0 / TWO_PI, scalar2=None,
                                                  op0=ALU.mult), [bang], [bki])
            P.op("dve", lambda e: e.tensor_copy(out=kf[:], in_=ki[:]), [bki], [bkf])
            P.op("dve", lambda e: e.scalar_tensor_tensor(out=ang[:], in0=kf[:], scalar=-C1, in1=ang[:],
                                                         op0=ALU.mult, op1=ALU.add), [bkf, bang], [bang])
            P.op("dve", lambda e: e.scalar_tensor_tensor(out=ang[:], in0=kf[:], scalar=-C2, in1=ang[:],
                                                         op0=ALU.mult, op1=ALU.add), [bkf, bang], [bang])
            PI_S = 3.1415925
            P.op("dve", lambda e: e.tensor_scalar(out=ang[:], in0=ang[:], scalar1=-PI_S, scalar2=PI_S,
                                                  op0=ALU.max, op1=ALU.min), [bang], [bang])
            P.op("act", lambda e: e.activation(out=tb_[:], in_=ang[:], func=AF.Sin), [bang], [btab])
            P.op("dve", lambda e: e.tensor_scalar(out=tb_[:], in0=tb_[:], scalar1=K.pcol("ropesign"), scalar2=None,
                                                  op0=ALU.mult), [btab, bpp], [btab])
            P.dma("act", SINS, tb_[:], reads=[btab], writes=[B("SINS")])
            P.op("dve", lambda e: e.tensor_single_scalar(out=kf[:], in_=ang[:], scalar=math.pi / 2.0, op=ALU.is_gt),
                 [bang], [bkf])
            P.op("dve", lambda e: e.scalar_tensor_tensor(out=ang[:], in0=kf[:], scalar=-TWO_PI, in1=ang[:],
                                                         op0=ALU.mult, op1=ALU.add), [bkf, bang], [bang])
            P.op("dve", lambda e: e.tensor_scalar(out=ang[:], in0=ang[:], scalar1=math.pi / 2.0, scalar2=PI_S,
                                                  op0=ALU.add, op1=ALU.min), [bang], [bang])
            tb2 = K.sb(st, "ttab2", [128, T], F32); btab2 = Buf()
            P.op("act", lambda e: e.activation(out=tb2[:], in_=ang[:], func=AF.Sin), [bang], [btab2])
            P.dma("act", COS, tb2[:], reads=[btab2], writes=[B("COS")])
            if phase_done("const"):
                return K

        def emit_norm(st, src, bsrc, gname, hn, bhn, out_f32=None):
            ht = [K.sb(st, "n_ht%d" % i, [128, 8, 512], F32) for i in range(2)]
            bht = [Buf(), Buf()]
            sq = [K.sb(st, "n_sq%d" % i, [128, 512], BF16) for i in range(2)]
            bsq = [Buf(), Buf()]
            sd = K.sb(st, "n_sd", [128, 512], F32); bsd = Buf()
            rs = K.sb(st, "n_rs", [128, 512], F32); brs = Buf()
            pst = K.ps(st, "n_ps"); bps = Buf()
            ost = None
            if out_f32 is not None:
                ost = [K.sb(st, "n_o%d" % i, [128, 512], F32) for i in range(2)]
                bost = [Buf(), Buf()]
            ntok = src.shape[2]
            for tb in range(ntok // 512):
                sl = slice(tb * 512, (tb + 1) * 512)
                h_ = ht[tb % 2]; bh_ = bht[tb % 2]
                P.dma("sp", h_[:], src[:, :, sl].rearrange("c p t -> p c t"), reads=[bsrc], writes=[bh_])
                for c in range(8):
                    s_ = sq[c % 2]; bs_ = bsq[c % 2]
                    P.op("act", lambda e, s_=s_, h_=h_, c=c: e.activation(out=s_[:], in_=h_[:, c, :], func=AF.Square),
                         [bh_], [bs_])
                    P.op("pe", lambda e, s_=s_, c=c: e.matmul(pst[:], lhsT=onesD[:], rhs=s_[:], start=(c == 0), stop=(c == 7)),
                         [bs_, bconst], [bps])
                P.op("act", lambda e: e.activation(out=sd[:], in_=pst[:], func=AF.Sqrt, bias=EPS), [bps], [bsd])
                P.op("dve", lambda e: e.reciprocal(out=rs[:], in_=sd[:]), [bsd], [brs])
                for c in range(8):
                    if out_f32 is None:
                        P.op("dve", lambda e, h_=h_, c=c, sl=sl: e.scalar_tensor_tensor(
                            out=hn[:, c, sl], in0=h_[:, c, :], scalar=K.pcol(gname, c), in1=rs[:],
                            op0=ALU.mult, op1=ALU.mult), [bh_, brs, bpp], [bhn[tb] if isinstance(bhn, list) else bhn])
                    else:
                        o_ = ost[c % 2]; bo_ = bost[c % 2]
                        P.op("dve", lambda e, h_=h_, c=c, o_=o_: e.scalar_tensor_tensor(
                            out=o_[:], in0=h_[:, c, :], scalar=K.pcol(gname, c), in1=rs[:],
                            op0=ALU.mult, op1=ALU.mult), [bh_, brs, bpp], [bo_])
                        P.dma("act", out_f32[c, :, sl], o_[:], reads=[bo_], writes=[])

        class Lin:
            def __init__(self, st, KC, npair=1, tag="l"):
                self.KC = KC
                self.npair = npair
                self.wt = [[K.sb(st, "%s_w%d_%d" % (tag, i, j), [128, KC, 128], BF16) for j in range(npair)] for i in range(2)]
                self.bw = [[Buf() for j in range(npair)] for i in range(2)]
                self.pst = [[K.ps(st, "%s_p%d_%d" % (tag, i, j)) for j in range(npair)] for i in range(2)]
                self.bp = [[Buf() for j in range(npair)] for i in range(2)]
                self.it = 0
                self.pit = 0

            def run(self, xin, bxin, ntok, wspecs, epilogue, M=128):
                wi = self.it % 2
                self.it += 1
                for j, (W, c0) in enumerate(wspecs):
                    src = W.rearrange("(kc p) f -> p kc f", p=128)[:, :, c0:c0 + M]
                    P.dma("pool", self.wt[wi][j][:, :, 0:M], src, writes=[self.bw[wi][j]])
                for tb in range(ntok // 512):
                    sl = slice(tb * 512, (tb + 1) * 512)
                    pi = self.pit % 2
                    self.pit += 1
                    for j in range(len(wspecs)):
                        wt = self.wt[wi][j]; ps_ = self.pst[pi][j]
                        for kc in range(self.KC):
                            P.op("pe", lambda e, wt=wt, ps_=ps_, kc=kc, sl=sl: e.matmul(
                                ps_[0:M, :], lhsT=wt[:, kc, 0:M], rhs=xin[:, kc, sl], start=(kc == 0), stop=(kc == self.KC - 1)),
                                [self.bw[wi][j], (bxin[tb] if isinstance(bxin, list) else bxin)], [self.bp[pi][j]])
                    epilogue(tb, sl, self.pst[pi], self.bp[pi])

        class Stage:
            def __init__(self, st, name, shape, dt, n=3):
                self.t = [K.sb(st, "%s%d" % (name, i), shape, dt) for i in range(n)]
                self.b = [Buf() for _ in range(n)]
                self.i = 0

            def next(self):
                k = self.i % len(self.t)
                self.i += 1
                return self.t[k], self.b[k]

        def store(dst, src_ap, bsrc, bdst, eng="act"):
            P.dma(eng, dst, src_ap, reads=[bsrc], writes=[bdst])

        def out_proj_phase(W, residual_src, bres_src, KC=8, xsrc=None, bxsrc=None):
            H = Hs[hsel[0]]; bH = bHs[hsel[0]]
            hsel[0] ^= 1
            P.begin()
            with ExitStack() as st:
                xin = K.sb(st, "op_x", [128, 8, T], BF16); bx = [Buf() for _ in range(8)]
                for tb0 in range(8):
                    sl0 = slice(tb0 * 512, (tb0 + 1) * 512)
                    P.dma("sp", xin[:, :, sl0], MIXO[:, :, sl0].rearrange("c p t -> p c t"), reads=[B("MIXO")], writes=[bx[tb0]])
                lin = Lin(st, 8, 1, "op")
                hst = Stage(st, "op_h", [128, 512], F32, 3)
                ost = Stage(st, "op_o", [128, 512], F32, 3)
                for fc in range(8):
                    def epi(tb, sl, pss, bps, fc=fc):
                        ht, bht = hst.next()
                        P.dma("sp", ht[:], residual_src[fc, :, sl], reads=[bres_src], writes=[bht])
                        ot, bot = ost.next()
                        P.op("dve", lambda e, ot=ot, ht=ht, p_=pss[0]: e.tensor_tensor(out=ot[:], in0=p_[:], in1=ht[:], op=ALU.add),
                             [bps[0], bht], [bot])
                        store(H[fc, :, sl], ot[:], bot, bH)
                    lin.run(xin, bx, T, [(W, fc * 128)], epi)
            return H, bH


        def xattn_phases(layer, cur_src, cur_bsrc):
            P.begin()
            with ExitStack() as st:
                hn = K.sb(st, "x_hn", [128, 8, T], BF16); bhn = [Buf() for _ in range(8)]
                emit_norm(st, cur_src, cur_bsrc, "g_xattn%d" % layer, hn, bhn)
                with ExitStack() as st2:
                    lin = Lin(st2, 8, 1, "x1")
                    obs = Stage(st2, "x1_ob", [128, 512], BF16, 3)
                    for j in range(8):
                        def epi(tb, sl, pss, bps, j=j):
                            ob, bob = obs.next()
                            P.op("act", lambda e: e.activation(out=ob[:], in_=pss[0][:], func=AF.Copy), [bps[0]], [bob])
                            store(XQ[j, :, sl], ob[:], bob, B("XQ"))
                        lin.run(hn, bhn, T, [(w_xq[layer], j * 128)], epi)
            P.end(); P.begin()
            NM = NS * NMEM
            with ExitStack() as st:
                mn = K.sb(st, "x_mn", [128, 8, NM], BF16); bmn = Buf()
                bmem = Buf()
                emit_norm(st, memT, bmem, "g_mem%d" % layer, mn, bmn)
                lin = Lin(st, 8, 1, "x1k")
                obs = Stage(st, "x1k_ob", [128, 512], BF16, 3)
                for j in range(8):
                    def epi(tb, sl, pss, bps, j=j):
                        ob, bob = obs.next()
                        P.op("act", lambda e: e.activation(out=ob[:], in_=pss[0][:], func=AF.Copy), [bps[0]], [bob])
                        store(XK[j, :, sl], ob[:], bob, B("XK"))
                    lin.run(mn, bmn, NM, [(w_xkv[layer], j * 128)], epi)
                wr = K.sb(st, "x1_wr", [128, 8, 512], BF16); bwr = Buf()
                pst = [K.ps(st, "x1_tp%d" % i) for i in range(2)]; bpt = [Buf(), Buf()]
                it = [0]
                for half in range(2):
                    P.dma("pool", wr[:], w_xkv[layer].rearrange("(kc p) f -> p kc f", p=128)[:, :, D + half * 512: D + (half + 1) * 512], writes=[bwr])
                    for blk in range(NM // 128):
                        def body(half=half, blk=blk):
                            tsl = slice(blk * 128, (blk + 1) * 128)
                            ps_ = pst[it[0] % 2]; bp_ = bpt[it[0] % 2]; it[0] += 1
                            for kc in range(8):
                                P.op("pe", lambda e, kc=kc: e.matmul(ps_[:], lhsT=mn[:, kc, tsl], rhs=wr[:, kc, :], start=(kc == 0), stop=(kc == 7)),
                                     [bmn, bwr], [bp_])
                            ob, bob = obs.next()
                            P.op("act", lambda e: e.activation(out=ob[:], in_=ps_[:], func=AF.Copy), [bp_], [bob])
                            store(XV[tsl, half * 512:(half + 1) * 512], ob[:], bob, B("XV"))
                        body()
            if phase_done("X1_%d" % layer):
                return None
            P.begin()
            with ExitStack() as st:
                xk = K.sb(st, "x2_k", [128, 8, NM], BF16); bxk = Buf()
                xv = K.sb(st, "x2_v", [128, NM // 128, D], BF16); bxv = Buf()
                P.dma("sp", xk[:], XK.rearrange("c p t -> p c t"), reads=[B("XK")], writes=[bxk])
                P.dma("sp", xv[:], XV.rearrange("(b p) f -> p b f", p=128), reads=[B("XV")], writes=[bxv])
                xq = [K.sb(st, "x2_q%d" % i, [128, 2, S], BF16) for i in range(2)]; bxq = [Buf(), Buf()]
                pss = [K.ps(st, "x2_s%d" % i) for i in range(2)]; bpss = [Buf(), Buf()]
                pnum = [[K.ps(st, "x2_n%d_%d" % (i, j)) for j in range(2)] for i in range(2)]
                bpn = [[Buf(), Buf()], [Buf(), Buf()]]
                pden = [K.ps(st, "x2_d%d" % i) for i in range(2)]; bpd = [Buf(), Buf()]
                exs = Stage(st, "x2_ex", [128, 512], BF16, 3)
                rec = K.sb(st, "x2_rec", [128, 512], F32); brec = Buf()
                obs = Stage(st, "x2_ob", [128, 512], BF16, 4)
                cnt = {"q": 0, "s": 0, "a": 0}
                pend_x = []

                def flush_x():
                    while pend_x:
                        d_, o_, bo_ = pend_x.pop(0)
                        store(d_, o_[:], bo_, B("MIXO"))
                for s in range(NS):
                    for h in range(4):
                        def head(s=s, h=h):
                            b_ = cnt["q"] % 2; cnt["q"] += 1
                            for dc in range(2):
                                P.dma("sp", xq[b_][:, dc, :], XQ[2 * h + dc, :, s * S:(s + 1) * S], reads=[B("XQ")], writes=[bxq[b_]])
                            units = [(qb, mb) for qb in range(4) for mb in range(2)]
                            acc_of = {}
                            stx = {}

                            def xA(u):
                                qb, mb = units[u]
                                if mb == 0:
                                    acc_of[qb] = cnt["a"] % 2; cnt["a"] += 1
                                p_ = pss[cnt["s"] % 2]; bp_ = bpss[cnt["s"] % 2]; cnt["s"] += 1
                                qsl = slice(qb * 512, (qb + 1) * 512)
                                msl = slice(s * NMEM + mb * 128, s * NMEM + (mb + 1) * 128)
                                for dc in range(2):
                                    P.op("pe", lambda e, dc=dc: e.matmul(p_[:], lhsT=xk[:, 2 * h + dc, msl], rhs=xq[b_][:, dc, qsl],
                                                                         start=(dc == 0), stop=(dc == 1)), [bxk, bxq[b_]], [bp_])
                                stx[u] = (p_, bp_)

                            def xB(u):
                                p_, bp_ = stx[u]
                                ex, bex = exs.next()
                                P.op("act", lambda e: e.activation(out=ex[:], in_=p_[:], func=AF.Exp, scale=1.0 / 16.0), [bp_], [bex])
                                stx[u] = (ex, bex)

                            def xC(u):
                                qb, mb = units[u]
                                a_ = acc_of[qb]
                                ex, bex = stx[u]
                                for dvc in range(2):
                                    P.op("pe", lambda e, dvc=dvc: e.matmul(pnum[a_][dvc][:], lhsT=xv[:, s * 2 + mb, h * 256 + dvc * 128: h * 256 + (dvc + 1) * 128],
                                                                           rhs=ex[:], start=(mb == 0), stop=(mb == 1)), [bxv, bex], [bpn[a_][dvc]])
                                P.op("pe", lambda e: e.matmul(pden[a_][:], lhsT=ones_bf[:], rhs=ex[:], start=(mb == 0), stop=(mb == 1)),
                                     [bconst, bex], [bpd[a_]])
                                if mb == 1:
                                    P.op("dve", lambda e: e.reciprocal(out=rec[:], in_=pden[a_][:]), [bpd[a_]], [brec])
                                    for dvc in range(2):
                                        ob, bob = obs.next()
                                        P.op("dve", lambda e, dvc=dvc, ob=ob: e.tensor_tensor(out=ob[:], in0=pnum[a_][dvc][:], in1=rec[:], op=ALU.mult),
                                             [bpn[a_][dvc], brec], [bob])
                                        pend_x.append((MIXO[2 * h + dvc, :, s * S + qb * 512: s * S + (qb + 1) * 512], ob, bob))

                            xA(0)
                            for u in range(len(units)):
                                if u + 1 < len(units):
                                    xA(u + 1)
                                xB(u)
                                if u % 2 == 1:
                                    flush_x()
                                xC(u)
                        head()
                flush_x()
            if phase_done("X2_%d" % layer):
                return None
            r_ = out_proj_phase(w_xo[layer], cur_src, cur_bsrc)
            if phase_done("X3_%d" % layer):
                return None
            return r_

        def ffn_phases(layer, cur_src, cur_bsrc):
            P.begin()
            with ExitStack() as st:
                hn = K.sb(st, "f_hn", [128, 8, T], BF16); bhn = Buf()
                with ExitStack() as st2:
                    emit_norm(st2, cur_src, cur_bsrc, "g_ffn%d" % layer, hn, bhn)
                P.end(); P.begin()
                with ExitStack() as st2:
                    lin = Lin(st2, 8, 2, "f1")
                    uV = K.sb(st2, "f_uV", [128, 2, S + 2], F32); buV = [Buf(), Buf()]
                    uG = K.sb(st2, "f_uG", [128, 2, S + 2], F32); buG = [Buf(), Buf()]
                    for t_, bb_ in ((uV, buV), (uG, buG)):
                        for q_ in range(2):
                            P.op("pool", lambda e, t_=t_, q_=q_: e.memset(t_[:, q_, 0:1], 0.0), [], [bb_[q_]])
                            P.op("pool", lambda e, t_=t_, q_=q_: e.memset(t_[:, q_, S + 1:S + 2], 0.0), [], [bb_[q_]])
                    aVs = Stage(st2, "f_aV", [128, S], F32, 2)
                    aGs = Stage(st2, "f_aG", [128, S], F32, 2)
                    sgs_ = Stage(st2, "f_sg", [128, S], F32, 2)
                    hids = Stage(st2, "f_hid", [128, S], BF16, 2)
                    pending = []
                    for c in range(22):
                        def epi(tb, sl, pss, bps, c=c):
                            sq_ = tb // 4
                            o_ = 1 + (tb % 4) * 512
                            P.op("act", lambda e: e.activation(out=uV[:, sq_, o_:o_ + 512], in_=pss[0][:], func=AF.Copy), [bps[0]], [buV[sq_]])
                            P.op("act", lambda e: e.activation(out=uG[:, sq_, o_:o_ + 512], in_=pss[1][:], func=AF.Copy), [bps[1]], [buG[sq_]])
                            if tb % 4 != 3:
                                return
                            while pending:
                                pending.pop(0)()
                            aV, baV = aVs.next(); aG, baG = aGs.next(); sg, bsg = sgs_.next()
                            for (u_, bu_, a_, ba_, ci) in ((uV, buV[sq_], aV, baV, c), (uG, buG[sq_], aG, baG, 22 + c)):
                                w0 = K.pcol("w_conv_ffn%d_0" % layer, ci); w1 = K.pcol("w_conv_ffn%d_1" % layer, ci)
                                w2 = K.pcol("w_conv_ffn%d_2" % layer, ci); bc = K.pcol("b_conv_ffn%d" % layer, ci)
                                P.op("act", lambda e, u_=u_, a_=a_, w0=w0, bc=bc: e.activation(out=a_[:], in_=u_[:, sq_, 0:S], func=AF.Identity, bias=bc, scale=w0),
                                     [bu_, bpp], [ba_])
                                P.op("dve", lambda e, u_=u_, a_=a_, w1=w1: e.scalar_tensor_tensor(out=a_[:], in0=u_[:, sq_, 1:S + 1], scalar=w1, in1=a_[:],
                                                                                                   op0=ALU.mult, op1=ALU.add), [bu_, bpp, ba_], [ba_])
                                P.op("dve", lambda e, u_=u_, a_=a_, w2=w2: e.scalar_tensor_tensor(out=a_[:], in0=u_[:, sq_, 2:S + 2], scalar=w2, in1=a_[:],
                                                                                                   op0=ALU.mult, op1=ALU.add), [bu_, bpp, ba_], [ba_])
                            def tail(aV=aV, baV=baV, aG=aG, baG=baG, sg=sg, bsg=bsg, c=c, sq_=sq_):
                                P.op("act", lambda e: e.activation(out=sg[:], in_=aG[:], func=AF.Silu), [baG], [bsg])
                                hd, bhd = hids.next()
                                P.op("dve", lambda e: e.tensor_tensor(out=hd[:], in0=aV[:], in1=sg[:], op=ALU.mult), [baV, bsg], [bhd])
                                store(HID[c, :, sq_ * S:(sq_ + 1) * S], hd[:], bhd, B("HID"))
                            pending.append(tail)
                        lin.run(hn, bhn, T, [(w_up[layer], c * 128), (w_up[layer], DFF + c * 128)], epi)
                    while pending:
                        pending.pop(0)()
            if phase_done("F1_%d" % layer):
                return None
            Hn = Hs[hsel[0]]; bHn = bHs[hsel[0]]
            hsel[0] ^= 1
            P.begin()
            with ExitStack() as st:
                xin = K.sb(st, "f2_x", [128, 22, S], BF16); bx = Buf()
                lin = Lin(st, 22, 1, "f2")
                hst = Stage(st, "f2_h", [128, 512], F32, 3)
                ost = Stage(st, "f2_o", [128, 512], F32, 3)
                for s in range(NS):
                    P.dma("sp", xin[:], HID[:, :, s * S:(s + 1) * S].rearrange("c p t -> p c t"), reads=[B("HID")], writes=[bx])
                    for fc in range(8):
                        def epi(tb, sl, pss, bps, fc=fc, s=s):
                            gsl = slice(s * S + tb * 512, s * S + (tb + 1) * 512)
                            ht, bht = hst.next()
                            P.dma("sp", ht[:], cur_src[fc, :, gsl], reads=[cur_bsrc], writes=[bht])
                            ot, bot = ost.next()
                            P.op("dve", lambda e: e.tensor_tensor(out=ot[:], in0=pss[0][:], in1=ht[:], op=ALU.add), [bps[0], bht], [bot])
                            store(Hn[fc, :, gsl], ot[:], bot, bHn)
                        lin.run(xin, bx, S, [(w_down[layer], fc * 128)], epi)
            if phase_done("F2_%d" % layer):
                return None
            return Hn, bHn


        C1_ = 6.28125
        C2_ = TWO_PI - C1_
        PI_S_ = 3.1415925

        def sincos(x, bx, ki, bki, kf, bkf, sn, bsn, cs, bcs, sl):
            P.op("dve", lambda e: e.tensor_scalar(out=ki[sl], in0=x[sl], scalar1=1.0 / TWO_PI, scalar2=None, op0=ALU.mult), [bx], [bki])
            P.op("dve", lambda e: e.tensor_copy(out=kf[sl], in_=ki[sl]), [bki], [bkf])
            P.op("dve", lambda e: e.scalar_tensor_tensor(out=x[sl], in0=kf[sl], scalar=-C1_, in1=x[sl], op0=ALU.mult, op1=ALU.add), [bkf, bx], [bx])
            P.op("dve", lambda e: e.scalar_tensor_tensor(out=x[sl], in0=kf[sl], scalar=-C2_, in1=x[sl], op0=ALU.mult, op1=ALU.add), [bkf, bx], [bx])
            P.op("dve", lambda e: e.tensor_scalar(out=x[sl], in0=x[sl], scalar1=-PI_S_, scalar2=PI_S_, op0=ALU.max, op1=ALU.min), [bx], [bx])
            P.op("act", lambda e: e.activation(out=sn[sl], in_=x[sl], func=AF.Sin), [bx], [bsn])
            P.op("dve", lambda e: e.tensor_single_scalar(out=kf[sl], in_=x[sl], scalar=math.pi / 2.0, op=ALU.is_gt), [bx], [bkf])
            P.op("dve", lambda e: e.scalar_tensor_tensor(out=x[sl], in0=kf[sl], scalar=-TWO_PI, in1=x[sl], op0=ALU.mult, op1=ALU.add), [bkf, bx], [bx])
            P.op("dve", lambda e: e.tensor_scalar(out=x[sl], in0=x[sl], scalar1=math.pi / 2.0, scalar2=PI_S_, op0=ALU.add, op1=ALU.min), [bx], [bx])
            P.op("act", lambda e: e.activation(out=cs[sl], in_=x[sl], func=AF.Sin), [bx], [bcs])

        def odd_phases(layer, li, cur_src, cur_bsrc):
            Wcd = w_in_cd[li]
            P.begin()
            with ExitStack() as st:
              if "S5_NOO1" not in K.dbg:
                  hn = K.sb(st, "o_hn", [128, 8, T], BF16); bhn = [Buf() for _ in range(8)]
                  emit_norm(st, cur_src, cur_bsrc, "g_mix%d" % layer, hn, bhn)
                  with ExitStack() as st2:
                      lin1 = Lin(st2, 8, 1, "o1a")
                      ofs = Stage(st2, "o1_of", [128, 512], F32, 3)
                      obs = Stage(st2, "o1_ob", [128, 512], BF16, 3)
                      for j in range(4):
                          def epi(tb, sl, pss, bps, j=j):
                              ot, bot = ofs.next()
                              P.op("act", lambda e: e.activation(out=ot[:], in_=pss[0][:], func=AF.Copy), [bps[0]], [bot])
                              store(UC[j, :, sl], ot[:], bot, B("UC"), eng="sp")
                              ob, bob = obs.next()
                              P.op("pool", lambda e: e.tensor_copy(out=ob[:], in_=ot[:]), [bot], [bob])
                              store(UCB[j, :, sl], ob[:], bob, B("UCB"), eng="sp")
                          if "S5_NOO1A" not in K.dbg:
                              lin1.run(hn, bhn, T, [(Wcd, j * 128)], epi)
                  P.end(); P.begin()
                  with ExitStack() as st2:
                      lin2 = Lin(st2, 8, 2, "o1b")
                      sgs = Stage(st2, "o1_sg", [128, 512], F32, 2)
                      ofs = Stage(st2, "o1_of2", [128, 512], F32, 3)
                      for j in range(4):
                          def epi(tb, sl, pss, bps, j=j):
                              sg, bsg = sgs.next()
                              P.op("act", lambda e: e.activation(out=sg[:], in_=pss[1][:], func=AF.Sigmoid), [bps[1]], [bsg])
                              ot, bot = ofs.next()
                              P.op("dve", lambda e: e.tensor_tensor(out=ot[:], in0=pss[0][:], in1=sg[:], op=ALU.mult), [bps[0], bsg], [bot])
                              store(DV[j, :, sl], ot[:], bot, B("DV"), eng="sp")
                          if "S5_NOO1B" not in K.dbg:
                              lin2.run(hn, bhn, T, [(Wcd, 512 + j * 128), (Wcd, 1024 + j * 128)], epi)
            if phase_done("O1_%d" % layer):
                return None
            P.begin()
            with ExitStack() as st:
                PB = K.sb(st, "s5_PB", [128, 32, 2, 128], BF16); bPB = Buf()
                PC = K.sb(st, "s5_PC", [128, 32, 2, 128], BF16); bPC = Buf()
                rl = K.sb(st, "s5_rl", [128, 32], F32); brl = Buf()
                thl = K.sb(st, "s5_thl", [128, 32], F32); bthl = Buf()
                zb = K.sb(st, "s5_zb", [128, 4, T], BF16); bzb = Buf()
                with ExitStack() as st2:
                    NH = 16 * 128
                    tA = K.sb(st2, "s5_A", [128, NH], F32); bA = Buf()
                    tB = K.sb(st2, "s5_B", [128, NH], F32); bB = Buf()
                    tC = K.sb(st2, "s5_C", [128, NH], F32); bC = Buf()
                    tD = K.sb(st2, "s5_D", [128, NH], F32); bD = Buf()
                    tE = K.sb(st2, "s5_E", [128, NH], F32); bE = Buf()
                    tF = K.sb(st2, "s5_F", [128, NH], F32); bF = Buf()
                    tG = K.sb(st2, "s5_G", [128, NH], F32); bG = Buf()
                    tH = K.sb(st2, "s5_H", [128, NH], F32); bHh = Buf()
                    tI = K.sb(st2, "s5_I", [128, NH], I32); bI = Buf()
                    al = (slice(None), slice(None))
                    for half in range(2):
                        def prep(half=half):
                            q0 = half * 16
                            rows = s5rows[li]
                            def brow(k):
                                return rows[k, q0:q0 + 16, :].rearrange("q l -> (q l)").unsqueeze(0).partition_broadcast(128)
                            P.dma("sp", tA[:], brow(0), writes=[bA])
                            P.dma("sp", tB[:], brow(1), writes=[bB])
                            P.dma("sp", tC[:], brow(2), writes=[bC])
                            P.op("act", lambda e: e.activation(out=tC[:], in_=tC[:], func=AF.Exp), [bC], [bC])
                            P.op("dve", lambda e: e.tensor_tensor(out=tD[:], in0=tA[:], in1=tC[:], op=ALU.mult), [bA, bC], [bD])
                            P.op("dve", lambda e: e.tensor_tensor(out=tE[:], in0=tB[:], in1=tC[:], op=ALU.mult), [bB, bC], [bE])
                            P.op("act", lambda e: e.activation(out=tD[:], in_=tD[:], func=AF.Exp), [bD], [bD])
                            sincos(tE, bE, tI, bI, tH, bHh, tF, bF, tG, bG, al)
                            P.op("dve", lambda e: e.tensor_tensor(out=tG[:], in0=tD[:], in1=tG[:], op=ALU.mult), [bD, bG], [bG])
                            P.op("dve", lambda e: e.tensor_tensor(out=tF[:], in0=tD[:], in1=tF[:], op=ALU.mult), [bD, bF], [bF])
                            P.op("dve", lambda e: e.tensor_tensor(out=tD[:], in0=tA[:], in1=tA[:], op=ALU.mult), [bA], [bD])
                            P.op("dve", lambda e: e.tensor_tensor(out=tE[:], in0=tB[:], in1=tB[:], op=ALU.mult), [bB], [bE])
                            P.op("dve", lambda e: e.tensor_tensor(out=tD[:], in0=tD[:], in1=tE[:], op=ALU.add), [bD, bE], [bD])
                            P.op("dve", lambda e: e.reciprocal(out=tD[:], in_=tD[:]), [bD], [bD])
                            P.op("dve", lambda e: e.tensor_scalar(out=tG[:], in0=tG[:], scalar1=-1.0, scalar2=None, op0=ALU.add), [bG], [bG])
                            P.op("dve", lambda e: e.tensor_tensor(out=tE[:], in0=tG[:], in1=tA[:], op=ALU.mult), [bG, bA], [bE])
                            P.op("dve", lambda e: e.tensor_tensor(out=tH[:], in0=tF[:], in1=tB[:], op=ALU.mult), [bF, bB], [bHh])
                            P.op("dve", lambda e: e.tensor_tensor(out=tE[:], in0=tE[:], in1=tH[:], op=ALU.add), [bE, bHh], [bE])
                            P.op("dve", lambda e: e.tensor_tensor(out=tE[:], in0=tE[:], in1=tD[:], op=ALU.mult), [bE, bD], [bE])
                            P.op("dve", lambda e: e.tensor_tensor(out=tH[:], in0=tF[:], in1=tA[:], op=ALU.mult), [bF, bA], [bHh])
                            P.op("dve", lambda e: e.tensor_tensor(out=tG[:], in0=tG[:], in1=tB[:], op=ALU.mult), [bG, bB], [bG])
                            P.op("dve", lambda e: e.tensor_tensor(out=tH[:], in0=tH[:], in1=tG[:], op=ALU.subtract), [bHh, bG], [bHh])
                            P.op("dve", lambda e: e.tensor_tensor(out=tH[:], in0=tH[:], in1=tD[:], op=ALU.mult), [bHh, bD], [bHh])
                            P.dma("sp", tA[:], s5B[li, 0, half], reads=[], writes=[bA])
                            P.dma("sp", tB[:], s5B[li, 1, half], reads=[], writes=[bB])
                            P.op("dve", lambda e: e.tensor_tensor(out=tG[:], in0=tE[:], in1=tA[:], op=ALU.mult), [bE, bA], [bG])
                            P.op("dve", lambda e: e.tensor_tensor(out=tF[:], in0=tHh_[:], in1=tB[:], op=ALU.mult), [bHh, bB], [bF])
                            P.op("dve", lambda e: e.tensor_tensor(out=PB[:, q0:q0 + 16, 0, :], in0=tG[:].rearrange("p (q l) -> p q l", q=16),
                                                                  in1=tF[:].rearrange("p (q l) -> p q l", q=16), op=ALU.subtract), [bG, bF], [bPB])
                            P.op("dve", lambda e: e.tensor_tensor(out=tG[:], in0=tE[:], in1=tB[:], op=ALU.mult), [bE, bB], [bG])
                            P.op("dve", lambda e: e.tensor_tensor(out=tF[:], in0=tHh_[:], in1=tA[:], op=ALU.mult), [bHh, bA], [bF])
                            P.op("dve", lambda e: e.tensor_tensor(out=PB[:, q0:q0 + 16, 1, :], in0=tG[:].rearrange("p (q l) -> p q l", q=16),
                                                                  in1=tF[:].rearrange("p (q l) -> p q l", q=16), op=ALU.add), [bG, bF], [bPB])
                            P.dma("sp", tC[:], s5C[li, 0, half], reads=[], writes=[bC])
                            P.dma("sp", tD[:], s5C[li, 1, half], reads=[], writes=[bD])
                            P.op("act", lambda e: e.activation(out=PC[:, q0:q0 + 16, 0, :], in_=tC[:].rearrange("p (q l) -> p q l", q=16), func=AF.Copy), [bC], [bPC])
                            P.op("act", lambda e: e.activation(out=PC[:, q0:q0 + 16, 1, :], in_=tD[:].rearrange("p (q l) -> p q l", q=16), func=AF.Copy, scale=-1.0), [bD], [bPC])
                        tHh_ = tH
                        if "S5_NOROW" not in K.dbg:
                            prep()
                    ln_ = K.sb(st2, "s5_ln", [128, 96], F32); bln = Buf()
                    lki = K.sb(st2, "s5_lki", [128, 32], I32); blki = Buf()
                    lkf = K.sb(st2, "s5_lkf", [128, 32], F32); blkf = Buf()
                    if "S5_NOLANE" in K.dbg:
                        P.op("dve", lambda e: e.memset(ln_[:], 0.0), [], [bln])
                    else:
                        P.dma("sp", ln_[:], s5lane[li], writes=[bln])
                    P.op("act", lambda e: e.activation(out=ln_[:, 64:96], in_=ln_[:, 64:96], func=AF.Exp), [bln], [bln])
                    P.op("dve", lambda e: e.tensor_tensor(out=rl[:], in0=ln_[:, 0:32], in1=ln_[:, 64:96], op=ALU.mult), [bln], [brl])
                    P.op("act", lambda e: e.activation(out=rl[:], in_=rl[:], func=AF.Exp), [brl], [brl])
                    P.op("dve", lambda e: e.tensor_tensor(out=thl[:], in0=ln_[:, 32:64], in1=ln_[:, 64:96], op=ALU.mult), [bln], [bthl])
                    P.op("dve", lambda e: e.tensor_scalar(out=lki[:], in0=thl[:], scalar1=1.0 / TWO_PI, scalar2=None, op0=ALU.mult), [bthl], [blki])
                    P.op("dve", lambda e: e.tensor_copy(out=lkf[:], in_=lki[:]), [blki], [blkf])
                    P.op("dve", lambda e: e.scalar_tensor_tensor(out=thl[:], in0=lkf[:], scalar=-C1_, in1=thl[:], op0=ALU.mult, op1=ALU.add), [blkf, bthl], [bthl])
                    P.op("dve", lambda e: e.scalar_tensor_tensor(out=thl[:], in0=lkf[:], scalar=-C2_, in1=thl[:], op0=ALU.mult, op1=ALU.add), [blkf, bthl], [bthl])
                    if "DBGS5P" in K.dbg:
                        dpb = K.scratch("DBGPB", [128, 32 * 2 * 128], BF16)
                        P.dma("act", dpb, PB[:].rearrange("p a b c -> p (a b c)"), reads=[bPB], writes=[B("DBGPB")])
                        dpl = K.scratch("DBGRL", [128, 64], F32)
                        P.dma("act", dpl[:, 0:32], rl[:], reads=[brl], writes=[B("DBGRL")])
                        P.dma("act", dpl[:, 32:64], thl[:], reads=[bthl], writes=[B("DBGRL")])
                if phase_done("O2a_%d" % layer):
                    return None
                P.begin()
                with ExitStack() as st2:
                    idx = K.sb(st2, "s5_idx", [128, 96], F32); bidx = Buf()
                    if "S5_NOIDX" not in K.dbg:
                        P.dma("sp", idx[:], s5idx.partition_broadcast(128), writes=[bidx])
                    sm = K.sb(st2, "s5_sm", [128, 96], F32); bsm = Buf()
                    smi = K.sb(st2, "s5_smi", [128, 96], I32); bsmi = Buf()
                    smf = K.sb(st2, "s5_smf", [128, 96], F32); bsmf = Buf()
                    ssn = K.sb(st2, "s5_ssn", [128, 96], F32); bssn = Buf()
                    scs = K.sb(st2, "s5_scs", [128, 96], F32); bscs = Buf()
                    c1x = K.sb(st2, "s5_c1x", [128, S], BF16); bc1x = Buf()
                    s1x = K.sb(st2, "s5_s1x", [128, S], BF16); bs1x = Buf()
                    c2b = K.sb(st2, "s5_c2b", [128, 64], BF16); bc2b = Buf()
                    s2b = K.sb(st2, "s5_s2b", [128, 64], BF16); bs2b = Buf()
                    tt1 = K.sb(st2, "s5_t1", [128, S], F32); bt1 = Buf()
                    tt2 = K.sb(st2, "s5_t2", [128, S], F32); bt2 = Buf()
                    bubs = [K.sb(st2, "s5_bub%d" % i, [128, 2, S], BF16) for i in range(2)]; bbubs = [Buf(), Buf()]
                    r1 = K.sb(st2, "s5_r1", [128, S], BF16); br1 = Buf()
                    r2 = K.sb(st2, "s5_r2", [128, S], BF16); br2 = Buf()
                    r3 = K.sb(st2, "s5_r3", [128, S], BF16); br3 = Buf()
                    r4 = K.sb(st2, "s5_r4", [128, S], BF16); br4 = Buf()
                    shrb = K.sb(st2, "s5_shrb", [128, S], BF16); bshrb = Buf()
                    shib = K.sb(st2, "s5_shib", [128, S], BF16); bshib = Buf()
                    Cb = K.sb(st2, "s5_Cb", [128, S], BF16); bCb = Buf()
                    Sb = K.sb(st2, "s5_Sb", [128, S], BF16); bSb = Buf()
                    u1 = K.sb(st2, "s5_u1", [128, S], BF16); bu1 = Buf()
                    u2 = K.sb(st2, "s5_u2", [128, S], BF16); bu2 = Buf()
                    u3 = K.sb(st2, "s5_u3", [128, S], BF16); bu3 = Buf()
                    u4 = K.sb(st2, "s5_u4", [128, S], BF16); bu4 = Buf()
                    srb = K.sb(st2, "s5_srb", [128, S], BF16); bsrb = Buf()
                    sib = K.sb(st2, "s5_sib", [128, S], BF16); bsib = Buf()
                    ub = K.sb(st2, "s5_ub", [128, T], BF16); bub = Buf()
                    yacc = K.sb(st2, "s5_yacc", [128, T], F32); byacc = Buf()
                    pbr = [K.ps(st2, "s5_pbr%d" % i) for i in range(2)]; bpbr = [Buf(), Buf()]
                    pbi = [K.ps(st2, "s5_pbi%d" % i) for i in range(2)]; bpbi = [Buf(), Buf()]
                    py = [K.ps(st2, "s5_py%d" % i) for i in range(2)]; bpy = [Buf(), Buf()]
                    cn = {"b": 0, "y": 0, "u": 0}
                    if "S5_DUMMY" in K.dbg:
                        P.op("act", lambda e: e.activation(out=ssn[:], in_=idx[:], func=AF.Copy), [bidx], [bssn])
                        P.op("pool", lambda e: e.memset(scs[:], 0.0), [], [bscs])
                    lvl = 9
                    for f_ in K.dbg:
                        if f_.startswith("S5LVL"):
                            lvl = int(f_[5:].replace("m", "-"))
                    for ct in range(4):
                        def chtile(ct=ct):
                            if "S5_NOUB" not in K.dbg:
                                P.dma("sp", ub[:], UCB[ct], reads=[B("UCB")], writes=[bub])
                            for s0 in range(NS):
                                if "S5_NODSKIP" in K.dbg:
                                    continue
                                def dskip(s0=s0):
                                    tk0 = slice(s0 * S, (s0 + 1) * S)
                                    P.dma("sp", tt1[:], UC[ct, :, tk0], reads=[B("UC")], writes=[bt1])
                                    P.op("dve", lambda e: e.tensor_scalar(out=yacc[:, tk0], in0=tt1[:], scalar1=K.pcol("s5_d%d" % li, ct), scalar2=None, op0=ALU.mult),
                                         [bt1, bpp], [byacc])
                                dskip()
                            for d_ in range(2):
                                for q4 in range(4):
                                    def lanetile(d_=d_, q4=q4):
                                        dq = d_ * 16 + ct * 4 + q4
                                        if lvl < 0:
                                            return
                                        thc = thl[:, dq:dq + 1]
                                        P.op("dve", lambda e: e.tensor_scalar(out=sm[:], in0=idx[:], scalar1=thc, scalar2=None, op0=ALU.mult), [bidx, bthl], [bsm])
                                        sincos(sm, bsm, smi, bsmi, smf, bsmf, ssn, bssn, scs, bscs, (slice(None), slice(None)))
                                        v3 = lambda t_: t_[:].rearrange("p (a b) -> p a b", a=32)
                                        P.op("act", lambda e: e.activation(out=v3(c1x), in_=scs[:, 0:32].unsqueeze(2).to_broadcast([128, 32, 64]), func=AF.Copy), [bscs], [bc1x])
                                        P.op("act", lambda e: e.activation(out=v3(s1x), in_=ssn[:, 0:32].unsqueeze(2).to_broadcast([128, 32, 64]), func=AF.Copy), [bssn], [bs1x])
                                        P.op("act", lambda e: e.activation(out=c2b[:], in_=scs[:, 32:96], func=AF.Copy), [bscs], [bc2b])
                                        P.op("act", lambda e: e.activation(out=s2b[:], in_=ssn[:, 32:96], func=AF.Copy), [bssn], [bs2b])
                                        c2 = c2b[:].unsqueeze(1).to_broadcast([128, 32, 64]); s2 = s2b[:].unsqueeze(1).to_broadcast([128, 32, 64])
                                        P.op("dve", lambda e: e.tensor_tensor(out=v3(u1), in0=v3(c1x), in1=c2, op=ALU.mult), [bc1x, bc2b], [bu1])
                                        P.op("dve", lambda e: e.tensor_tensor(out=v3(u2), in0=v3(s1x), in1=s2, op=ALU.mult), [bs1x, bs2b], [bu2])
                                        P.op("dve", lambda e: e.tensor_tensor(out=v3(u3), in0=v3(s1x), in1=c2, op=ALU.mult), [bs1x, bc2b], [bu3])
                                        P.op("dve", lambda e: e.tensor_tensor(out=v3(u4), in0=v3(c1x), in1=s2, op=ALU.mult), [bc1x, bs2b], [bu4])
                                        P.op("dve", lambda e: e.tensor_tensor(out=Cb[:], in0=u1[:], in1=u2[:], op=ALU.subtract), [bu1, bu2], [bCb])
                                        P.op("dve", lambda e: e.tensor_tensor(out=Sb[:], in0=u3[:], in1=u4[:], op=ALU.add), [bu3, bu4], [bSb])
                                        rbc = rl[:, dq:dq + 1].to_broadcast([128, S])
                                        if lvl < 1:
                                            return
                                        for s in range(NS):
                                            def seq(s=s):
                                                rev = (d_ == 1)
                                                bub16 = bubs[cn["u"] % 2]; bbub = bbubs[cn["u"] % 2]; cn["u"] += 1

                                                def tv(tile_, a, b):
                                                    if not rev:
                                                        return tile_[:, a:b]
                                                    lo = S - 1 - a
                                                    hi = S - 1 - b
                                                    return tile_[:, lo:hi:-1] if hi >= 0 else tile_[:, lo::-1]
                                                rv = (lambda ap: ap) if not rev else (lambda ap: ap[:, ::-1])
                                                for tb in range(4):
                                                    k_ = cn["b"] % 2; cn["b"] += 1
                                                    tsl = slice(s * S + tb * 512, s * S + (tb + 1) * 512)
                                                    a0 = tb * 512; b0 = (tb + 1) * 512
                                                    lsl = slice(a0, b0)
                                                    P.op("pe", lambda e, k_=k_, tsl=tsl: e.matmul(pbr[k_][:], lhsT=PB[:, dq, 0, :], rhs=ub[:, tsl], start=True, stop=True), [bPB, bub], [bpbr[k_]])
                                                    P.op("pe", lambda e, k_=k_, tsl=tsl: e.matmul(pbi[k_][:], lhsT=PB[:, dq, 1, :], rhs=ub[:, tsl], start=True, stop=True), [bPB, bub], [bpbi[k_]])
                                                    P.op("act", lambda e, k_=k_, lsl=lsl: e.activation(out=bub16[:, 0, lsl], in_=pbr[k_][:], func=AF.Copy), [bpbr[k_]], [bbub])
                                                    P.op("act", lambda e, k_=k_, lsl=lsl: e.activation(out=bub16[:, 1, lsl], in_=pbi[k_][:], func=AF.Copy), [bpbi[k_]], [bbub])
                                                Cbv0 = rv(Cb[:]); Sbv0 = rv(Sb[:])
                                                P.op("dve", lambda e: e.tensor_tensor(out=r1[:], in0=bub16[:, 0, :], in1=Cbv0, op=ALU.mult), [bbub, bCb], [br1])
                                                P.op("dve", lambda e: e.tensor_tensor(out=r2[:], in0=bub16[:, 1, :], in1=Sbv0, op=ALU.mult), [bbub, bSb], [br2])
                                                P.op("dve", lambda e: e.tensor_tensor(out=r3[:], in0=bub16[:, 1, :], in1=Cbv0, op=ALU.mult), [bbub, bCb], [br3])
                                                P.op("dve", lambda e: e.tensor_tensor(out=r4[:], in0=bub16[:, 0, :], in1=Sbv0, op=ALU.mult), [bbub, bSb], [br4])
                                                P.op("dve", lambda e: e.tensor_tensor(out=r1[:], in0=r1[:], in1=r2[:], op=ALU.add), [br1, br2], [br1])
                                                P.op("dve", lambda e: e.tensor_tensor(out=r3[:], in0=r3[:], in1=r4[:], op=ALU.subtract), [br3, br4], [br3])
                                                P.op("dve", lambda e: e.tensor_tensor_scan(out=rv(shrb[:]), data0=rbc, data1=rv(r1[:]), initial=0.0, op0=ALU.mult, op1=ALU.add),
                                                     [brl, br1], [bshrb])
                                                P.op("dve", lambda e: e.tensor_tensor_scan(out=rv(shib[:]), data0=rbc, data1=rv(r3[:]), initial=0.0, op0=ALU.mult, op1=ALU.add),
                                                     [brl, br3], [bshib])
                                                Cbv = rv(Cb[:]); Sbv = rv(Sb[:])
                                                P.op("dve", lambda e: e.tensor_tensor(out=u1[:], in0=shrb[:], in1=Cbv, op=ALU.mult), [bshrb, bCb], [bu1])
                                                P.op("dve", lambda e: e.tensor_tensor(out=u2[:], in0=shib[:], in1=Sbv, op=ALU.mult), [bshib, bSb], [bu2])
                                                P.op("dve", lambda e: e.tensor_tensor(out=u3[:], in0=shrb[:], in1=Sbv, op=ALU.mult), [bshrb, bSb], [bu3])
                                                P.op("dve", lambda e: e.tensor_tensor(out=u4[:], in0=shib[:], in1=Cbv, op=ALU.mult), [bshib, bCb], [bu4])
                                                P.op("dve", lambda e: e.tensor_tensor(out=srb[:], in0=u1[:], in1=u2[:], op=ALU.subtract), [bu1, bu2], [bsrb])
                                                P.op("dve", lambda e: e.tensor_tensor(out=sib[:], in0=u3[:], in1=u4[:], op=ALU.add), [bu3, bu4], [bsib])
                                                for tb in range(4):
                                                    k_ = cn["y"] % 2; cn["y"] += 1
                                                    lsl = slice(tb * 512, (tb + 1) * 512)
                                                    tsl = slice(s * S + tb * 512, s * S + (tb + 1) * 512)
                                                    P.op("pe", lambda e, k_=k_, lsl=lsl: e.matmul(py[k_][:], lhsT=PC[:, dq, 0, :], rhs=srb[:, lsl], start=True, stop=False), [bPC, bsrb], [bpy[k_]])
                                                    P.op("pe", lambda e, k_=k_, lsl=lsl: e.matmul(py[k_][:], lhsT=PC[:, dq, 1, :], rhs=sib[:, lsl], start=False, stop=True), [bPC, bsib], [bpy[k_]])
                                                    P.op("dve", lambda e, k_=k_, tsl=tsl: e.tensor_tensor(out=yacc[:, tsl], in0=py[k_][:], in1=yacc[:, tsl], op=ALU.add), [bpy[k_], byacc], [byacc])
                                            seq()
                                    lanetile()
                            for s in range(NS):
                                if "S5_NOGELU" in K.dbg:
                                    continue
                                def gel(s=s):
                                    tk = slice(s * S, (s + 1) * S)
                                    P.op("pool", lambda e: e.tensor_tensor(out=tt1[:], in0=yacc[:, tk], in1=yacc[:, tk], op=ALU.mult), [byacc], [bt1])
                                    P.op("pool", lambda e: e.tensor_scalar(out=tt1[:], in0=tt1[:], scalar1=0.044715, scalar2=1.0, op0=ALU.mult, op1=ALU.add), [bt1], [bt1])
                                    P.op("pool", lambda e: e.tensor_tensor(out=tt1[:], in0=tt1[:], in1=yacc[:, tk], op=ALU.mult), [bt1, byacc], [bt1])
                                    P.op("act", lambda e: e.activation(out=tt2[:], in_=tt1[:], func=AF.Sigmoid, scale=2.0 * math.sqrt(2.0 / math.pi)), [bt1], [bt2])
                                    P.op("dve", lambda e: e.tensor_tensor(out=zb[:, ct, tk], in0=yacc[:, tk], in1=tt2[:], op=ALU.mult), [byacc, bt2], [bzb])
                                gel()
                        chtile()
                if phase_done("O2b_%d" % layer):
                    return None
                P.begin()
                with ExitStack() as st2:
                    lin = Lin(st2, 4, 1, "s5g")
                    sgs = Stage(st2, "s5_sg", [128, 512], F32, 2)
                    obs = Stage(st2, "s5_ob", [128, 512], BF16, 3)
                    for oc in range(4):
                        if "S5_NOGLU" in K.dbg:
                            continue
                        def epi(tb, sl, pss, bps, oc=oc):
                            sg, bsg = sgs.next()
                            P.op("act", lambda e: e.activation(out=sg[:], in_=pss[0][:], func=AF.Sigmoid, bias=K.pcol("s5_b_glu%d" % li, oc)), [bps[0], bpp], [bsg])
                            ob, bob = obs.next()
                            P.op("dve", lambda e: e.tensor_tensor(out=ob[:], in0=zb[:, oc, sl], in1=sg[:], op=ALU.mult), [bzb, bsg], [bob])
                            store(MIXO[oc, :, sl], ob[:], bob, B("MIXO"))
                        lin.run(zb, bzb, T, [(w_glu[li], oc * 128)], epi)
            if phase_done("O2_%d" % layer):
                return None
            P.begin()
            with ExitStack() as st:
                xt = [K.sb(st, "c_x%d" % i, [128, S + 30], BF16) for i in range(2)]; bxt = [Buf(), Buf()]
                for i in range(2):
                    P.op("pool", lambda e, i=i: e.memset(xt[i][:, 0:15], 0.0), [], [bxt[i]])
                    P.op("pool", lambda e, i=i: e.memset(xt[i][:, S + 15:S + 30], 0.0), [], [bxt[i]])
                ucv = K.sb(st, "c_u", [128, 4, S], F32); bucv = Buf()
                sqt = K.sb(st, "c_sq", [128, 512], F32); bsqt = Buf()
                sd = K.sb(st, "c_sd", [128, 512], F32); bsd = Buf()
                yt = K.sb(st, "c_y", [128, 512], F32); byt = Buf()
                obs = Stage(st, "c_ob", [128, 512], BF16, 3)
                pm = K.ps(st, "c_pm"); bpm = Buf()
                pv = K.ps(st, "c_pv"); bpv = Buf()
                pcv = [K.ps(st, "c_pc%d" % i) for i in range(2)]; bpcv = [Buf(), Buf()]
                identf = K.sb(st, "c_id", [128, 128], F32); bid = Buf()
                P.dma("sp", identf[:], identd, writes=[bid])
                Dg = K.sb(st, "c_Dg", [128, 4, 31, 128], BF16); bDg = Buf()
                for c in range(4):
                    for k in range(31):
                        eng = "dve" if (k % 3) != 2 else "pool"
                        P.op(eng, lambda e, c=c, k=k: e.tensor_scalar(out=Dg[:, c, k, :], in0=identf[:], scalar1=K.pcol("conv_w%d_%d" % (li, k), c), scalar2=None,
                                                                      op0=ALU.mult), [bid, bpp], [bDg])
                cn3 = {"x": 0, "p": 0}
                for s in range(NS):
                    def cseq(s=s):
                        for c in range(4):
                            def cconv(c=c):
                                i_ = cn3["x"] % 2; cn3["x"] += 1
                                x_ = xt[i_]; bx_ = bxt[i_]
                                P.dma("pool", x_[:, 15:15 + S], DV[c, :, s * S:(s + 1) * S], reads=[B("DV")], writes=[bx_])
                                for tb in range(4):
                                    j_ = cn3["p"] % 2; cn3["p"] += 1
                                    for k in range(31):
                                        P.op("pe", lambda e, k=k, tb=tb, j_=j_: e.matmul(pcv[j_][:], lhsT=Dg[:, c, k, :], rhs=x_[:, tb * 512 + k: tb * 512 + k + 512],
                                                                                          start=(k == 0), stop=(k == 30)), [bDg, bx_], [bpcv[j_]])
                                    P.op("act", lambda e, tb=tb, j_=j_: e.activation(out=ucv[:, c, tb * 512:(tb + 1) * 512], in_=pcv[j_][:], func=AF.Identity,
                                                                                     bias=K.pcol("conv_b%d" % li, c)), [bpcv[j_], bpp], [bucv])
                            cconv()
                        for tb in range(4):
                            def ln(tb=tb):
                                sl = slice(tb * 512, (tb + 1) * 512)
                                for c in range(4):
                                    P.op("pe", lambda e, c=c: e.matmul(pm[:], lhsT=ones512f[:], rhs=ucv[:, c, sl], start=(c == 0), stop=(c == 3)), [bconst, bucv], [bpm])
                                for c in range(4):
                                    P.op("dve", lambda e, c=c: e.tensor_tensor(out=ucv[:, c, sl], in0=ucv[:, c, sl], in1=pm[:], op=ALU.subtract), [bucv, bpm], [bucv])
                                for c in range(4):
                                    P.op("act", lambda e, c=c: e.activation(out=sqt[:], in_=ucv[:, c, sl], func=AF.Square), [bucv], [bsqt])
                                    P.op("pe", lambda e, c=c: e.matmul(pv[:], lhsT=ones512f[:], rhs=sqt[:], start=(c == 0), stop=(c == 3)), [bconst, bsqt], [bpv])
                                P.op("act", lambda e: e.activation(out=sd[:], in_=pv[:], func=AF.Sqrt, bias=EPS), [bpv], [bsd])
                                P.op("dve", lambda e: e.reciprocal(out=sd[:], in_=sd[:]), [bsd], [bsd])
                                for c in range(4):
                                    P.op("dve", lambda e, c=c: e.tensor_tensor(out=yt[:], in0=ucv[:, c, sl], in1=sd[:], op=ALU.mult), [bucv, bsd], [byt])
                                    P.op("pool", lambda e, c=c: e.tensor_scalar(out=yt[:], in0=yt[:], scalar1=K.pcol("conv_ln_g%d" % li, c),
                                                                                scalar2=K.pcol("conv_ln_b%d" % li, c), op0=ALU.mult, op1=ALU.add), [byt, bpp], [byt])
                                    ob, bob = obs.next()
                                    P.op("act", lambda e, ob=ob: e.activation(out=ob[:], in_=yt[:], func=AF.Silu), [byt], [bob])
                                    store(MIXO[4 + c, :, s * S + tb * 512: s * S + (tb + 1) * 512], ob[:], bob, B("MIXO"))
                            ln()
                    cseq()
            if phase_done("O3_%d" % layer):
                return None
            r_ = out_proj_phase(w_out_cd[li], cur_src, cur_bsrc)
            if phase_done("O4_%d" % layer):
                return None
            return r_

        cur_src, cur_bsrc = xT, bXT
        for layer in range(DEPTH):
            li = layer // 2
            if "ONLYODD" in K.dbg and layer == 0:
                continue
            if layer % 2 == 0:
                P.begin()
                with ExitStack() as st:
                    hn = K.sb(st, "hn", [128, 8, T], BF16); bhn = [Buf() for _ in range(8)]
                    emit_norm(st, cur_src, cur_bsrc, "g_mix%d" % layer, hn, bhn)
                    Wab = w_in_ab[li]
                    Wpm = w_qkperm[li]
                    with ExitStack() as st2:
                        cosT = K.sb(st2, "cosT", [128, T], F32); bcos = Buf()
                        sinT = K.sb(st2, "sinT", [128, T], F32); bsin = Buf()
                        P.dma("sp", cosT[:], COS, reads=[B("COS")], writes=[bcos])
                        P.dma("sp", sinT[:], SINS, reads=[B("SINS")], writes=[bsin])
                        lin2 = Lin(st2, 8, 2, "e1a")
                        t1s = Stage(st2, "e1_t1", [128, 512], F32, 2)
                        t2s = Stage(st2, "e1_t2", [128, 512], F32, 2)
                        obs = Stage(st2, "e1_ob", [128, 512], BF16, 3)
                        for j in range(8):
                            dst = QA if j < 4 else KA
                            bd = B("QA") if j < 4 else B("KA")
                            def epi(tb, sl, pss, bps, j=j, dst=dst, bd=bd):
                                t1, bt1 = t1s.next(); t2, bt2 = t2s.next(); ob, bob = obs.next()
                                P.op("dve", lambda e: e.tensor_tensor(out=t1[:], in0=pss[0][:], in1=cosT[:, sl], op=ALU.mult),
                                     [bps[0], bcos], [bt1])
                                P.op("dve", lambda e: e.tensor_tensor(out=t2[:], in0=pss[1][:], in1=sinT[:, sl], op=ALU.mult),
                                     [bps[1], bsin], [bt2])
                                P.op("pool", lambda e: e.tensor_tensor(out=ob[:], in0=t1[:], in1=t2[:], op=ALU.add),
                                     [bt1, bt2], [bob])
                                store(dst[j % 4, :, sl], ob[:], bob, bd)
                            lin2.run(hn, bhn, T, [(Wab, j * 128), (Wpm, j * 128)], epi)
                        if "DBGCOS" in K.dbg:
                            dc = K.scratch("DBGCOS", [128, T], F32)
                            P.dma("act", dc, cosT[:], reads=[bcos], writes=[B("DBGCOS")])
                    P.end(); P.begin()
                    if "DBGHN" in K.dbg:
                        dh = K.scratch("DBGHN", [8, 128, T], BF16)
                        for c in range(8):
                            P.dma("act", dh[c], hn[:, c, :], reads=bhn, writes=[B("DBGHN")])
                    with ExitStack() as st2:
                        lin1 = Lin(st2, 8, 1, "e1b")
                        ofs = Stage(st2, "e1_of", [128, 512], F32, 3)
                        obs = Stage(st2, "e1_ob2", [128, 512], BF16, 3)
                        zext = K.sb(st2, "zext", [33, T], F32); bz = Buf()
                        P.op("dve", lambda e: e.memset(zext[32:33, :], 1.0), [], [bz])
                        for j in range(2):
                            for (c0, dst, bd, scl) in ((1536, QB, B("QB"), 0.125), (1792, KBf, B("KBF"), 1.0)):
                                def epi(tb, sl, pss, bps, j=j, dst=dst, bd=bd, scl=scl):
                                    ot, bot = ofs.next()
                                    P.op("act", lambda e: e.activation(out=ot[:], in_=pss[0][:], func=AF.Copy, scale=scl),
                                         [bps[0]], [bot])
                                    store(dst[j, :, sl], ot[:], bot, bd)
                                lin1.run(hn, bhn, T, [(Wab, c0 + j * 128)], epi)
                        for j in range(4):
                            def epi(tb, sl, pss, bps, j=j):
                                ob, bob = obs.next()
                                P.op("act", lambda e: e.activation(out=ob[:], in_=pss[0][:], func=AF.Silu), [bps[0]], [bob])
                                store(RB[j, :, sl], ob[:], bob, B("RB"))
                            lin1.run(hn, bhn, T, [(Wab, 2560 + j * 128)], epi)
                        def epi(tb, sl, pss, bps):
                            P.op("act", lambda e: e.activation(out=zext[0:32, sl], in_=pss[0][0:32, :], func=AF.Copy), [bps[0]], [bz])
                        lin1.run(hn, bhn, T, [(Wab, 3072)], epi, M=32)
                        wr = K.sb(st2, "e1_wr", [128, 8, 512], BF16); bwr = Buf()
                        w2b = K.sb(st2, "e1_w2b", [33, 512], F32); bw2 = Buf()
                        P.dma("sp", w2b[:], w2blk_d[li], writes=[bw2])
                        pst = [K.ps(st2, "e1_tp%d" % i) for i in range(2)]
                        bpt = [Buf(), Buf()]
                        tobs = Stage(st2, "e1_tob", [128, 512], BF16, 3)
                        tofs = Stage(st2, "e1_tof", [128, 512], F32, 3)
                        it = 0
                        for (c0, ncol, dst, bd, isbf) in ((1024, 512, VA, B("VA"), True), (2048, 512, VB, B("VB"), True),
                                                         (1792, 256, KBT, B("KBT"), False)):
                            P.dma("pool", wr[:, :, 0:ncol], Wab.rearrange("(kc p) f -> p kc f", p=128)[:, :, c0:c0 + ncol], writes=[bwr])
                            for blk in range(T // 128):
                                tsl = slice(blk * 128, (blk + 1) * 128)
                                ps_ = pst[it % 2]; bp_ = bpt[it % 2]; it += 1
                                for kc in range(8):
                                    P.op("pe", lambda e, ps_=ps_, kc=kc, tsl=tsl, ncol=ncol: e.matmul(
                                        ps_[:, 0:ncol], lhsT=hn[:, kc, tsl], rhs=wr[:, kc, 0:ncol], start=(kc == 0), stop=(kc == 7)),
                                        [bhn[blk // 4], bwr], [bp_])
                                if isbf:
                                    ot, bot = tobs.next()
                                else:
                                    ot, bot = tofs.next()
                                P.op("act", lambda e, ot=ot, ps_=ps_, ncol=ncol: e.activation(out=ot[:, 0:ncol], in_=ps_[:, 0:ncol], func=AF.Copy),
                                     [bp_], [bot])
                                store(dst[tsl, :], ot[:, 0:ncol], bot, bd)
                        for blk in range(T // 128):
                            tsl = slice(blk * 128, (blk + 1) * 128)
                            ps_ = pst[it % 2]; bp_ = bpt[it % 2]; it += 1
                            P.op("pe", lambda e, ps_=ps_, tsl=tsl: e.matmul(ps_[:], lhsT=zext[0:33, tsl], rhs=w2b[0:33, :], start=True, stop=True),
                                 [bz, bw2], [bp_])
                            t1, bt1 = tofs.next()
                            P.op("act", lambda e, t1=t1, ps_=ps_: e.activation(out=t1[:], in_=ps_[:], func=AF.Exp, scale=-1.0), [bp_], [bt1])
                            ot, bot = tofs.next()
                            P.op("act", lambda e, t1=t1, ot=ot: e.activation(out=ot[:], in_=t1[:], func=AF.Ln, bias=1.0), [bt1], [bot])
                            store(LG[tsl, :], ot[:], bot, B("LG"))
                if phase_done("E1_%d" % layer):
                    return K
                P.begin()
                with ExitStack() as st:
                    mk = K.sb(st, "a_mask", [128, 20, 512], BF16); bmk = Buf()
                    for n in range(0, 20, 4):
                        P.dma("pool", mk[:, n:n + 4, :], maskd[:, n * 512:(n + 4) * 512].rearrange("p (a b) -> p a b", a=4), writes=[bmk])
                    onesel = K.sb(st, "a_osel", [128, 2, 128], BF16); bos = Buf()
                    P.op("dve", lambda e: e.memset(onesel[:], 0.0), [], [bos])
                    P.op("dve", lambda e: e.memset(onesel[:, 0, 0:64], 1.0), [], [bos])
                    P.op("dve", lambda e: e.memset(onesel[:, 1, 64:128], 1.0), [], [bos])
                    qt = [K.sb(st, "a_q%d" % i, [128, S], BF16) for i in range(2)]; bq = [Buf(), Buf()]
                    kt = [K.sb(st, "a_k%d" % i, [128, S], BF16) for i in range(2)]; bk = [Buf(), Buf()]
                    vp = [K.sb(st, "a_v%d" % i, [128, 16, 2, 128], BF16) for i in range(2)]; bv = [Buf(), Buf()]
                    for i in range(2):
                        P.op("pool", lambda e, i=i: e.memset(vp[i][:, :, 0, 64:128], 1.0), [], [bv[i]])
                        P.op("pool", lambda e, i=i: e.memset(vp[i][:, :, 1, 0:64], 1.0), [], [bv[i]])
                    pss = [K.ps(st, "a_s%d" % i) for i in range(3)]; bpss = [Buf() for _ in range(3)]
                    pacc = [[K.ps(st, "a_acc%d_%d" % (i, j)) for j in range(2)] for i in range(2)]
                    bpacc = [[Buf(), Buf()], [Buf(), Buf()]]
                    exs = Stage(st, "a_ex", [128, 512], BF16, 4)
                    rec = K.sb(st, "a_rec", [128, 512], F32); brec = Buf()
                    obs = Stage(st, "a_ob", [128, 512], BF16, 2)
                    it = 0; sit = [0]; ait = 0
                    pend_a = []

                    def flush_a():
                        while pend_a:
                            d_, o_, bo_ = pend_a.pop(0)
                            store(d_, o_[:], bo_, B("MIXO"))
                    for s in range(NS):
                        for c in range(4):
                            b_ = it % 2; it += 1
                            tok = slice(s * S, (s + 1) * S)
                            P.dma("sp", qt[b_][:], QA[c, :, tok], reads=[B("QA")], writes=[bq[b_]])
                            P.dma("sp", kt[b_][:], KA[c, :, tok], reads=[B("KA")], writes=[bk[b_]])
                            for e2 in range(2):
                                h = 2 * c + e2
                                P.dma("sp", vp[b_][:, :, e2, e2 * 64:(e2 + 1) * 64],
                                      VA[tok, h * 64:(h + 1) * 64].rearrange("(b p) f -> p b f", p=128),
                                      reads=[B("VA")], writes=[bv[b_]])
                            for qb in range(4):
                                a_ = ait % 2; ait += 1
                                kbs = [kb for kb in range(16) if -8 <= kb - 4 * qb <= 11]
                                units = [(kb, e2) for kb in kbs for e2 in range(2)]
                                nmm = len(units)
                                st_ = {}

                                def stA(u, b_=b_, qb=qb, units=units, st_=st_):
                                    kb, e2 = units[u]
                                    rows = slice(e2 * 64, (e2 + 1) * 64)
                                    p_ = pss[sit[0] % 3]; bp_ = bpss[sit[0] % 3]; sit[0] += 1
                                    P.op("pe", lambda e: e.matmul(p_[:], lhsT=kt[b_][rows, kb * 128:(kb + 1) * 128], rhs=qt[b_][rows, qb * 512:(qb + 1) * 512],
                                                                  start=True, stop=True), [bk[b_], bq[b_]], [bp_])
                                    st_[u] = (p_, bp_)

                                def stB(u, qb=qb, units=units, st_=st_):
                                    kb, e2 = units[u]
                                    p_, bp_ = st_[u]
                                    ex, bex = exs.next()
                                    P.op("act", lambda e: e.activation(out=ex[:], in_=p_[:], func=AF.Exp, scale=0.125), [bp_], [bex])
                                    n = kb - 4 * qb + 8
                                    P.op("dve", lambda e: e.tensor_tensor(out=ex[:], in0=ex[:], in1=mk[:, n, :], op=ALU.mult), [bex, bmk], [bex])
                                    st_[u] = (ex, bex)

                                def stC(u, a_=a_, b_=b_, units=units, st_=st_, nmm=nmm):
                                    kb, e2 = units[u]
                                    ex, bex = st_[u]
                                    first = (u < 2); last = (u >= nmm - 2)
                                    P.op("pe", lambda e: e.matmul(pacc[a_][e2][:], lhsT=vp[b_][:, kb, e2, :], rhs=ex[:], start=first, stop=last),
                                         [bv[b_], bex], [bpacc[a_][e2]])

                                for step in range(nmm + 2):
                                    if step == 8:
                                        flush_a()
                                    if step < nmm:
                                        stA(step)
                                    if 0 <= step - 1 < nmm:
                                        stB(step - 1)
                                    if 0 <= step - 2 < nmm:
                                        stC(step - 2)
                                ob, bob = obs.next()
                                P.op("dve", lambda e, a_=a_: e.reciprocal(out=rec[64:128, :], in_=pacc[a_][0][64:128, :]), [bpacc[a_][0]], [brec])
                                P.op("dve", lambda e, a_=a_: e.reciprocal(out=rec[0:64, :], in_=pacc[a_][1][0:64, :]), [bpacc[a_][1]], [brec])
                                P.op("dve", lambda e, a_=a_, ob=ob: e.tensor_tensor(out=ob[0:64, :], in0=pacc[a_][0][0:64, :], in1=rec[64:128, :], op=ALU.mult),
                                     [bpacc[a_][0], brec], [bob])
                                P.op("dve", lambda e, a_=a_, ob=ob: e.tensor_tensor(out=ob[64:128, :], in0=pacc[a_][1][64:128, :], in1=rec[0:64, :], op=ALU.mult),
                                     [bpacc[a_][1], brec], [bob])
                                pend_a.append((MIXO[c, :, s * S + qb * 512: s * S + (qb + 1) * 512], ob, bob))
                    flush_a()
                if phase_done("E2_%d" % layer):
                    return K
                P.begin()
                with ExitStack() as st:
                    gc = K.sb(st, "g_c", [128, 770], F32); bgc = Buf()
                    P.dma("sp", gc[:], glac, writes=[bgc])
                    A_pi = gc[:, 0:128]; A_si = gc[:, 128:256]; A_se = gc[:, 256:384]; A_pe = gc[:, 384:512]
                    MFB = gc[:, 512:768]; CH = gc[:, 768:770]
                    qf = K.sb(st, "g_q", [128, 2, S], F32); bqf = Buf()
                    kf_ = K.sb(st, "g_k", [128, 2, S], F32); bkf_ = Buf()
                    kbt = K.sb(st, "g_kt", [128, 16, 256], F32); bkbt = Buf()
                    vb = K.sb(st, "g_v", [128, 16, 512], BF16); bvb = Buf()
                    lg = K.sb(st, "g_lg", [128, 16, 512], F32); blg = Buf()
                    rb = K.sb(st, "g_rb", [128, 4, S], BF16); brb = Buf()
                    Sst = K.sb(st, "g_Sst", [128, 2, 32, 128], BF16); bSst = Buf()
                    Sb = K.sb(st, "g_Sb", [128, 2, 128], F32); bSb = Buf()
                    Sf = K.sb(st, "g_Sf", [128, 2, 128], F32); bSf = Buf()
                    Sfb = K.sb(st, "g_Sfb", [128, 2, 128], BF16); bSfb = Buf()
                    eE = K.sb(st, "g_eE", [128, 256], F32); beE = Buf()
                    kU = K.sb(st, "g_kU", [128, 256], BF16); bkU = Buf()
                    av = K.sb(st, "g_a", [128, 2, 2], F32); bav = Buf()
                    eG = K.sb(st, "g_eG", [128, 4, 128], F32); beG = Buf()
                    enG = K.sb(st, "g_enG", [128, 4, 128], F32); benG = Buf()
                    qin = K.sb(st, "g_qin", [128, 2, 2, 128], BF16); bqin = Buf()
                    kin = K.sb(st, "g_kin", [128, 2, 2, 128], BF16); bkin = Buf()
                    atts = Stage(st, "g_att", [128, 256], BF16, 2)
                    osb = K.sb(st, "g_osb", [128, 512], F32); bosb = Buf()
                    osq = K.sb(st, "g_osq", [128, 512], BF16); bosq = Buf()
                    sd = K.sb(st, "g_sd", [128, 512], F32); bsd = Buf()
                    obs = Stage(st, "g_ob", [128, 512], BF16, 3)
                    pend_g = []

                    def flush_g():
                        while pend_g:
                            d_, o_, bo_ = pend_g.pop(0)
                            store(d_, o_[:], bo_, B("MIXO"))
                    pO = [K.ps(st, "g_pO%d" % i) for i in range(4)]; bpO = [Buf() for _ in range(4)]
                    pG = K.ps(st, "g_pG"); bpG = Buf()
                    pA = K.ps(st, "g_pA"); bpA = Buf()
                    pU = K.ps(st, "g_pU"); bpU = Buf()
                    pM = K.ps(st, "g_pM"); bpM = Buf()
                    for s in range(NS):
                        tok = slice(s * S, (s + 1) * S)
                        for ct in range(2):
                            P.dma("sp", qf[:, ct, :], QB[ct, :, tok], reads=[B("QB")], writes=[bqf])
                            P.dma("sp", kf_[:, ct, :], KBf[ct, :, tok], reads=[B("KBF")], writes=[bkf_])
                        for c in range(4):
                            P.dma("sp", rb[:, c, :], RB[c, :, tok], reads=[B("RB")], writes=[brb])
                        P.dma("sp", kbt[:], KBT[tok, :].rearrange("(b p) f -> p b f", p=128), reads=[B("KBT")], writes=[bkbt])
                        P.dma("sp", vb[:], VB[tok, :].rearrange("(b p) f -> p b f", p=128), reads=[B("VB")], writes=[bvb])
                        P.dma("sp", lg[:], LG[tok, :].rearrange("(b p) f -> p b f", p=128), reads=[B("LG")], writes=[blg])
                        P.op("dve", lambda e: e.memset(Sb[:], 0.0), [], [bSb])
                        P.op("dve", lambda e: e.memset(Sf[:], 0.0), [], [bSf])
                        P.op("dve", lambda e: e.memset(Sfb[:], 0.0), [], [bSfb])

                        def state_step(blk, cc, dcol, Sx, bSx):
                            trow = slice(cc * 64, (cc + 1) * 64)
                            for ct in range(2):
                                P.op("pe", lambda e, ct=ct: e.matmul(pU[:, ct * 256:(ct + 1) * 256], lhsT=kU[trow, ct * 128:(ct + 1) * 128],
                                                                     rhs=vb[trow, blk, ct * 256:(ct + 1) * 256], start=True, stop=True),
                                     [bkU, bvb], [bpU])
                            for ct in range(2):
                                for h2 in range(2):
                                    rows = slice(h2 * 64, (h2 + 1) * 64)
                                    P.op("dve", lambda e, ct=ct, h2=h2, rows=rows: e.scalar_tensor_tensor(
                                        out=Sx[rows, ct, :], in0=Sx[rows, ct, :], scalar=av[rows, ct, cc:cc + 1],
                                        in1=pU[rows, ct * 256 + h2 * 128: ct * 256 + (h2 + 1) * 128], op0=ALU.mult, op1=ALU.add),
                                        [bSx, bav, bpU], [bSx])

                        def prep_dir(blk, A_e, lcol0):
                            P.op("pe", lambda e: e.matmul(pM[:, 0:256], lhsT=A_e, rhs=lg[:, blk, lcol0:lcol0 + 256], start=True, stop=True),
                                 [bgc, blg], [bpM])
                            P.op("act", lambda e: e.activation(out=eE[:], in_=pM[:, 0:256], func=AF.Exp), [bpM], [beE])
                            P.op("dve", lambda e: e.tensor_tensor(out=kU[:], in0=kbt[:, blk, :], in1=eE[:], op=ALU.mult), [bkbt, beE], [bkU])
                            for ct in range(2):
                                P.op("pe", lambda e, ct=ct: e.matmul(pM[:, 256 + 2 * ct:256 + 2 * ct + 2], lhsT=lg[:, blk, lcol0 + ct * 128:lcol0 + (ct + 1) * 128],
                                                                     rhs=CH, start=True, stop=True), [bgc, blg], [bpM])
                            P.op("act", lambda e: e.activation(out=av[:].rearrange("p a b -> p (a b)"), in_=pM[:, 256:260], func=AF.Exp), [bpM], [bav])

                        for blk in range(15, -1, -1):
                            prep_dir(blk, A_pe, 256)
                            for cc in (1, 0):
                                n = 2 * blk + cc
                                P.op("act", lambda e, n=n: e.activation(out=Sst[:, :, n, :], in_=Sb[:], func=AF.Copy), [bSb], [bSst])
                                state_step(blk, cc, 256, Sb, bSb)
                        for grp in range(4):
                            for b4 in range(4):
                                blk = grp * 4 + b4
                                bt = slice(blk * 128, (blk + 1) * 128)
                                oc = slice(b4 * 128, (b4 + 1) * 128)
                                for d_, (A_g, lc0) in enumerate(((A_pi, 0), (A_si, 256))):
                                    for ct in range(2):
                                        P.op("pe", lambda e, d_=d_, ct=ct, A_g=A_g, lc0=lc0, blk=blk: e.matmul(
                                            pG[:, (d_ * 2 + ct) * 128:(d_ * 2 + ct + 1) * 128],
                                            lhsT=lg[:, blk, lc0 + ct * 128:lc0 + (ct + 1) * 128], rhs=A_g, start=True, stop=True),
                                            [bgc, blg], [bpG])
                                P.op("act", lambda e: e.activation(out=eG[:].rearrange("p a b -> p (a b)"), in_=pG[:], func=AF.Exp), [bpG], [beG])
                                P.op("act", lambda e: e.activation(out=enG[:].rearrange("p a b -> p (a b)"), in_=pG[:], func=AF.Exp, scale=-1.0), [bpG], [benG])
                                for d_ in range(2):
                                    P.op("dve", lambda e, d_=d_, bt=bt: e.tensor_tensor(out=qin[:, d_, :, :], in0=qf[:, :, bt], in1=eG[:, 2 * d_:2 * d_ + 2, :], op=ALU.mult),
                                         [bqf, beG], [bqin])
                                    P.op("pool", lambda e, d_=d_, bt=bt: e.tensor_tensor(out=kin[:, d_, :, :], in0=kf_[:, :, bt], in1=enG[:, 2 * d_:2 * d_ + 2, :], op=ALU.mult),
                                         [bkf_, benG], [bkin])
                                prep_dir(blk, A_se, 0)
                                for ct in range(2):
                                    for h2 in range(2):
                                        h = 2 * ct + h2
                                        rows = slice(h2 * 64, (h2 + 1) * 64)
                                        for d_ in range(2):
                                            P.op("pe", lambda e, d_=d_, ct=ct, rows=rows: e.matmul(
                                                pA[:, d_ * 128:(d_ + 1) * 128], lhsT=kin[rows, d_, ct, :], rhs=qin[rows, d_, ct, :], start=True, stop=True),
                                                [bkin, bqin], [bpA])
                                        at, bat = atts.next()
                                        P.op("dve", lambda e, at=at: e.tensor_tensor(out=at[:], in0=pA[:, 0:256], in1=MFB, op=ALU.mult), [bpA, bgc], [bat])
                                        P.op("pe", lambda e, at=at, h=h, oc=oc, blk=blk: e.matmul(pO[h][:, oc], lhsT=vb[:, blk, h * 128:(h + 1) * 128], rhs=at[:, 0:128], start=True, stop=False),
                                             [bvb, bat], [bpO[h]])
                                        if "NOBWD" not in K.dbg:
                                            P.op("pe", lambda e, at=at, h=h, oc=oc, blk=blk: e.matmul(pO[h][:, oc], lhsT=vb[:, blk, h * 128:(h + 1) * 128], rhs=at[:, 128:256], start=False, stop=False),
                                                 [bvb, bat], [bpO[h]])
                                for cc in range(2):
                                    n = 2 * blk + cc
                                    occ = slice(b4 * 128 + cc * 64, b4 * 128 + (cc + 1) * 64)
                                    qc = slice(cc * 64, (cc + 1) * 64)
                                    for ct in range(2):
                                        for h2 in range(2):
                                            h = 2 * ct + h2
                                            rows = slice(h2 * 64, (h2 + 1) * 64)
                                            nob = "NOBWD" in K.dbg
                                            nof = "NOFWDINTER" in K.dbg
                                            if not nof:
                                                P.op("pe", lambda e, h=h, ct=ct, rows=rows, occ=occ, qc=qc, cc=cc, nob=nob: e.matmul(
                                                    pO[h][:, occ], lhsT=Sfb[rows, ct, :], rhs=qin[rows, 0, ct, qc], start=False, stop=(nob and cc == 1)),
                                                    [bSfb, bqin], [bpO[h]])
                                            if not nob:
                                                P.op("pe", lambda e, h=h, ct=ct, rows=rows, occ=occ, qc=qc, n=n, cc=cc: e.matmul(
                                                    pO[h][:, occ], lhsT=Sst[rows, ct, n, :], rhs=qin[rows, 1, ct, qc], start=False, stop=(cc == 1)),
                                                    [bSst, bqin], [bpO[h]])
                                    state_step(blk, cc, 0, Sf, bSf)
                                    P.op("act", lambda e: e.activation(out=Sfb[:], in_=Sf[:], func=AF.Copy), [bSf], [bSfb])
                            gt = slice(grp * 512, (grp + 1) * 512)
                            for h in range(4):
                                P.op("act", lambda e, h=h: e.activation(out=osb[:], in_=pO[h][:], func=AF.Copy), [bpO[h]], [bosb])
                                if "DBGGLA" in K.dbg:
                                    if "DBGGLA" not in K.scr:
                                        K.scratch("DBGGLA", [4, 128, T], F32)
                                    P.dma("act", K.scr["DBGGLA"][0][h, :, s * S + grp * 512: s * S + (grp + 1) * 512], osb[:], reads=[bosb], writes=[B("DBGGLA")])
                                P.op("act", lambda e, h=h: e.activation(out=osq[:], in_=pO[h][:], func=AF.Square), [bpO[h]], [bosq])
                                P.op("pe", lambda e: e.matmul(pM[:], lhsT=ones128[:], rhs=osq[:], start=True, stop=True), [bosq, bconst], [bpM])
                                P.op("act", lambda e: e.activation(out=sd[:], in_=pM[:], func=AF.Sqrt, bias=EPS), [bpM], [bsd])
                                flush_g()
                                P.op("dve", lambda e: e.reciprocal(out=sd[:], in_=sd[:]), [bsd], [bsd])
                                P.op("dve", lambda e, h=h: e.scalar_tensor_tensor(out=osb[:], in0=osb[:], scalar=K.pcol("gla_norm%d" % li, h), in1=sd[:],
                                                                              op0=ALU.mult, op1=ALU.mult), [bosb, bsd, bpp], [bosb])
                                ob, bob = obs.next()
                                P.op("dve", lambda e, h=h, ob=ob, gt=gt: e.tensor_tensor(out=ob[:], in0=osb[:], in1=rb[:, h, gt], op=ALU.mult),
                                     [bosb, brb], [bob])
                                pend_g.append((MIXO[4 + h, :, s * S + grp * 512: s * S + (grp + 1) * 512], ob, bob))
                    flush_g()
                if phase_done("E3_%d" % layer):
                    return K
                cur_src, cur_bsrc = out_proj_phase(w_out_ab[li], cur_src, cur_bsrc)
                if phase_done("E4_%d" % layer):
                    return K
            else:
                r_ = odd_phases(layer, li, cur_src, cur_bsrc)
                if r_ is None:
                    return K
                cur_src, cur_bsrc = r_
            r_ = xattn_phases(layer, cur_src, cur_bsrc)
            if r_ is None:
                return K
            cur_src, cur_bsrc = r_
            r_ = ffn_phases(layer, cur_src, cur_bsrc)
            if r_ is None:
                return K
            cur_src, cur_bsrc = r_
        P.begin()
        with ExitStack() as st:
            emit_norm(st, cur_src, cur_bsrc, "g_final", None, None, out_f32=out_d)
        P.end()
        return K


def prepare_inputs(inputs):
    f = lambda k: np.asarray(inputs[k], np.float32)
    pk = PPack()
    invf = (np.float32(10000.0) ** (-np.arange(0, 64, 2, dtype=np.float32) / np.float32(64.0))).astype(np.float32)
    pk.add("invfreq", invf[np.arange(128) % 32])
    pk.add("ropesign", np.where((np.arange(128) % 64) < 32, -1.0, 1.0).astype(np.float32))
    for l in range(DEPTH):
        pk.add("g_mix%d" % l, _chunks(f("g_mix")[l]))
        pk.add("g_xattn%d" % l, _chunks(f("g_xattn")[l]))
        pk.add("g_mem%d" % l, _chunks(f("g_mem")[l]))
        pk.add("g_ffn%d" % l, _chunks(f("g_ffn")[l]))
        pk.add("b_conv_ffn%d" % l, _chunks(f("b_conv_ffn")[l]))
        for k in range(3):
            pk.add("w_conv_ffn%d_%d" % (l, k), _chunks(f("w_conv_ffn")[l, k]))
    pk.add("g_final", _chunks(f("g_final")))
    for i in range(2):
        pk.add("gla_norm%d" % i, _chunks(f("gla_norm")[i]))
        pk.add("s5_d%d" % i, _chunks(f("s5_d")[i]))
        pk.add("s5_b_glu%d" % i, _chunks(f("s5_b_glu")[i]))
        pk.add("conv_b%d" % i, _chunks(f("conv_b")[i]))
        pk.add("conv_ln_g%d" % i, _chunks(f("conv_ln_g")[i]))
        pk.add("conv_ln_b%d" % i, _chunks(f("conv_ln_b")[i]))
        for k in range(31):
            pk.add("conv_w%d_%d" % (i, k), _chunks(f("conv_w")[i, k]))
    pp = pk.build()
    perm = rope_perm()
    w_in_ab = f("w_in_ab")
    w_qkperm = np.ascontiguousarray(np.concatenate([w_in_ab[:, :, 0:512][:, :, perm], w_in_ab[:, :, 512:1024][:, :, perm]], axis=2))
    wg2 = f("gla_wg2"); bg = f("gla_bg")
    w2blk = np.zeros((2, 33, 512), np.float32)
    for i in range(2):
        w2blk[i, 0:16, 0:256] = wg2[i, 0]
        w2blk[i, 16:32, 256:512] = wg2[i, 1]
        w2blk[i, 32, 0:256] = bg[i, 0]
        w2blk[i, 32, 256:512] = bg[i, 1]
    lam_re = f("s5_lam_re"); lam_im = f("s5_lam_im"); log_dt = f("s5_log_dt")
    b_re = f("s5_b_re"); b_im = f("s5_b_im"); c_re = f("s5_c_re"); c_im = f("s5_c_im")
    s5rows = np.zeros((2, 3, 32, 128), np.float32)
    s5B = np.zeros((2, 2, 32, 128, 128), np.float32)
    s5C = np.zeros((2, 2, 32, 128, 128), np.float32)
    for i in range(2):
        for d in range(2):
            for q in range(16):
                dq = d * 16 + q
                for g2 in range(2):
                    g = 2 * q + g2
                    ls = slice(g2 * 64, (g2 + 1) * 64)
                    s5rows[i, 0, dq, ls] = lam_re[i, d, g]
                    s5rows[i, 1, dq, ls] = lam_im[i, d, g]
                    s5rows[i, 2, dq, ls] = log_dt[i, d, g]
                    r0 = (q % 4) * 32 + g2 * 16
                    s5B[i, 0, dq, r0:r0 + 16, ls] = b_re[i, d, g].T
                    s5B[i, 1, dq, r0:r0 + 16, ls] = b_im[i, d, g].T
                    s5C[i, 0, dq, ls, r0:r0 + 16] = c_re[i, d, g].T
                    s5C[i, 1, dq, ls, r0:r0 + 16] = c_im[i, d, g].T
    s5lane = np.ascontiguousarray(s5rows.transpose(0, 3, 1, 2).reshape(2, 128, 96))
    s5B = np.ascontiguousarray(s5B.reshape(2, 2, 2, 16, 128, 128).transpose(0, 1, 2, 4, 3, 5).reshape(2, 2, 2, 128, 2048))
    s5C = np.ascontiguousarray(s5C.reshape(2, 2, 2, 16, 128, 128).transpose(0, 1, 2, 4, 3, 5).reshape(2, 2, 2, 128, 2048))
    shared = {
        "pp": pp, "amask": attn_masks(), "glac": gla_consts(), "ident": np.eye(128, dtype=np.float32),
        "w_in_ab": w_in_ab, "w_qkperm": w_qkperm, "w_out_ab": f("w_out_ab"), "w2blk": w2blk,
        "w_in_cd": f("w_in_cd"), "w_out_cd": f("w_out_cd"), "w_xq": f("w_xq"), "w_xkv": f("w_xkv"),
        "w_xo": f("w_xo"), "w_up": f("w_up"), "w_down": f("w_down"), "w_glu": f("s5_w_glu"),
        "s5rows": s5rows, "s5lane": s5lane, "s5B": s5B, "s5C": s5C,
        "s5idx": np.concatenate([64.0 * np.arange(32), np.arange(64)]).astype(np.float32)[None, :],
    }
    x = f("x"); mem = f("mem"); posn = np.asarray(inputs["positions"], np.int32)
    in_maps = []
    for c in range(NCORES):
        xs = x[NS * c:NS * (c + 1)].reshape(T, D)
        ms = mem[NS * c:NS * (c + 1)].reshape(NS * NMEM, D)
        m = dict(shared)
        m["xT"] = np.ascontiguousarray(xs.T).reshape(8, 128, T)
        m["memT"] = np.ascontiguousarray(ms.T).reshape(8, 128, NS * NMEM)
        m["pos"] = np.ascontiguousarray(posn[NS * c:NS * (c + 1)].reshape(1, T))
        in_maps.append(m)
    return in_maps, pk.idx, pp.shape[1]


def kernel(**inputs):
    in_maps, ppidx, npp = prepare_inputs(inputs)
    K = build_program(ppidx, npp)
    res = run_bass_kernel_spmd(K.nc, in_maps, core_ids=list(range(NCORES)))
    out = np.zeros((NCORES * NS, S, D), np.float32)
    for c in range(NCORES):
        o = np.asarray(res.results[c]["outT"]).reshape(D, T)
        out[NS * c:NS * (c + 1)] = o.T.reshape(NS, S, D)
    return out
```
